# Optimizing a Trainium2 kernel written in Bass

```python
import math
import jax, jax.numpy as jnp
from jax import lax
import numpy as np

D_MODEL = 1024
BATCH = 32
SEQ = 256
DEPTH = 2
DEC_BATCH = 4
DEC_SEQ = 1024
PAST_LEN = 512

GRID_W = 64
ROPE_THETA = 10000.0
Q_BLOCK = 128
EPS = 1e-6

N_EVEN = (DEPTH + 1) // 2
N_ODD = DEPTH // 2

HALF = D_MODEL // 2
HY_CH = HALF
HY_ORDER = 2
HY_SHORT = 3
HY_BANDS = 16
HY_EMB = 1 + 2 * HY_BANDS
HY_FILT = 64
HY_DECAY_TARGET = 1e-2
HY_FAST_PCT = 0.3
HY_SLOW_PCT = 1.5
HY_MAX_DECAY = math.log(HY_DECAY_TARGET) / HY_FAST_PCT
HY_MIN_DECAY = math.log(HY_DECAY_TARGET) / HY_SLOW_PCT

DF_HEADS = 4
DF_DK = 64
DF_DV = 2 * DF_DK
DF_QK = DF_HEADS * 2 * DF_DK

GQ_HEADS = 8
GQ_KV = 2
GQ_REP = GQ_HEADS // GQ_KV
GQ_DH = D_MODEL // GQ_HEADS

EV_IN = 3 * HY_CH + HY_CH + 2 * DF_QK + DF_HEADS * DF_DV + HALF
EV_SPLITS = [3 * HY_CH, 4 * HY_CH, 4 * HY_CH + DF_QK, 4 * HY_CH + 2 * DF_QK,
             4 * HY_CH + 2 * DF_QK + DF_HEADS * DF_DV]
OD_IN = D_MODEL + 2 * GQ_KV * GQ_DH + D_MODEL
OD_SPLITS = [D_MODEL, D_MODEL + GQ_KV * GQ_DH, D_MODEL + 2 * GQ_KV * GQ_DH]

kernel_name = "hybrid_hyena_diffattn_gqa_prefix_dit_step"


def rms_norm(x, g):
    xf = x.astype(jnp.float32)
    y = xf * lax.rsqrt(jnp.mean(xf * xf, axis=-1, keepdims=True) + EPS)
    return (y * g.astype(jnp.float32)).astype(x.dtype)


def adaln_in(x, g, mod):
    shift, scale, gate = jnp.split(mod[:, None, :], 3, axis=-1)
    return rms_norm(x, g) * (1 + scale) + shift, gate


def grid_rope(length, head_dim):
    rows = length // GRID_W
    row = jnp.repeat(jnp.arange(rows, dtype=jnp.float32), GRID_W)
    col = jnp.tile(jnp.arange(GRID_W, dtype=jnp.float32), rows)
    half = head_dim // 2
    inv = ROPE_THETA ** (-jnp.arange(0, half, 2, dtype=jnp.float32) / half)
    ang = jnp.concatenate([row[:, None] * inv, col[:, None] * inv], axis=-1)
    return jnp.cos(ang), jnp.sin(ang)


def apply_rope(x, cos, sin):
    shp = x.shape
    half = shp[-1] // 2
    xf = x.astype(jnp.float32).reshape(shp[:-1] + (half, 2))
    bshape = (1, shp[1]) + (1,) * (x.ndim - 3) + (half,)
    c, s = cos.reshape(bshape), sin.reshape(bshape)
    x0, x1 = xf[..., 0], xf[..., 1]
    out = jnp.stack([x0 * c - x1 * s, x0 * s + x1 * c], axis=-1)
    return out.reshape(shp).astype(x.dtype)


def sweep_query_blocks(fn, q):
    b, sq = q.shape[:2]
    nb = sq // Q_BLOCK
    qb = jnp.moveaxis(q.reshape((b, nb, Q_BLOCK) + q.shape[2:]), 1, 0)
    out = jnp.moveaxis(lax.map(fn, qb), 0, 1)
    return out.reshape((b, sq) + out.shape[3:])


def diff_attention(q, k, v, lam):
    scale = DF_DK ** -0.5

    def block(qb):
        s = jnp.einsum('bqhmd,bkhmd->bhmqk', qb, k).astype(jnp.float32) * scale
        p = jax.nn.softmax(s, axis=-1)
        w = p[:, :, 0] - lam * p[:, :, 1]
        return jnp.einsum('bhqk,bkhd->bqhd', w.astype(v.dtype), v)

    return sweep_query_blocks(block, q)


def gqa_attention(q, k, v):
    scale = GQ_DH ** -0.5

    def block(qb):
        s = jnp.einsum('bqgrd,bkgd->bgrqk', qb, k).astype(jnp.float32) * scale
        p = jax.nn.softmax(s, axis=-1)
        return jnp.einsum('bgrqk,bkgd->bqgrd', p.astype(v.dtype), v)

    return sweep_query_blocks(block, q)


def short_conv(x, w, b):
    y = lax.conv_general_dilated(
        x, w[:, None, :].astype(x.dtype), window_strides=(1,),
        padding=((HY_SHORT // 2, HY_SHORT // 2),),
        dimension_numbers=('NWC', 'WIO', 'NWC'), feature_group_count=x.shape[-1])
    return y + b


def hyena_filters(length, w1, b1, freq, w2, b2, w3):
    f32 = jnp.float32
    t = jnp.linspace(0.0, 1.0, length, dtype=f32)[:, None]
    t_idx = jnp.arange(length, dtype=f32)[:, None]
    bands = jnp.linspace(1e-4, HY_BANDS - 1, HY_BANDS, dtype=f32)[None, :]
    ang = 2.0 * math.pi * t_idx * bands / length
    z = jnp.concatenate([t, jnp.cos(ang), -jnp.sin(ang)], axis=-1)
    fr = freq.astype(f32)
    hid = jnp.sin(fr * (z @ w1.astype(f32) + b1.astype(f32)))
    hid = jnp.sin(fr * (hid @ w2.astype(f32) + b2.astype(f32)))
    filt = (hid @ w3.astype(f32)).reshape(length, HY_ORDER, 2, HY_CH)
    deltas = jnp.abs(jnp.linspace(HY_MIN_DECAY, HY_MAX_DECAY, HY_CH, dtype=f32))
    window = jnp.exp(-t * deltas[None, :])
    return filt * window[:, None, None, :]


def bidir_long_conv(z, k_fwd, k_bwd, skip):
    length = z.shape[1]
    kc = jnp.concatenate([k_fwd, jnp.zeros_like(k_fwd[:1]), k_bwd[:0:-1]], axis=0)
    z32 = z.astype(jnp.float32)
    spec = jnp.fft.rfft(z32, n=2 * length, axis=1) * jnp.fft.rfft(kc, axis=0)[None]
    y = jnp.fft.irfft(spec, n=2 * length, axis=1)[:, :length]
    return (y + z32 * skip.astype(jnp.float32)).astype(z.dtype)


def even_mixer(h, ctx_k, ctx_v, rope, filt, lam, lam_init,
               w_in, w_out, conv_w, conv_b, skip, subln_g):
    b, length, _ = h.shape
    hy_u, hy_g, q, k, v, df_g = jnp.split(h @ w_in, EV_SPLITS, axis=-1)
    u = short_conv(hy_u, conv_w, conv_b)
    z, g1, g2 = jnp.split(u, 3, axis=-1)
    for o, g in enumerate((g1, g2)):
        z = g * bidir_long_conv(z, filt[:, o, 0], filt[:, o, 1], skip[o])
    y_a = z * jax.nn.silu(hy_g)
    q = q.reshape(b, length, DF_HEADS, 2, DF_DK)
    k = k.reshape(b, length, DF_HEADS, 2, DF_DK)
    v = v.reshape(b, length, DF_HEADS, DF_DV)
    new_k, new_v = k, v
    if rope is not None:
        q = apply_rope(q, *rope)
        k = jnp.concatenate([ctx_k.astype(k.dtype), apply_rope(k, *rope)], axis=1)
        v = jnp.concatenate([ctx_v.astype(v.dtype), v], axis=1)
    o = diff_attention(q, k, v, lam)
    o = rms_norm(o, subln_g) * (1.0 - lam_init)
    y_b = o.reshape(b, length, DF_HEADS * DF_DV) * jax.nn.silu(df_g)
    return jnp.concatenate([y_a, y_b], axis=-1) @ w_out, (new_k, new_v)


def odd_mixer(h, ctx_k, ctx_v, rope, w_in, w_out, q_g, k_g):
    b, length, _ = h.shape
    q, k, v, g = jnp.split(h @ w_in, OD_SPLITS, axis=-1)
    q = rms_norm(q.reshape(b, length, GQ_HEADS, GQ_DH), q_g)
    k = rms_norm(k.reshape(b, length, GQ_KV, GQ_DH), k_g)
    v = v.reshape(b, length, GQ_KV, GQ_DH)
    new_k, new_v = k, v
    if rope is not None:
        q = apply_rope(q, *rope)
        k = jnp.concatenate([ctx_k.astype(k.dtype), apply_rope(k, *rope)], axis=1)
        v = jnp.concatenate([ctx_v.astype(v.dtype), v], axis=1)
    q = q.reshape(b, length, GQ_KV, GQ_REP, GQ_DH)
    o = gqa_attention(q, k, v).reshape(b, length, D_MODEL)
    return (o * jax.nn.silu(g)) @ w_out, (new_k, new_v)


def setup_inputs(seed: int = 0) -> dict:
    key = jax.random.key(seed)
    ks = iter(jax.random.split(key, 32))
    f32 = jnp.float32

    def nrm(shape, scale):
        return jax.random.normal(next(ks), shape, f32) * scale

    return {
        "x_prompt": nrm((BATCH, SEQ, D_MODEL), 1.0),
        "x_sample": nrm((DEC_BATCH, DEC_SEQ, D_MODEL), 1.0),
        "cache_diff_k": nrm((DEC_BATCH, N_EVEN, PAST_LEN, DF_HEADS, 2, DF_DK), 1.0),
        "cache_diff_v": nrm((DEC_BATCH, N_EVEN, PAST_LEN, DF_HEADS, DF_DV), 1.0),
        "cache_gqa_k": nrm((DEC_BATCH, N_ODD, PAST_LEN, GQ_KV, GQ_DH), 1.0),
        "cache_gqa_v": nrm((DEC_BATCH, N_ODD, PAST_LEN, GQ_KV, GQ_DH), 1.0),
        "c": nrm((DEC_BATCH, D_MODEL), 1.0),
        "c_ctx": nrm((D_MODEL,), 1.0),
        "w_mod": nrm((DEPTH, D_MODEL, 3 * D_MODEL), 0.5 * D_MODEL ** -0.5),
        "b_mod": nrm((DEPTH, 3 * D_MODEL), 0.02),
        "norm_g": 1.0 + nrm((DEPTH, D_MODEL), 0.02),
        "final_g": 1.0 + nrm((D_MODEL,), 0.02),
        "ev_w_in": nrm((N_EVEN, D_MODEL, EV_IN), D_MODEL ** -0.5),
        "ev_w_out": nrm((N_EVEN, 2 * HALF, D_MODEL), (2 * HALF) ** -0.5),
        "hy_conv_w": nrm((N_EVEN, HY_SHORT, 3 * HY_CH), HY_SHORT ** -0.5),
        "hy_conv_b": nrm((N_EVEN, 3 * HY_CH), 0.02),
        "hy_w1": nrm((N_EVEN, HY_EMB, HY_FILT), HY_EMB ** -0.5),
        "hy_b1": nrm((N_EVEN, HY_FILT), 0.02),
        "hy_freq": 1.0 + nrm((N_EVEN, HY_FILT), 0.02),
        "hy_w2": nrm((N_EVEN, HY_FILT, HY_FILT), HY_FILT ** -0.5),
        "hy_b2": nrm((N_EVEN, HY_FILT), 0.02),
        "hy_w3": nrm((N_EVEN, HY_FILT, HY_ORDER * 2 * HY_CH), 0.05 * HY_FILT ** -0.5),
        "hy_skip": nrm((N_EVEN, HY_ORDER, HY_CH), 0.5),
        "df_lambda": nrm((N_EVEN, 4, DF_DK), 0.1),
        "df_subln_g": 1.0 + nrm((N_EVEN, DF_DV), 0.02),
        "od_w_in": nrm((N_ODD, D_MODEL, OD_IN), D_MODEL ** -0.5),
        "od_w_out": nrm((N_ODD, D_MODEL, D_MODEL), D_MODEL ** -0.5),
        "gq_q_g": 1.0 + nrm((N_ODD, GQ_DH), 0.02),
        "gq_k_g": 1.0 + nrm((N_ODD, GQ_DH), 0.02),
    }


def reference(x_prompt, x_sample, cache_diff_k, cache_diff_v, cache_gqa_k, cache_gqa_v,
              c, c_ctx, w_mod, b_mod, norm_g, final_g,
              ev_w_in, ev_w_out, hy_conv_w, hy_conv_b, hy_w1, hy_b1, hy_freq, hy_w2, hy_b2,
              hy_w3, hy_skip, df_lambda, df_subln_g,
              od_w_in, od_w_out, gq_q_g, gq_k_g):
    len_ctx = x_prompt.shape[1]
    len_lat = x_sample.shape[1]
    rope_df = grid_rope(len_lat, DF_DK)
    rope_gq = grid_rope(len_lat, GQ_DH)
    cond_ctx = jax.nn.silu(c_ctx)[None, :]
    cond_lat = jax.nn.silu(c)
    xp, xs = x_prompt, x_sample
    diff_k, diff_v, gqa_k, gqa_v = [], [], [], []
    for layer in range(DEPTH):
        j = layer // 2
        hp, gate_p = adaln_in(xp, norm_g[layer], cond_ctx @ w_mod[layer] + b_mod[layer])
        hs, gate_s = adaln_in(xs, norm_g[layer], cond_lat @ w_mod[layer] + b_mod[layer])
        if layer % 2 == 0:
            lam_init = 0.8 - 0.6 * math.exp(-0.3 * layer)
            lq1, lk1, lq2, lk2 = df_lambda[j].astype(jnp.float32)
            lam = jnp.exp(jnp.sum(lq1 * lk1)) - jnp.exp(jnp.sum(lq2 * lk2)) + lam_init
            filt_p = hyena_filters(len_ctx, hy_w1[j], hy_b1[j], hy_freq[j], hy_w2[j], hy_b2[j], hy_w3[j])
            filt_s = hyena_filters(len_lat, hy_w1[j], hy_b1[j], hy_freq[j], hy_w2[j], hy_b2[j], hy_w3[j])
            out_p, (kp, vp) = even_mixer(hp, None, None, None, filt_p, lam, lam_init,
                                         ev_w_in[j], ev_w_out[j], hy_conv_w[j], hy_conv_b[j],
                                         hy_skip[j], df_subln_g[j])
            out_s, _ = even_mixer(hs, cache_diff_k[:, j], cache_diff_v[:, j], rope_df, filt_s,
                                  lam, lam_init, ev_w_in[j], ev_w_out[j], hy_conv_w[j],
                                  hy_conv_b[j], hy_skip[j], df_subln_g[j])
            diff_k.append(kp)
            diff_v.append(vp)
        else:
            out_p, (kp, vp) = odd_mixer(hp, None, None, None, od_w_in[j], od_w_out[j],
                                        gq_q_g[j], gq_k_g[j])
            out_s, _ = odd_mixer(hs, cache_gqa_k[:, j], cache_gqa_v[:, j], rope_gq,
                                 od_w_in[j], od_w_out[j], gq_q_g[j], gq_k_g[j])
            gqa_k.append(kp)
            gqa_v.append(vp)
        xp = xp + gate_p * out_p
        xs = xs + gate_s * out_s
    y_prompt = rms_norm(xp, final_g)
    y_sample = rms_norm(xs, final_g)
    new_diff_k = jnp.stack(diff_k, axis=1)
    new_diff_v = jnp.stack(diff_v, axis=1)
    new_gqa_k = jnp.stack(gqa_k, axis=1)
    new_gqa_v = jnp.stack(gqa_v, axis=1)
    return (y_prompt, y_sample, new_diff_k, new_diff_v, new_gqa_k, new_gqa_v)
```

```python
import math
from contextlib import ExitStack

import numpy as np
import ml_dtypes

import concourse.bass as bass
import concourse.mybir as mybir
from concourse.bass_utils import run_bass_kernel_spmd

F32 = mybir.dt.float32
BF16 = mybir.dt.bfloat16
ALU = mybir.AluOpType
AF = mybir.ActivationFunctionType
AX = mybir.AxisListType

_DTSIZE = {F32: 4, BF16: 2}
ENGS = ("pe", "act", "dve", "pool", "sp")
SKIP_SAME_ENGINE_WAR = False
PREFETCH_MOD1 = True


def _region(ap):
    t = ap.tensor
    name = t.name
    esz = _DTSIZE.get(ap.dtype, 4)
    dims = list(ap.ap)
    off = int(ap.offset)
    space = str(ap.space)
    if space == "PSUM":
        pstep = dims[0][0] if dims[0][0] else 1
        foff = off % pstep
        lo = hi = foff
        for st, cnt in dims[1:]:
            if cnt > 1:
                if st > 0:
                    hi += st * (cnt - 1)
                else:
                    lo += st * (cnt - 1)
        return (name, 0, 128, (lo * esz // 2048) * 2048, (hi * esz // 2048 + 1) * 2048)
    if space == "SB":
        pstep, pcnt = dims[0]
        if pstep == 0:
            row = int(np.prod(t.shape[1:]))
            p0 = off // row
            foff = off % row
            pcnt = 1
        else:
            p0 = off // pstep
            foff = off % pstep
        lo = hi = foff
        for st, cnt in dims[1:]:
            if cnt > 1:
                if st > 0:
                    hi += st * (cnt - 1)
                else:
                    lo += st * (cnt - 1)
        return (name, p0, p0 + pcnt, lo * esz, (hi + 1) * esz)
    lo = hi = off
    for st, cnt in dims:
        if cnt > 1:
            if st > 0:
                hi += st * (cnt - 1)
            else:
                lo += st * (cnt - 1)
    return (name, 0, 1, lo * esz, (hi + 1) * esz)


def _ovl(a, b):
    return a[1] < b[2] and b[1] < a[2] and a[3] < b[4] and b[3] < a[4]


def _covers(a, b):
    return a[1] <= b[1] and a[2] >= b[2] and a[3] <= b[3] and a[4] >= b[4]


class Op:
    __slots__ = ("eng", "fn", "idx", "deps", "signal", "semval", "is_dma", "dsem", "dval", "tag", "phase")

    def __init__(self, eng, fn, is_dma, tag=""):
        self.eng = eng
        self.fn = fn
        self.is_dma = is_dma
        self.deps = {}
        self.signal = False
        self.semval = None
        self.dsem = None
        self.dval = None
        self.tag = tag


class Sched:
    def __init__(self, nc, n_dma_sems=16):
        self.nc = nc
        self.ops = {e: [] for e in ENGS}
        self.writers = {}
        self.readers = {}
        self.n_dma_sems = n_dma_sems
        self.ro = set()
        self.phase = "setup"

    def _add_dep(self, op, d, raw=True):
        if d is op:
            return
        if SKIP_SAME_ENGINE_WAR and (not raw) and (not d.is_dma) and (not op.is_dma) and d.eng == op.eng:
            return
        if d.is_dma:
            op.deps[("dma", id(d))] = d
        else:
            cur = op.deps.get(d.eng)
            if cur is None or cur.idx < d.idx:
                op.deps[d.eng] = d

    def record(self, eng, fn, reads, writes, is_dma=False, tag=""):
        op = Op(eng, fn, is_dma, tag)
        op.phase = self.phase
        op.idx = len(self.ops[eng])
        rregs = [_region(a) for a in reads if a is not None and not isinstance(a, (int, float))]
        wregs = [_region(a) for a in writes if a is not None]
        for r in rregs:
            if r[0].startswith("pball") and r not in wregs:
                wregs.append(r)
        for r in rregs:
            if r[0] in self.ro:
                continue
            for wr, wop in self.writers.get(r[0], ()):
                if _ovl(wr, r):
                    self._add_dep(op, wop)
        for w in wregs:
            wl = self.writers.setdefault(w[0], [])
            for wr, wop in wl:
                if _ovl(wr, w):
                    self._add_dep(op, wop, raw=False)
            rl = self.readers.setdefault(w[0], [])
            for rr, rops in rl:
                if _ovl(rr, w):
                    for rop in rops.values():
                        self._add_dep(op, rop, raw=False)
        for w in wregs:
            wl = self.writers[w[0]]
            wl[:] = [e for e in wl if not _covers(w, e[0])]
            wl.append([w, op])
            rl = self.readers[w[0]]
            rl[:] = [e for e in rl if not _covers(w, e[0])]
        for r in rregs:
            if r[0] in self.ro:
                continue
            rl = self.readers.setdefault(r[0], [])
            key = ("dma", id(op)) if is_dma else eng
            for e in rl:
                if e[0] == r:
                    e[1][key] = op
                    break
            else:
                rl.append([r, {key: op}])
        for d in op.deps.values():
            d.signal = True
        self.ops[eng].append(op)
        return op

    def emit(self, stack):
        nc = self.nc
        sems = {e: stack.enter_context(nc.semaphore("s_" + e)) for e in ENGS}
        dsems = {e: [stack.enter_context(nc.semaphore(f"d_{e}_{i}")) for i in range(self.n_dma_sems)]
                 for e in ("sp", "pool", "act")}
        all_dma = []
        for e in ENGS:
            c = 0
            dcount = [0] * self.n_dma_sems
            kk = 0
            for op in self.ops[e]:
                if op.is_dma:
                    j = kk % self.n_dma_sems
                    kk += 1
                    dcount[j] += 16
                    op.dsem = dsems[e][j]
                    op.dval = dcount[j]
                    all_dma.append(op)
                elif op.signal:
                    c += 1
                    op.semval = c
        block = stack.enter_context(nc.Block())
        eng_obj = {"pe": "tensor", "act": "scalar", "dve": "vector", "pool": "gpsimd", "sp": "sync"}

        def make(e):
            def body(engine):
                waited = {}

                def wait(sem, val):
                    if waited.get(sem.num, 0) >= val:
                        return
                    waited[sem.num] = val
                    engine.wait_ge(sem, val)

                for op in self.ops[e]:
                    for d in op.deps.values():
                        if d.is_dma:
                            wait(d.dsem, d.dval)
                        else:
                            if d.eng == e and e == "pe":
                                continue
                            wait(sems[d.eng], d.semval)
                    if op.is_dma and op.dval > 16:
                        wait(op.dsem, op.dval - 16)
                    ins = op.fn(engine)
                    if op.is_dma:
                        ins.then_inc(op.dsem, 16)
                    elif op.signal:
                        ins.then_inc(sems[e], 1)
                if e == "sp":
                    last = {}
                    for op in all_dma:
                        last[op.dsem.num] = (op.dsem, op.dval)
                    for sem, val in last.values():
                        engine.wait_ge(sem, val)
            return body

        for e in ENGS:
            getattr(block, eng_obj[e])(make(e))


class K:
    def __init__(self, nc, sched):
        self.nc = nc
        self.s = sched

    def mm(self, out, lhsT, rhs, start=True, stop=True, sgc=False, tag="mm"):
        if sgc:
            fn = lambda e: e.matmul(out, lhsT, rhs, start=start, stop=stop, skip_group_check=True)
        else:
            fn = lambda e: e.matmul(out, lhsT, rhs, start=start, stop=stop)
        return self.s.record("pe", fn, [lhsT, rhs] + ([] if start else [out]), [out], tag=tag)

    def transpose(self, out, in_, ident, tag="tr"):
        return self.s.record("pe", lambda e: e.transpose(out, in_, ident), [in_, ident], [out], tag=tag)

    def act(self, out, in_, func, bias=None, scale=None, accum_out=None, tag="act"):
        kw = {}
        if bias is not None:
            kw["bias"] = bias
        if scale is not None:
            kw["scale"] = scale
        if accum_out is not None:
            kw["accum_out"] = accum_out
        rd = [in_] + [a for a in (bias, scale) if a is not None and not isinstance(a, (int, float))]
        wr = [out] + ([accum_out] if accum_out is not None else [])
        return self.s.record("act", lambda e: e.activation(out, in_, func, **kw), rd, wr, tag=tag)

    def tt(self, eng, out, in0, in1, op, tag="tt"):
        return self.s.record(eng, lambda e: e.tensor_tensor(out, in0, in1, op), [in0, in1], [out], tag=tag)

    def ts(self, eng, out, in0, s1, s2, op0, op1=None, tag="ts"):
        rd = [in0] + [a for a in (s1, s2) if a is not None and not isinstance(a, (int, float))]
        if op1 is None:
            return self.s.record(eng, lambda e: e.tensor_scalar(out, in0, s1, None, op0), rd, [out], tag=tag)
        return self.s.record(eng, lambda e: e.tensor_scalar(out, in0, s1, s2, op0, op1), rd, [out], tag=tag)

    def stt(self, eng, out, in0, scalar, in1, op0, op1, tag="stt"):
        rd = [in0, in1] + ([scalar] if not isinstance(scalar, (int, float)) else [])
        return self.s.record(eng, lambda e: e.scalar_tensor_tensor(out, in0, scalar, in1, op0, op1),
                             rd, [out], tag=tag)

    def copy(self, eng, out, in_, tag="copy"):
        if eng == "act":
            return self.s.record(eng, lambda e: e.copy(out, in_), [in_], [out], tag=tag)
        return self.s.record(eng, lambda e: e.tensor_copy(out, in_), [in_], [out], tag=tag)

    def memset(self, eng, out, val, tag="memset"):
        return self.s.record(eng, lambda e: e.memset(out, val), [], [out], tag=tag)

    def recip(self, out, in_, tag="recip"):
        return self.s.record("dve", lambda e: e.reciprocal(out, in_), [in_], [out], tag=tag)

    def rsum(self, out, in_, tag="rsum"):
        return self.s.record("dve", lambda e: e.reduce_sum(out, in_, AX.X), [in_], [out], tag=tag)

    def dma(self, q, out, in_, tag="dma"):
        return self.s.record(q, lambda e: e.dma_start(out=out, in_=in_), [in_], [out], is_dma=True, tag=tag)


D = 1024
NPS = 4
LP = 256
LS = 1024
PAST = 512
EPS = 1e-6
MAGIC = 12582912.0
TWO_PI = 2.0 * math.pi
LAM_INIT0 = 0.8 - 0.6 * math.exp(-0.3 * 0)

STAGES = ("l0", "l1", "final")
import os
EXPERIMENT = os.environ.get("KEXP", "")


class _Stop(Exception):
    pass


def build_program(stages=STAGES, dbg=None, stop_at=None):
    nc = bass.Bass("TRN2", target_bir_lowering=False)

    def din(name, shape, dt=F32):
        return nc.dram_tensor(name, list(shape), dt, kind="ExternalInput").ap()

    def dout(name, shape, dt=F32):
        return nc.dram_tensor(name, list(shape), dt, kind="ExternalOutput").ap()

    d_xp = din("xp", [NPS * LP, D])
    d_xs = din("xs", [LS, D])
    d_cdk = din("cdk", [PAST, 512])
    d_cdv = din("cdv", [PAST, 512])
    d_cgk = din("cgk", [PAST, 256])
    d_cgv = din("cgv", [PAST, 256])
    d_vecA = din("vecA", [128, 128])
    d_vecB = din("vecB", [128, 128])
    d_wmod = din("w_mod", [2, D, 3 * D])
    d_evin = din("ev_w_in", [D, 4096])
    d_evout = din("ev_w_out", [D, D])
    d_odin = din("od_w_in", [D, 2560])
    d_odout = din("od_w_out", [D, D])
    d_w1 = din("hy_w1", [33, 64])
    d_w2 = din("hy_w2", [64, 64])
    d_w3 = din("hy_w3", [64, 2048])
    d_skip = din("hy_skip", [2, 512])
    d_fg = din("final_g", [D])
    d_lam = din("df_lambda", [1, 256])
    d_gkg = din("gq_k_g", [128])
    d_identb = din("ident_bf", [128, 128], BF16)
    d_identf = din("ident_f", [128, 128])
    d_perm = din("perm_bf", [128, 128], BF16)
    d_ropeD = din("ropeD", [2, 128, LS])
    d_ropeG = din("ropeG", [2, 128, LS])
    d_zposP = din("zposP", [33, LP])
    d_zposS = din("zposS", [33, LS])
    d_winP = din("winP", [2, LP, 512])
    d_winS = din("winS", [2, LS, 512])
    d_FfP = din("FfP", [128, 2, 512], BF16)
    d_FinvP = din("FinvP", [128, 4, 256], BF16)
    d_FfS = din("FfS", [4, 128, 8, 256], BF16)
    d_FinvS = din("FinvS", [8, 128, 16, 128], BF16)
    d_acol = din("acol", [128, 2])

    o_yp = dout("y_p", [NPS * LP, D])
    o_ys = dout("y_s", [LS // 2, D])
    o_ndk = dout("ndk", [NPS * LP, 512])
    o_ndv = dout("ndv", [NPS * LP, 512])
    o_ngk = dout("ngk", [NPS * LP, 256])
    o_ngv = dout("ngv", [NPS * LP, 256])

    S = Sched(nc)
    k = K(nc, S)
    for a in (d_xp, d_xs, d_cdk, d_cdv, d_cgk, d_cgv, d_vecA, d_vecB, d_wmod, d_evin, d_evout, d_odin, d_odout,
              d_w1, d_w2, d_w3, d_skip, d_fg, d_lam, d_gkg, d_identb, d_identf, d_perm, d_ropeD, d_ropeG,
              d_zposP, d_zposS, d_winP, d_winS, d_FfP, d_FinvP, d_FfS, d_FinvS, d_acol):
        S.ro.add(a.tensor.name)

    dbg_outs = {}

    with ExitStack() as st:
        def sb(name, shape, dt):
            return st.enter_context(nc.sbuf_tensor("sb_" + name, list(shape), dt))

        xP = sb("xP", [128, 8, D], F32)
        xS = sb("xS", [128, 8, D], F32)
        hT = sb("hT", [128, 8, 1024], BF16)
        ybuf = sb("ybuf", [128, 8, 1024], BF16)
        wbuf = [sb(f"wbuf{i}", [128, 8, 256], BF16) for i in range(2)]
        REG_BYTES = 58 * 1024
        region = sb("region", [128, REG_BYTES // 2], BF16)
        ring = [sb(f"ring{i}", [128, 2048], BF16) for i in range(3)]
        spec = sb("spec", [128, 16, 256], BF16)
        gate_bc = sb("gate_bc", [128, D], F32)
        ident_b = sb("ident_b", [128, 128], BF16)
        ident_f = sb("ident_f", [128, 128], F32)
        perm_b = sb("perm_b", [128, 128], BF16)
        ones_f = sb("ones_f", [128, 128], F32)
        ones_b = sb("ones_b", [128, 128], BF16)
        epsc = sb("epsc", [128, 1], F32)
        acol = sb("acol", [128, 2], F32)
        vA = sb("vA", [128, 128], F32)
        vB = sb("vB", [128, 128], F32)
        vrow = sb("vrow", [128, 128], F32)
        sc = sb("sc", [128, 8, 2], BF16)
        modTs = [sb(f"modT{l}", [128, 24, 2], F32) for l in range(2)]
        gsTs = [sb(f"gsT{l}", [128, 2, 8], F32) for l in range(2)]
        modw = sb("modw", [128, 8, 256], BF16)
        cur = {"modT": modTs[0], "gsT": gsTs[0]}
        diag = [sb(f"diag{i}", [128, 128], F32) for i in range(2)]
        ss = sb("ss", [128, 8], F32)
        rs = sb("rs", [128, 8], F32)
        FfP = sb("FfP", [128, 2, 512], BF16)
        FinvP = sb("FinvP", [128, 4, 256], BF16)
        w1s = sb("w1s", [33, 64], F32)
        w2s = sb("w2s", [64, 64], F32)
        w3b = sb("w3b", [64, 2048], BF16)
        hid2b = sb("hid2b", [64, 1024], BF16)
        lamrow = sb("lamrow", [1, 256], F32)
        lamt = sb("lamt", [1, 8], F32)
        neglam = sb("neglam", [128, 1], F32)
        subg = sb("subg", [128, 1], F32)
        small = sb("small", [128, 64], F32)

        pball = st.enter_context(nc.psum_tensor("pball", [128, 4096], F32))
        pb = [pball[:, i * 512:(i + 1) * 512] for i in range(8)]

        def pbf(i):
            return pb[i].bitcast(BF16)

        def RV(off, shape, dt):
            n = int(np.prod(shape))
            esz = 2 if dt == BF16 else 4
            assert off % 4 == 0 and off + n * esz <= REG_BYTES, (off, shape)
            v = region[:, off // 2:(off + n * esz) // 2]
            if dt == F32:
                v = v.bitcast(F32)
            if len(shape) == 2:
                v = v.rearrange("p (a b) -> p a b", a=shape[0])
            elif len(shape) == 3:
                v = v.rearrange("p (a b c) -> p a b c", a=shape[0], b=shape[1])
            return v

        def dbg_dump(name, ap, shape, dt=F32):
            if dbg is not None and name in dbg:
                o = dout("dbg_" + name, shape, dt)
                k.dma("sp", o, ap)
                dbg_outs[name] = o

        rot = {"p": 0, "t": 0}

        def mark(label):
            S.phase = "after_" + label
            if stop_at is not None and label == stop_at:
                raise _Stop()

        pp_pool = {"banks": [0, 1]}

        def set_pp(banks):
            pp_pool["banks"] = list(banks)

        def next_pp():
            rot["p"] = (rot["p"] + 1) % len(pp_pool["banks"])
            return pb[pp_pool["banks"][rot["p"]]]

        pt_fixed = {"on": False}

        def next_pt():
            if pt_fixed["on"]:
                return 3
            rot["t"] ^= 1
            return 2 + rot["t"]

        k.dma("sp", ident_b[:], d_identb)
        k.dma("sp", ident_f[:], d_identf)
        k.dma("sp", perm_b[:], d_perm)
        k.dma("sp", acol[:], d_acol)
        k.dma("sp", vrow[:], d_vecA)
        k.memset("dve", ones_f[:], 1.0)
        k.memset("dve", ones_b[:], 1.0)
        k.memset("dve", epsc[:], EPS)
        k.transpose(pb[2][:, 0:128], vrow[:], ident_f[:])
        k.copy("dve", vA[:], pb[2][:, 0:128])
        k.dma("sp", vrow[:], d_vecB)
        k.transpose(pb[3][:, 0:128], vrow[:], ident_f[:])
        k.copy("dve", vB[:], pb[3][:, 0:128])
        k.dma("sp", FfP[:], d_FfP)
        k.dma("sp", FinvP[:], d_FinvP)
        k.dma("sp", w1s[:], d_w1)
        k.dma("sp", w2s[:], d_w2)
        k.dma("pool", w3b[:], d_w3)
        for t in range(8):
            k.dma("sp", xP[:, t, :], d_xp[t * 128:(t + 1) * 128, :])
        k.act(sc[:, :, 0], vA[:, 72:80], AF.Silu)
        k.act(sc[:, :, 1], vA[:, 80:88], AF.Silu)
        k.dma("sp", lamrow[:], d_lam)
        k.tt("dve", lamrow[:, 0:64], lamrow[:, 0:64], lamrow[:, 64:128], ALU.mult)
        k.tt("dve", lamrow[:, 128:192], lamrow[:, 128:192], lamrow[:, 192:256], ALU.mult)
        k.rsum(lamt[:, 0:1], lamrow[:, 0:64])
        k.rsum(lamt[:, 1:2], lamrow[:, 128:192])
        k.act(lamt[:, 2:4], lamt[:, 0:2], AF.Exp)
        k.tt("dve", lamt[:, 4:5], lamt[:, 3:4], lamt[:, 2:3], ALU.subtract)
        k.ts("dve", lamt[:, 5:6], lamt[:, 4:5], -LAM_INIT0, None, ALU.add)
        k.mm(pb[2][:, 0:1], ones_f[0:1, :], lamt[0:1, 5:6])
        k.copy("dve", neglam[:], pb[2][:, 0:1])
        k.ts("dve", subg[:], vB[:, 19:20], 1.0 - LAM_INIT0, None, ALU.mult)

        def wblock_dummy():
            pass

        def wblock(w2d, c0, ncols, buf):
            k.dma("pool", buf[:, :, 0:ncols],
                  w2d[:, c0:c0 + ncols].rearrange("(kk p) c -> p kk c", p=128))

        wrot = {"i": 0}

        def next_wbuf():
            wrot["i"] ^= 1
            return wbuf[wrot["i"]]

        def mod_block(l, blk, wb, pm):
            modT = modTs[l]
            wblock(d_wmod[l], blk * 256, 256, wb)
            for cc in range(2):
                for kc in range(8):
                    k.mm(pm[:, cc * 2:cc * 2 + 2], wb[:, kc, cc * 128:(cc + 1) * 128], sc[:, kc, :],
                         start=(kc == 0), stop=(kc == 7))
            c0 = blk * 2
            pm3 = pm[:, 0:4].rearrange("p (c n) -> p c n", n=2)
            for cond in range(2):
                k.tt("dve", modT[:, c0:c0 + 2, cond], pm3[:, :, cond],
                     vA[:, 24 + 24 * l + c0:24 + 24 * l + c0 + 2], ALU.add)

        def mod_finish(l):
            for cond in range(2):
                k.stt("dve", gsTs[l][:, cond, :], modTs[l][:, 8:16, cond], 1.0, vA[:, 8 * l:8 * l + 8],
                      ALU.add, ALU.mult)

        def compute_mod(l):
            for blk in range(12):
                mod_block(l, blk, next_wbuf(), pb[2])
            mod_finish(l)

        def compute_gate_bc(cond):
            for j in range(8):
                dg = diag[j % 2]
                k.ts("dve", dg[:], ident_f[:], cur["modT"][:, 16 + j, cond:cond + 1], None, ALU.mult)
                k.mm(pb[4 + j // 4][:, (j % 4) * 128:(j % 4 + 1) * 128], ones_f[:], dg[:])
            k.copy("dve", gate_bc[:, 0:512], pb[4][:])
            k.copy("dve", gate_bc[:, 512:1024], pb[5][:])

        pre_rs = {}

        def adaln_stats(xg, ntiles, key, junk_off):
            ss_ = sb("ss_" + key, [128, 8], F32)
            rs_ = sb("rs_" + key, [128, 8], F32)
            junk = RV(junk_off, [1024], BF16)
            k.memset("dve", ss_[:], 0.0)
            for t in range(ntiles):
                k.act(junk, xg[:, t, :], AF.Square, accum_out=ss_[:, t:t + 1])
            k.act(rs_[:, 0:ntiles], ss_[:, 0:ntiles], AF.Ln, bias=epsc[:], scale=1.0 / D)
            k.act(rs_[:, 0:ntiles], rs_[:, 0:ntiles], AF.Exp, scale=-0.5)
            pre_rs[key] = rs_

        def adaln(xg, ntiles, cond, pre=None):
            xn = [RV(i * 2048, [1024], BF16) for i in range(8)]
            if pre is not None and pre in pre_rs:
                rs_u = pre_rs[pre]
            else:
                rs_u = rs
                k.memset("dve", ss[:], 0.0)
                for t in range(ntiles):
                    k.act(xn[t], xg[:, t, :], AF.Square, accum_out=ss[:, t:t + 1])
                k.act(rs[:, 0:ntiles], ss[:, 0:ntiles], AF.Ln, bias=epsc[:], scale=1.0 / D)
                k.act(rs[:, 0:ntiles], rs[:, 0:ntiles], AF.Exp, scale=-0.5)
            for t in range(ntiles):
                k.ts("dve", xn[t], xg[:, t, :], rs_u[:, t:t + 1], None, ALU.mult)
            for half in range(ntiles // 4):
                for j in range(8):
                    bank = pbf(next_pt())
                    for tt_ in range(4):
                        k.transpose(bank[:, tt_ * 128:(tt_ + 1) * 128],
                                    xn[half * 4 + tt_][:, j * 128:(j + 1) * 128], ident_b[:])
                    if j % 2 == 0:
                        k.act(hT[:, j, half * 512:(half + 1) * 512], bank[:, 0:512], AF.Identity,
                              bias=cur["modT"][:, j, cond:cond + 1], scale=cur["gsT"][:, cond, j:j + 1])
                    else:
                        k.ts("dve", hT[:, j, half * 512:(half + 1) * 512], bank[:, 0:512],
                             cur["gsT"][:, cond, j:j + 1], cur["modT"][:, j, cond:cond + 1], ALU.mult, ALU.add)

        deferred = []

        def defer(fn, delay=1):
            deferred.append([delay, fn])

        def run_deferred(flush=False):
            for e_ in deferred:
                e_[0] -= 1
            while deferred and (flush or deferred[0][0] <= 0):
                deferred.pop(0)[1]()

        def proj_fm(w2d, c0, nblk, ntok, evac):
            for b in range(nblk):
                wb = next_wbuf()
                wblock(w2d, c0 + b * 256, 256, wb)
                for cc in range(2):
                    for tb in range(ntok // 512):
                        ps = next_pp()
                        for kc in range(8):
                            k.mm(ps[:], wb[:, kc, cc * 128:(cc + 1) * 128], hT[:, kc, tb * 512:(tb + 1) * 512],
                                 start=(kc == 0), stop=(kc == 7))
                        run_deferred()
                        evac(b * 2 + cc, tb, ps)
            run_deferred(flush=True)

        def proj_tm(w2d, c0, nblk, ntiles, evac):
            for b in range(nblk):
                wb = next_wbuf()
                wblock(w2d, c0 + b * 256, 256, wb)
                for t in range(ntiles):
                    ps = next_pp()
                    for kc in range(8):
                        k.mm(ps[:, 0:256], hT[:, kc, t * 128:(t + 1) * 128], wb[:, kc, :],
                             start=(kc == 0), stop=(kc == 7))
                    run_deferred()
                    evac(b, t, ps)
            run_deferred(flush=True)

        def wout_residual(w2d, xg, ntiles):
            tmp = [RV(i * 1024, [256], F32) for i in range(4)]
            set_pp([0, 1, 4, 5, 6, 7])
            for b in range(4):
                wb = next_wbuf()
                wblock(w2d, b * 256, 256, wb)
                for t in range(ntiles):
                    ps = next_pp()
                    for kc in range(8):
                        k.mm(ps[:, 0:256], ybuf[:, kc, t * 128:(t + 1) * 128], wb[:, kc, :],
                             start=(kc == 0), stop=(kc == 7))
                    tm = tmp[t % 4]
                    k.tt("dve", tm, ps[:, 0:256], gate_bc[:, b * 256:(b + 1) * 256], ALU.mult)
                    k.tt("dve", xg[:, t, b * 256:(b + 1) * 256], xg[:, t, b * 256:(b + 1) * 256], tm, ALU.add)
            set_pp([0, 1])

        def sin_rr(out, ps_in, bcol, fcol, tmp_a, tmp_b, npart):
            k.ts("dve", tmp_a, ps_in, bcol, fcol, ALU.add, ALU.mult)
            k.ts("dve", tmp_b, tmp_a, 1.0 / TWO_PI, MAGIC, ALU.mult, ALU.add)
            k.ts("dve", tmp_b, tmp_b, MAGIC, None, ALU.subtract)
            k.stt("dve", tmp_a, tmp_b, -TWO_PI, tmp_a, ALU.mult, ALU.add)
            k.act(out, tmp_a, AF.Sin)

        OFF_ZT = 0
        OFF_G1 = 8192
        OFF_GG = 16384
        OFF_Y = 24576
        OFF_RAW = 32768
        OFF_T = 41472
        OFF_SKIP = 51712

        def hyena_group(xg_cond, L, nseq, d_zpos, d_win, ff_piece, finv_piece):
            kt_n = L // 128
            rt_n = 2 * kt_n
            nsl = nseq * 256
            seglen = 256 if nseq == 4 else 512
            nseg = 1024 // seglen
            zT = RV(OFF_ZT, [kt_n, 2, nsl], BF16)
            g1T = RV(OFF_G1, [kt_n, 2, nsl], BF16)
            gg = RV(OFF_GG, [4, 1024], BF16)
            Y = RV(OFF_Y, [rt_n, nsl], BF16)
            fa = RV(OFF_RAW, [kt_n, 256], BF16)
            fb = RV(OFF_RAW + 4096, [kt_n, 256], BF16)
            raw = [RV(OFF_RAW + i * 4352, [nseg, seglen + 2], F32) for i in range(2)]
            skipbc = RV(OFF_SKIP, [2, 512], F32)
            U = RV(OFF_T, [nseg, seglen], F32)
            ubfs = [RV(OFF_T + 4096, [1024], BF16), RV(OFF_T + 8192, [1024], BF16)]
            sgt = [RV(OFF_T + 6144 + i * 1024, [512], BF16) for i in range(2)]
            zp = RV(OFF_T, [1024], F32)
            ha = RV(OFF_T + 4096, [512], F32)
            hb = RV(OFF_T + 6144, [512], F32)
            h1 = RV(OFF_RAW, [1024], F32)
            winf = [RV(OFF_T + i * 2048, [256], F32) for i in range(2)]
            winb = [RV(OFF_T + 2048 + i * 2048, [256], F32) for i in range(2)]
            winf = [RV(OFF_T + 0, [256], F32), RV(OFF_T + 2048, [256], F32)]
            winb = [RV(OFF_T + 1024, [256], F32), RV(OFF_T + 3072, [256], F32)]
            ft1 = RV(OFF_T + 4096, [256], F32)
            ft2 = RV(OFF_T + 5120, [256], F32)
            xr = [RV(OFF_T + i * 512, [256], BF16) for i in range(2)]
            xi = [RV(OFF_T + 1024 + i * 512, [256], BF16) for i in range(2)]
            t1 = RV(OFF_T + 2048, [256], BF16)
            t2 = RV(OFF_T + 2560, [256], BF16)
            t3 = RV(OFF_T + 3072, [256], BF16)
            t4 = RV(OFF_T + 3584, [256], BF16)
            ytm = [RV(OFF_T + 8192 + i * 1024, [512], BF16) for i in range(2)]

            for o in range(2):
                k.dma("sp", skipbc[:, o, :], d_skip[o].partition_broadcast(128))

            k.dma("sp", zp[0:33, 0:L], d_zpos)
            nb = max(1, L // 512)
            bw = min(L, 512)
            for tb in range(nb):
                ps = next_pp()
                k.mm(ps[0:64, 0:bw], w1s[0:33, :], zp[0:33, tb * bw:(tb + 1) * bw])
                sin_rr(h1[0:64, tb * bw:(tb + 1) * bw], ps[0:64, 0:bw], vB[0:64, 12:13], vB[0:64, 13:14],
                       ha[0:64, 0:bw], hb[0:64, 0:bw], 64)
            for tb in range(nb):
                ps = next_pp()
                k.mm(ps[0:64, 0:bw], w2s[0:64, :], h1[0:64, tb * bw:(tb + 1) * bw])
                sin_rr(hid2b[0:64, tb * bw:(tb + 1) * bw], ps[0:64, 0:bw], vB[0:64, 14:15], vB[0:64, 13:14],
                       ha[0:64, 0:bw], hb[0:64, 0:bw], 64)

            mark("filt" + ("P" if nseq == 4 else "S"))
            for i in range(2):
                k.memset("dve", raw[i], 0.0)
            conv_state = {}

            def evacA(cc, tb, ps):
                if cc < 12:
                    rw = raw[cc % 2]
                    nsb = 512 // seglen
                    k.copy("act", rw[:, tb * nsb:(tb + 1) * nsb, 1:1 + seglen],
                           ps[:].rearrange("p (s t) -> p s t", s=nsb))
                    if tb == 1:
                        if nseq == 1:
                            a_ = acol[:, 0:1]
                            na_ = acol[:, 1:2]
                            k.ts("dve", rw[:, 0, 0:1], rw[:, 1, 512:513], a_, None, ALU.mult)
                            k.ts("dve", rw[:, 0, 513:514], rw[:, 1, 1:2], na_, None, ALU.mult)
                            k.ts("dve", rw[:, 1, 0:1], rw[:, 0, 512:513], na_, None, ALU.mult)
                            k.ts("dve", rw[:, 1, 513:514], rw[:, 0, 1:2], a_, None, ALU.mult)
                        w0 = vA[:, 88 + cc:89 + cc]
                        w1_ = vA[:, 100 + cc:101 + cc]
                        w2_ = vA[:, 112 + cc:113 + cc]
                        k.act(U, rw[:, :, 1:1 + seglen], AF.Identity, bias=vB[:, cc:cc + 1], scale=w1_)
                        k.stt("dve", U, rw[:, :, 0:seglen], w0, U, ALU.mult, ALU.add)
                        ubf = ubfs[cc % 2]
                        if cc < 8:
                            dst = ubf.rearrange("p (s t) -> p s t", s=nseg)
                        else:
                            dst = gg[:, cc - 8, :].rearrange("p (s t) -> p s t", s=nseg)
                        k.stt("dve", dst, rw[:, :, 2:2 + seglen], w2_, U, ALU.mult, ALU.add)
                        if cc < 8:
                            def tail(cc=cc, ubf=ubf):
                                bank = pbf(next_pt())
                                uv = ubf.rearrange("p (s t two) -> p s t two", s=nseq, two=2)
                                khh = kt_n // 2
                                for t in range(8):
                                    if nseq == 1:
                                        s_i, k_i = t // kt_n, t % kt_n
                                        par_, kk_i = k_i // khh, k_i % khh
                                        k.transpose(bank[:, t * 128:(t + 1) * 128],
                                                    uv[:, s_i, kk_i * 128:(kk_i + 1) * 128, par_], ident_b[:])
                                    else:
                                        k.transpose(bank[:, t * 128:(t + 1) * 128], ubf[:, t * 128:(t + 1) * 128],
                                                    ident_b[:])
                                dstT = zT if cc < 4 else g1T
                                c4 = cc % 4
                                hc, off = c4 // 2, (c4 % 2) * 128
                                src = bank[:, 0:1024].rearrange("p (s kk c) -> p kk s c", s=nseq, kk=kt_n)
                                dd = dstT[:, :, hc, :].rearrange("p kk (s c) -> p kk s c", s=nseq)[:, :, :,
                                                                                                   off:off + 128]
                                k.copy("dve", dd, src)
                            defer(tail, delay=2)
                else:
                    sg = sgt[tb]
                    k.act(sg, ps[:], AF.Silu)
                    k.tt("dve", gg[:, cc - 12, tb * 512:(tb + 1) * 512], gg[:, cc - 12, tb * 512:(tb + 1) * 512],
                         sg, ALU.mult)

            set_pp([0, 1, 4, 5, 6, 7])
            proj_fm(d_evin, 0, 8, 1024, evacA)
            set_pp([0, 1])
            dbg_dump("zT" + ("P" if nseq == 4 else "S"), region[:, OFF_ZT // 2:OFF_ZT // 2 + 4096], [128, 4096], BF16)
            dbg_dump("gg" + ("P" if nseq == 4 else "S"), region[:, OFF_GG // 2:OFF_GG // 2 + 4096], [128, 4096], BF16)

            mark("projA" + ("P" if nseq == 4 else "S"))
            sgn = 2 if nseq == 4 else 1
            ngrp = nseq // sgn
            ncol = sgn * 256
            ft1s = [RV(OFF_T + 4096, [256], F32), RV(OFF_T + 6144, [256], F32)]
            ft2s = [RV(OFF_T + 5120, [256], F32), RV(OFF_T + 7168, [256], F32)]

            def emit_taps(o, hc, kt2):
                pss_ = []
                for u in range(2):
                    kt = kt2 + u
                    ps = next_pp()
                    k.mm(ps[:, 0:256], hid2b[0:64, kt * 128:(kt + 1) * 128],
                         w3b[0:64, (o * 2) * 512 + hc * 256:(o * 2) * 512 + hc * 256 + 256])
                    k.mm(ps[:, 256:512], hid2b[0:64, kt * 128:(kt + 1) * 128],
                         w3b[0:64, (o * 2 + 1) * 512 + hc * 256:(o * 2 + 1) * 512 + hc * 256 + 256])
                    k.dma("sp", winf[u], d_win[0, kt * 128:(kt + 1) * 128, hc * 256:(hc + 1) * 256])
                    k.dma("sp", winb[u], d_win[1, kt * 128:(kt + 1) * 128, hc * 256:(hc + 1) * 256])
                    pss_.append(ps)
                for u in range(2):
                    k.tt("dve", ft1s[u], pss_[u][:, 0:256], winf[u], ALU.mult)
                for u in range(2):
                    k.tt("dve", ft2s[u], pss_[u][:, 256:512], winb[u], ALU.mult)
                for u in range(2):
                    k.tt("dve", fa[:, kt2 + u, :], ft1s[u], ft2s[u], ALU.add)
                for u in range(2):
                    k.tt("dve", fb[:, kt2 + u, :], ft1s[u], ft2s[u], ALU.subtract)

            eo = (nseq == 1)
            if eo:
                kh = kt_n // 2
                nj = kt_n // 2
                tb_ = [RV(OFF_T + i * 512, [256], BF16) for i in range(14)]
                Ao_sb, Bo_sb, Xc, Xs, Xcm, Xsm, p1, p2, p3, p4, p5, p6, p7, p8 = tb_
                xrot = {"i": 0}
                combos = [(o, hc) for o in range(2) for hc in range(2)]

                def Ytile(ci, r):
                    if ci % 2 == 0:
                        return Y[:, r, 0:256]
                    return wbuf[r // 8][:, r % 8, :]

                def stage_A(ci, j):
                    o, hc = combos[ci]
                    pc = ff_piece(j)
                    pa = next_pp()
                    for kk in range(kh):
                        k.mm(pa[:, 0:256], pc[:, kk, 0:128], fa[:, kk, :], start=(kk == 0), stop=(kk == kh - 1))
                    for kk in range(kh):
                        k.mm(pa[:, 256:512], pc[:, kh + kk, 0:128], fa[:, kh + kk, :],
                             start=(kk == 0), stop=(kk == kh - 1))
                    pbk = next_pp()
                    for kk in range(kh):
                        k.mm(pbk[:, 0:256], pc[:, kk, 128:256], fb[:, kk, :], start=(kk == 0), stop=(kk == kh - 1))
                    for kk in range(kh):
                        k.mm(pbk[:, 256:512], pc[:, kh + kk, 128:256], fb[:, kh + kk, :],
                             start=(kk == 0), stop=(kk == kh - 1))
                    k.copy("act", Ao_sb, pa[:, 256:512])
                    k.copy("act", Bo_sb, pbk[:, 256:512])
                    k.tt("dve", p1, pa[:, 0:256], skipbc[:, o, hc * 256:(hc + 1) * 256], ALU.add)
                    k.tt("dve", spec[:, 4 * j + 1, :], pbk[:, 0:256], Bo_sb, ALU.add)
                    k.tt("dve", spec[:, 4 * j + 3, :], Bo_sb, pbk[:, 0:256], ALU.subtract)
                    k.tt("dve", spec[:, 4 * j + 0, :], p1, Ao_sb, ALU.add)
                    k.tt("dve", spec[:, 4 * j + 2, :], p1, Ao_sb, ALU.subtract)
                    Kc, Ks, Kcm, Ksm = (spec[:, 4 * j + q_, :] for q_ in range(4))
                    xb_ = xrot["i"] % 2
                    xrot["i"] += 1
                    bA, bB = pb[4 + xb_ * 2], pb[5 + xb_ * 2]
                    zc = slice(0, 256)
                    for kk in range(kh):
                        k.mm(bA[:, 0:256], pc[:, kk, 0:128], zT[:, kk, hc, zc], start=(kk == 0), stop=(kk == kh - 1))
                    for kk in range(kh):
                        k.mm(bA[:, 256:512], pc[:, kh + kk, 0:128], zT[:, kh + kk, hc, zc],
                             start=(kk == 0), stop=(kk == kh - 1))
                    for kk in range(kh):
                        k.mm(bB[:, 0:256], pc[:, kk, 128:256], zT[:, kk, hc, zc], start=(kk == 0), stop=(kk == kh - 1))
                    for kk in range(kh):
                        k.mm(bB[:, 256:512], pc[:, kh + kk, 128:256], zT[:, kh + kk, hc, zc],
                             start=(kk == 0), stop=(kk == kh - 1))
                    k.copy("act", Ao_sb, bA[:, 256:512])
                    k.copy("act", Bo_sb, bB[:, 256:512])
                    k.tt("dve", Xc, bA[:, 0:256], Ao_sb, ALU.add)
                    k.tt("dve", Xs, bB[:, 0:256], Bo_sb, ALU.add)
                    k.tt("dve", Xcm, bA[:, 0:256], Ao_sb, ALU.subtract)
                    k.tt("dve", Xsm, Bo_sb, bB[:, 0:256], ALU.subtract)
                    k.tt("dve", p1, Xc, Kc, ALU.mult)
                    k.tt("dve", p2, Xs, Ks, ALU.mult)
                    k.tt("dve", p3, Xc, Ks, ALU.mult)
                    k.tt("dve", p4, Xs, Kc, ALU.mult)
                    k.tt("dve", p5, Xcm, Kcm, ALU.mult)
                    k.tt("dve", p6, Xsm, Ksm, ALU.mult)
                    k.tt("dve", p7, Xcm, Ksm, ALU.mult)
                    k.tt("dve", p8, Xsm, Kcm, ALU.mult)
                    k.tt("dve", Ytile(ci, 4 * j + 0), p1, p2, ALU.subtract)
                    k.tt("dve", Ytile(ci, 4 * j + 1), p3, p4, ALU.add)
                    k.tt("dve", Ytile(ci, 4 * j + 2), p5, p6, ALU.subtract)
                    k.tt("dve", Ytile(ci, 4 * j + 3), p7, p8, ALU.add)

                def stage_B(ci, i):
                    o, hc = combos[ci]
                    pc = finv_piece(i)
                    par, kk_ = i // kh, i % kh
                    ps = next_pp()
                    nmm = 4 * nj
                    for jf in range(nj):
                        for part in range(4):
                            idx = jf * 4 + part
                            k.mm(ps[:, 0:256], pc[:, idx, :], Ytile(ci, 4 * jf + part),
                                 start=(idx == 0), stop=(idx == nmm - 1))
                    cs = slice(0, 256)
                    if o == 0:
                        k.tt("dve", zT[:, i, hc, cs], ps[:, 0:256], g1T[:, i, hc, cs], ALU.mult)
                    else:
                        yt = ytm[i % 2]
                        k.copy("act", yt[:, 0:256], ps[:, 0:256])
                        bank = pbf(next_pt())
                        for c2 in range(2):
                            k.transpose(bank[:, c2 * 128:(c2 + 1) * 128], yt[:, c2 * 128:(c2 + 1) * 128], ident_b[:])
                        for c2 in range(2):
                            ch = hc * 2 + c2
                            yv = ybuf[:, ch, :].rearrange("p (s t two) -> p s t two", s=nseq, two=2)
                            gv = gg[:, ch, :].rearrange("p (s t two) -> p s t two", s=nseq, two=2)
                            k.tt("dve", yv[:, 0, kk_ * 128:(kk_ + 1) * 128, par], bank[:, c2 * 128:(c2 + 1) * 128],
                                 gv[:, 0, kk_ * 128:(kk_ + 1) * 128, par], ALU.mult)

                ncomb = len(combos)
                set_pp([0, 1, 2])
                pt_fixed["on"] = True
                for kt2 in range(0, kt_n, 2):
                    emit_taps(combos[0][0], combos[0][1], kt2)
                for j in range(nj):
                    stage_A(0, j)
                for ci in range(ncomb):
                    if ci + 1 < ncomb:
                        no_, nhc_ = combos[ci + 1]
                        taps_l = list(range(0, kt_n, 2))
                        assert len(taps_l) == kt_n // 2 and nj == kt_n // 2
                        for t_ in range(kt_n // 2):
                            emit_taps(no_, nhc_, taps_l[t_])
                            stage_B(ci, t_)
                        for j in range(nj):
                            stage_A(ci + 1, j)
                            stage_B(ci, kt_n // 2 + j)
                    else:
                        for i in range(kt_n):
                            stage_B(ci, i)
                set_pp([0, 1])
                pt_fixed["on"] = False

            else:
                combos = [(o, hc) for o in range(2) for hc in range(2)]
                for kt2 in range(0, kt_n, 2):
                    emit_taps(combos[0][0], combos[0][1], kt2)
                for ci, (o, hc) in enumerate(combos):
                    if True:
                        nxt = combos[ci + 1] if ci + 1 < len(combos) else None
                        pairs_left = list(range(0, kt_n, 2)) if nxt is not None else []
                        npairs = kt_n // 2
                        for j in range(kt_n):
                            pc = ff_piece(j)
                            psc = next_pp()
                            for kk in range(kt_n):
                                k.mm(psc[:, 0:256], pc[:, kk, 0:128], fa[:, kk, :], start=(kk == 0), stop=(kk == kt_n - 1))
                            pss = next_pp()
                            for kk in range(kt_n):
                                k.mm(pss[:, 0:256], pc[:, kk, 128:256], fb[:, kk, :], start=(kk == 0), stop=(kk == kt_n - 1))
                            k.tt("dve", spec[:, 2 * j, :], psc[:, 0:256], skipbc[:, o, hc * 256:(hc + 1) * 256], ALU.add)
                            k.copy("act", spec[:, 2 * j + 1, :], pss[:, 0:256])
                            for g in range(ngrp):
                                xb_ = (g if ngrp > 1 else j) % 2
                                xc = pb[4 + xb_ * 2]
                                xs_ = pb[5 + xb_ * 2]
                                for kk in range(kt_n):
                                    k.mm(xc[:, 0:ncol], pc[:, kk, 0:128], zT[:, kk, hc, g * ncol:(g + 1) * ncol],
                                         start=(kk == 0), stop=(kk == kt_n - 1))
                                for kk in range(kt_n):
                                    k.mm(xs_[:, 0:ncol], pc[:, kk, 128:256], zT[:, kk, hc, g * ncol:(g + 1) * ncol],
                                         start=(kk == 0), stop=(kk == kt_n - 1))
                                for s_ in range(sgn):
                                    sq = g * sgn + s_
                                    a_, b_ = xr[s_ % 2], xi[s_ % 2]
                                    k.copy("act", a_, xc[:, s_ * 256:(s_ + 1) * 256])
                                    k.copy("act", b_, xs_[:, s_ * 256:(s_ + 1) * 256])
                                    Kr, Ki = spec[:, 2 * j, :], spec[:, 2 * j + 1, :]
                                    k.tt("dve", t1, a_, Kr, ALU.mult)
                                    k.tt("dve", t2, b_, Ki, ALU.mult)
                                    k.tt("dve", t3, a_, Ki, ALU.mult)
                                    k.tt("dve", t4, b_, Kr, ALU.mult)
                                    k.tt("dve", Y[:, 2 * j, sq * 256:(sq + 1) * 256], t1, t2, ALU.subtract)
                                    k.tt("dve", Y[:, 2 * j + 1, sq * 256:(sq + 1) * 256], t3, t4, ALU.add)
                        for i in range(kt_n):
                            if pairs_left and ((i + 1) * npairs) // kt_n > npairs - len(pairs_left):
                                emit_taps(nxt[0], nxt[1], pairs_left.pop(0))
                            pc = finv_piece(i)
                            for g in range(ngrp):
                                ps = next_pp()
                                for r in range(rt_n):
                                    k.mm(ps[:, 0:ncol], pc[:, r, :], Y[:, r, g * ncol:(g + 1) * ncol],
                                         start=(r == 0), stop=(r == rt_n - 1))
                                if o == 0:
                                    k.tt("dve", zT[:, i, hc, g * ncol:(g + 1) * ncol], ps[:, 0:ncol],
                                         g1T[:, i, hc, g * ncol:(g + 1) * ncol], ALU.mult)
                                else:
                                    yt = ytm[(i * ngrp + g) % 2]
                                    k.copy("act", yt[:, 0:ncol], ps[:, 0:ncol])
                                    bank = pbf(next_pt())
                                    for s_ in range(sgn):
                                        for c2 in range(2):
                                            slot = s_ * 2 + c2
                                            k.transpose(bank[:, slot * 128:(slot + 1) * 128],
                                                        yt[:, s_ * 256 + c2 * 128:s_ * 256 + (c2 + 1) * 128], ident_b[:])
                                    for s_ in range(sgn):
                                        sq = g * sgn + s_
                                        tok0 = sq * L + i * 128
                                        for c2 in range(2):
                                            slot = s_ * 2 + c2
                                            ch = hc * 2 + c2
                                            k.tt("dve", ybuf[:, ch, tok0:tok0 + 128], bank[:, slot * 128:(slot + 1) * 128],
                                                 gg[:, ch, tok0:tok0 + 128], ALU.mult)
                        if o == 0 and hc == 0:
                            dbg_dump("z2" + ("P" if nseq == 4 else "S"), region[:, OFF_ZT // 2:OFF_ZT // 2 + 4096], [128, 4096], BF16)

        ring_rot = {"i": 0}

        def ffS_piece(j):
            r = ring[ring_rot["i"] % 3]
            ring_rot["i"] += 1
            v = r[:, 0:2048].rearrange("p (kk c) -> p kk c", kk=8)
            k.dma("sp", v, d_FfS[j])
            return v

        def finvS_piece(i):
            r = ring[ring_rot["i"] % 3]
            ring_rot["i"] += 1
            v = r[:, 0:2048].rearrange("p (r c) -> p r c", r=16)
            k.dma("sp", v, d_FinvS[i])
            return v

        def ffP_piece(j):
            return FfP[:, :, j * 256:(j + 1) * 256]

        def finvP_piece(i):
            return FinvP[:, :, i * 128:(i + 1) * 128]

        OFF_Q = 0
        OFF_SG0 = 8192
        OFF_K0 = 16384
        OFF_V0 = 28672
        OFF_SG1 = 16384
        OFF_K1 = 32768
        OFF_V1P = 36864
        OFF_V1S = 24576
        ropeC = RV(41216, [LS], F32)
        ropeS = RV(45312, [LS], F32)
        att_tmp = RV(49408, [2048], F32)

        def attention(nheads, ncomp, dk, qT, kT, Vaug, sgT, units, scale, ych0, finalize_diff, extra=None):
            Eb = att_tmp[:].bitcast(BF16)
            E = [Eb[:, i * 512:(i + 1) * 512] for i in range(8)]
            fin_sq = RV(57600, [512], BF16)
            pair = all(u_[1] <= 256 for u_ in units)

            steps = []
            for (q0, nq, ktl, tok0) in units:
                for h in range(nheads):
                    hh = (h % 2) if pair else 0
                    for ki, (ktile, kc0) in enumerate(ktl):
                        for m in range(ncomp):
                            steps.append((q0, nq, tok0, h, m, ki, ktile, kc0, ki == 0, ki == len(ktl) - 1, hh))
            n = len(steps)

            def banks(h, m):
                if ncomp == 2:
                    g = m
                else:
                    g = ((h // 2) % 2) if pair else (h % 2)
                return pb[4 + 2 * g], pb[5 + 2 * g]

            def issue_S(i):
                q0, nq, tok0, h, m, ki, ktile, kc0, first, last, hh = steps[i]
                ps = pb[i % 3]
                if ncomp == 2:
                    k.mm(ps[:, 0:nq], kT[:, h, kc0:kc0 + 128], qT[m][:, h, q0:q0 + nq])
                else:
                    k.mm(ps[:, 0:nq], kT[0:128, h, kc0:kc0 + 128], qT[0:128, h, q0:q0 + nq])
                k.act(E[i % 8][:, 0:nq], ps[:, 0:nq], AF.Exp, scale=scale)

            def finalize_stages(st_):
                q0, nq, tok0, h, m, ki, ktile, kc0, first, last, hh = st_
                W = 2 * nq if pair else nq
                A, B = ropeC[:, 0:W], ropeC[:, 512:512 + W]
                C, Dd = ropeS[:, 0:W], ropeS[:, 512:512 + W]
                sq_ = fin_sq[:, 0:W]
                if pair:
                    h0 = h - 1
                    dst = ybuf[:, ych0 + h0:ych0 + h0 + 2, tok0:tok0 + nq]
                    sg_ = sgT[:, h0:h0 + 2, tok0:tok0 + nq]
                    C_ = C.rearrange("p (a c) -> p a c", a=2)
                else:
                    dst = ybuf[:, ych0 + h, tok0:tok0 + nq]
                    sg_ = sgT[:, h, tok0:tok0 + nq]
                    C_ = C
                if ncomp == 2:
                    O0, S0 = banks(h, 0)
                    O1, S1 = banks(h, 1)

                    def s0a():
                        k.copy("dve", C, O0[:, 0:W])
                        k.act(A, S0[:, 0:W], AF.Ln)

                    def s0():
                        k.copy("dve", Dd, O1[:, 0:W])
                        k.act(B, S1[:, 0:W], AF.Ln)

                    def s1():
                        k.act(A, A, AF.Exp, scale=-1.0)
                        k.act(B, B, AF.Exp, scale=-1.0)

                    def s2():
                        k.tt("dve", C, C, A, ALU.mult)
                        k.tt("dve", Dd, Dd, B, ALU.mult)
                        k.stt("dve", C, Dd, neglam[:], C, ALU.mult, ALU.add)
                        k.tt("dve", sq_, C, C, ALU.mult)
                        k.mm(pb[3][:, 0:W], ones_b[:], sq_)

                    def s3():
                        k.act(A, pb[3][:, 0:W], AF.Ln, bias=epsc[:], scale=1.0 / 128)
                        k.act(A, A, AF.Exp, scale=-0.5)

                    def s4():
                        k.tt("dve", C, C, A, ALU.mult)
                        k.stt("dve", dst, C_, subg[:], sg_, ALU.mult, ALU.mult)
                    return [s0a, s0, s1, s2, s3, s4]
                O0, S0 = banks(h, 0)

                def g0():
                    k.copy("dve", C, O0[:, 0:W])
                    k.act(A, S0[:, 0:W], AF.Ln)

                def g1():
                    k.act(A, A, AF.Exp, scale=-1.0)

                def g2():
                    k.tt("dve", C, C, A, ALU.mult)
                    k.tt("dve", dst, C_, sg_, ALU.mult)
                return [g0, g1, g2]

            pending = []
            cur_fin = {}

            def issue_PV(i):
                q0, nq, tok0, h, m, ki, ktile, kc0, first, last, hh = steps[i]
                Ob, Sb = banks(h, m)
                e = E[i % 8]
                c0 = hh * 256
                k.mm(Ob[:, c0:c0 + nq], Vaug(ktile, h), e[:, 0:nq], start=first, stop=last)
                k.mm(Sb[:, c0:c0 + nq], ones_b[:], e[:, 0:nq], start=first, stop=last)
                for p_ in list(pending):
                    p_.pop(0)()
                    if not p_:
                        pending.remove(p_)
                fin_now = last and (hh == 1 or not pair)
                if ncomp == 2:
                    if fin_now and m == 0:
                        cur_fin["s"] = finalize_stages(steps[i])
                        cur_fin["s"].pop(0)()
                    elif fin_now and m == 1:
                        stg_ = cur_fin["s"]
                        stg_.pop(0)()
                        pending.append(stg_)
                elif fin_now:
                    stg_ = finalize_stages(steps[i])
                    stg_.pop(0)()
                    pending.append(stg_)

            LA = 2
            for j in range(min(LA, n)):
                issue_S(j)
            for i in range(n):
                if i + LA < n:
                    issue_S(i + LA)
                issue_PV(i)
                if extra is not None:
                    extra(i)
            while pending:
                for p_ in list(pending):
                    p_.pop(0)()
                    if not p_:
                        pending.remove(p_)

        rope_cnt = {"i": 0}

        def rope_evac(ps, dst, col0, n, gcol=None, post=None):
            par = rope_cnt["i"] % 2
            rope_cnt["i"] += 1
            qb = att_tmp[:, par * 256:(par + 1) * 256].bitcast(BF16)
            rt1 = att_tmp[:, 512 + par * 512:1024 + par * 512]
            rt2 = att_tmp[:, 1536:2048]
            if gcol is None:
                k.copy("act", qb[:, 0:n], ps[:, 0:n])
            else:
                k.ts("dve", qb[:, 0:n], ps[:, 0:n], gcol, None, ALU.mult)

            def tail():
                pq = pb[next_pt()]
                k.mm(pq[:, 0:n], perm_b[:], qb[:, 0:n])
                k.tt("dve", rt1[:, 0:n], qb[:, 0:n], ropeC[:, col0:col0 + n], ALU.mult)
                k.tt("dve", rt2[:, 0:n], pq[:, 0:n], ropeS[:, col0:col0 + n], ALU.mult)
                if post is None:
                    k.tt("dve", dst, rt1[:, 0:n], rt2[:, 0:n], ALU.add)
                else:
                    k.tt("dve", rt1[:, 0:n], rt1[:, 0:n], rt2[:, 0:n], ALU.add)
                    k.tt("dve", dst, rt1[:, 0:n], post, ALU.mult)
            defer(tail)

        def layer0_attn(is_sample):
            Tk = 1536 if is_sample else 1024
            nkt = Tk // 128
            qT = RV(OFF_Q, [4, 1024], BF16)
            sgT = RV(OFF_SG0, [4, 1024], BF16)
            kT = RV(OFF_K0, [4, Tk], BF16)
            Va = RV(OFF_V0, [nkt, 4, 130], BF16)
            k.memset("dve", Va[:, :, :, 128:130], 1.0)
            kst = [ring[0][:, i * 512:(i + 1) * 512].bitcast(F32) for i in range(4)]
            kbf = [ring[1][:, i * 256:(i + 1) * 256] for i in range(4)]
            koff = 512 if is_sample else 0
            if is_sample:
                k.dma("sp", ropeC[:], d_ropeD[0])
                k.dma("sp", ropeS[:], d_ropeD[1])
                stg = ring[2][:, 0:2048].rearrange("p (t c) -> p t c", t=4)
                k.dma("pool", stg, d_cdk.rearrange("(t p) c -> p t c", p=128))
                for ct in range(4):
                    bank = pbf(next_pt())
                    for h in range(4):
                        k.transpose(bank[:, h * 128:(h + 1) * 128], stg[:, ct, h * 128:(h + 1) * 128], ident_b[:])
                    k.copy("dve", kT[:, :, ct * 128:(ct + 1) * 128],
                           bank[:, 0:512].rearrange("p (h c) -> p h c", h=4))
                for ct in range(4):
                    k.dma("pool", Va[:, ct, :, 0:128],
                          d_cdv[ct * 128:(ct + 1) * 128, :].rearrange("p (h d) -> p h d", h=4))

            def evac_q(cc, tb, ps):
                if is_sample:
                    rope_evac(ps, qT[:, cc, tb * 512:(tb + 1) * 512], tb * 512, 512)
                else:
                    k.copy("act", qT[:, cc, tb * 512:(tb + 1) * 512], ps[:])

            def evac_k_fm(cc, tb, ps):
                rope_evac(ps, kT[:, cc, koff + tb * 512:koff + (tb + 1) * 512], tb * 512, 512)

            cnt = {"i": 0}

            def evac_k_tm(b, t, ps):
                i = cnt["i"] % 4
                cnt["i"] += 1
                if EXPERIMENT == "B":
                    return
                k.copy("act", kst[i], ps[:, 0:256])
                if EXPERIMENT != "A":
                    k.dma("sp", o_ndk[t * 128:(t + 1) * 128, b * 256:(b + 1) * 256], kst[i])
                k.copy("dve", kbf[i], ps[:, 0:256])

                def tail(i=i, b=b, t=t):
                    bank = pbf(next_pt())
                    for h2 in range(2):
                        k.transpose(bank[:, h2 * 128:(h2 + 1) * 128], kbf[i][:, h2 * 128:(h2 + 1) * 128], ident_b[:])
                    k.copy("dve", kT[:, 2 * b:2 * b + 2, t * 128:(t + 1) * 128],
                           bank[:, 0:256].rearrange("p (h c) -> p h c", h=2))
                defer(tail, delay=2)

            def evac_v(b, t, ps):
                if not is_sample:
                    i = cnt["i"] % 4
                    cnt["i"] += 1
                    k.copy("act", kst[i], ps[:, 0:256])
                    k.dma("sp", o_ndv[t * 128:(t + 1) * 128, b * 256:(b + 1) * 256], kst[i])
                vt = t + (4 if is_sample else 0)
                k.copy("dve", Va[:, vt, 2 * b:2 * b + 2, 0:128], ps[:, 0:256].rearrange("p (h c) -> p h c", h=2))

            def evac_g(cc, tb, ps):
                k.act(sgT[:, cc, tb * 512:(tb + 1) * 512], ps[:], AF.Silu)

            sfx = "S" if is_sample else "P"
            set_pp([0, 1, 4, 5, 6, 7])
            mark("pre" + sfx)
            proj_fm(d_evin, 2048, 2, 1024, evac_q)
            mark("pq" + sfx)
            if is_sample:
                proj_fm(d_evin, 2560, 2, 1024, evac_k_fm)
            else:
                proj_tm(d_evin, 2560, 2, 8, evac_k_tm)
            mark("pk" + sfx)
            proj_tm(d_evin, 3072, 2, 8, evac_v)
            mark("pv" + sfx)
            proj_fm(d_evin, 3584, 2, 1024, evac_g)
            mark("pg" + sfx)
            set_pp([0, 1])
            if is_sample:
                units = [(qb * 512, 512, [(kt, kt * 128) for kt in range(12)], qb * 512) for qb in range(2)]
            else:
                units = [(s_ * 256, 256, [(2 * s_ + i, (2 * s_ + i) * 128) for i in range(2)], s_ * 256)
                         for s_ in range(4)]
            dbg_dump("qT" + ("S" if is_sample else "P"), region[:, OFF_Q // 2:OFF_Q // 2 + 4096], [128, 4096], BF16)
            dbg_dump("kT" + ("S" if is_sample else "P"), region[:, OFF_K0 // 2:OFF_K0 // 2 + 4 * Tk], [128, 4 * Tk], BF16)
            qpad = [hT[:, 0:4, :], hT[:, 4:8, :]]
            k.memset("dve", qpad[0][64:128, :, :], 0.0)
            k.memset("dve", qpad[1][0:64, :, :], 0.0)
            k.copy("dve", qpad[0][0:64, :, :], qT[0:64, :, :])
            k.copy("dve", qpad[1][64:128, :, :], qT[64:128, :, :])
            extra = None
            if (not is_sample) and PREFETCH_MOD1 and "l1" in stages:
                def extra(i):
                    if i % 5 == 0 and i // 5 < 12:
                        mod_block(1, i // 5, wbuf[(i // 5) % 2], pb[2][:, 256:512])
            attention(4, 2, 64, qpad, kT, lambda kt, h: Va[:, kt, h, 0:128], sgT, units, 0.125, 4, True, extra=extra)

        def layer0_group(is_sample):
            xg = xS if is_sample else xP
            cond = 0 if is_sample else 1
            adaln(xg, 8, cond, pre=("S0" if is_sample else "P0"))
            mark("adaln" + ("S" if is_sample else "P"))
            dbg_dump("hT" + ("S" if is_sample else "P"), hT[:], [128, 8, 1024], BF16)
            if is_sample:
                hyena_group(cond, LS, 1, d_zposS, d_winS, ffS_piece, finvS_piece)
            else:
                hyena_group(cond, LP, 4, d_zposP, d_winP, ffP_piece, finvP_piece)
            mark("hy" + ("S" if is_sample else "P"))
            layer0_attn(is_sample)
            mark("att" + ("S" if is_sample else "P"))
            dbg_dump("y" + ("S" if is_sample else "P"), ybuf[:], [128, 8, 1024], BF16)
            compute_gate_bc(cond)
            wout_residual(d_evout, xg, 8)
            mark("wout" + ("S" if is_sample else "P"))

        def layer1_group(is_sample):
            xg = xS if is_sample else xP
            cond = 0 if is_sample else 1
            Tq = 512 if is_sample else 1024
            Tk = 1536 if is_sample else 1024
            nkt = Tk // 128
            adaln(xg, 8, cond)
            qT = RV(OFF_Q, [8, Tq], BF16)
            sgT = RV(OFF_SG1, [8, Tq], BF16)
            kT = RV(OFF_K1, [2, Tk], BF16)
            Va = RV(OFF_V1S if is_sample else OFF_V1P, [nkt, 2, 130], BF16)
            k.memset("dve", Va[:, :, :, 128:130], 1.0)
            kst = [ring[0][:, i * 512:(i + 1) * 512].bitcast(F32) for i in range(4)]
            kbf = [ring[1][:, i * 256:(i + 1) * 256] for i in range(4)]
            gkbc = spec[:, 0, :].bitcast(F32)
            k.dma("sp", gkbc, d_gkg.partition_broadcast(128))
            koff = 512 if is_sample else 0
            if is_sample:
                k.dma("sp", ropeC[:], d_ropeG[0])
                k.dma("sp", ropeS[:], d_ropeG[1])
                stg = ring[2][:, 0:1024].rearrange("p (t c) -> p t c", t=4)
                k.dma("pool", stg, d_cgk.rearrange("(t p) c -> p t c", p=128))
                for ct in range(4):
                    bank = pbf(next_pt())
                    for h in range(2):
                        k.transpose(bank[:, h * 128:(h + 1) * 128], stg[:, ct, h * 128:(h + 1) * 128], ident_b[:])
                    k.copy("dve", kT[:, :, ct * 128:(ct + 1) * 128],
                           bank[:, 0:256].rearrange("p (h c) -> p h c", h=2))
                for ct in range(4):
                    k.dma("pool", Va[:, ct, :, 0:128],
                          d_cgv[ct * 128:(ct + 1) * 128, :].rearrange("p (h d) -> p h d", h=2))

            if is_sample:
                sqbs = [RV(8192 + i * 1024, [512], BF16) for i in range(2)]
                rstds = [RV(10240 + i * 2048, [512], F32) for i in range(2)]
            else:
                sqbs = [att_tmp[:, i * 256:(i + 1) * 256].bitcast(BF16) for i in range(2)]
                rstds = [att_tmp[:, 512 + i * 512:1024 + i * 512] for i in range(2)]
            fm_cnt = {"i": 0}

            def fm_norm(ps, n, gcol, dst, rope_col0):
                par = fm_cnt["i"] % 2
                fm_cnt["i"] += 1
                sqb, rstd = sqbs[par], rstds[par]
                k.act(sqb[:, 0:n], ps[:, 0:n], AF.Square)

                def tail():
                    pn = pb[next_pt()]
                    k.mm(pn[:, 0:n], ones_b[:], sqb[:, 0:n])
                    k.act(rstd[:, 0:n], pn[:, 0:n], AF.Ln, bias=epsc[:], scale=1.0 / 128)
                    k.act(rstd[:, 0:n], rstd[:, 0:n], AF.Exp, scale=-0.5)
                    if rope_col0 is None:
                        k.stt("dve", dst, ps[:, 0:n], gcol, rstd[:, 0:n], ALU.mult, ALU.mult)
                if rope_col0 is None:
                    defer(tail)
                else:
                    defer(tail)
                    rope_evac(ps, dst, rope_col0, n, gcol=gcol, post=rstd[:, 0:n])

            qg = vB[:, 15:16]
            kg = vB[:, 16:17]

            def evac_q(cc, tb, ps):
                fm_norm(ps, 512, qg, qT[:, cc, tb * 512:(tb + 1) * 512], (tb * 512) if is_sample else None)

            def evac_k_fm(cc, tb, ps):
                fm_norm(ps, 512, kg, kT[:, cc, koff + tb * 512:koff + (tb + 1) * 512], tb * 512)

            cnt = {"i": 0}

            def evac_k_tm(b, t, ps):
                i = cnt["i"] % 4
                cnt["i"] += 1
                rr = small[:, 16 + 4 * i:20 + 4 * i]
                k.memset("dve", rr[:, 0:2], 0.0)
                for h2 in range(2):
                    k.act(kbf[i][:, h2 * 128:(h2 + 1) * 128], ps[:, h2 * 128:(h2 + 1) * 128], AF.Square,
                          accum_out=rr[:, h2:h2 + 1])

                def tail2(i=i, t=t):
                    bank = pbf(next_pt())
                    for h2 in range(2):
                        k.transpose(bank[:, h2 * 128:(h2 + 1) * 128], kbf[i][:, h2 * 128:(h2 + 1) * 128], ident_b[:])
                    k.copy("dve", kT[:, :, t * 128:(t + 1) * 128], bank[:, 0:256].rearrange("p (h c) -> p h c", h=2))

                def tail1(i=i, t=t, ps=ps, rr=rr):
                    k.act(rr[:, 2:4], rr[:, 0:2], AF.Ln, bias=epsc[:], scale=1.0 / 128)
                    k.act(rr[:, 2:4], rr[:, 2:4], AF.Exp, scale=-0.5)
                    for h2 in range(2):
                        k.stt("dve", kst[i][:, h2 * 128:(h2 + 1) * 128], ps[:, h2 * 128:(h2 + 1) * 128],
                              rr[:, 2 + h2:3 + h2], gkbc, ALU.mult, ALU.mult)
                    k.dma("sp", o_ngk[t * 128:(t + 1) * 128, :], kst[i])
                    k.copy("dve", kbf[i], kst[i])
                    defer(tail2, delay=2)
                defer(tail1, delay=1)

            def evac_v(b, t, ps):
                if not is_sample:
                    i = cnt["i"] % 4
                    cnt["i"] += 1
                    k.copy("act", kst[i], ps[:, 0:256])
                    k.dma("sp", o_ngv[t * 128:(t + 1) * 128, :], kst[i])
                vt = t + (4 if is_sample else 0)
                k.copy("dve", Va[:, vt, :, 0:128], ps[:, 0:256].rearrange("p (h c) -> p h c", h=2))

            def evac_g(cc, tb, ps):
                k.act(sgT[:, cc, tb * 512:(tb + 1) * 512], ps[:], AF.Silu)

            sfx1 = "S" if is_sample else "P"
            mark("l1pre" + sfx1)
            set_pp([0, 1, 4, 5, 6, 7])
            proj_fm(d_odin, 0, 4, Tq, evac_q)
            mark("l1q" + sfx1)
            if is_sample:
                proj_fm(d_odin, 1024, 1, 1024, evac_k_fm)
            else:
                proj_tm(d_odin, 1024, 1, 8, evac_k_tm)
            proj_tm(d_odin, 1280, 1, 8, evac_v)
            mark("l1kv" + sfx1)
            proj_fm(d_odin, 1536, 4, Tq, evac_g)
            mark("l1g" + sfx1)
            set_pp([0, 1])
            if is_sample:
                units = [(0, 512, [(kt, kt * 128) for kt in range(12)], 0)]
            else:
                units = [(s_ * 256, 256, [(2 * s_ + i, (2 * s_ + i) * 128) for i in range(2)], s_ * 256)
                         for s_ in range(4)]
            dbg_dump("q1" + ("S" if is_sample else "P"), region[:, OFF_Q // 2:OFF_Q // 2 + 8 * Tq], [128, 8 * Tq], BF16)
            dbg_dump("k1" + ("S" if is_sample else "P"), region[:, OFF_K1 // 2:OFF_K1 // 2 + 2 * Tk], [128, 2 * Tk], BF16)

            class KV:
                pass
            attention_gqa(qT, kT, Va, sgT, units)
            mark("l1att" + sfx1)
            compute_gate_bc(cond)
            wout_residual(d_odout, xg, Tq // 128)
            mark("l1wout" + sfx1)

        def attention_gqa(qT, kT, Va, sgT, units):
            class KTv:
                def __getitem__(self, idx):
                    p, h, c = idx
                    return kT[p, h // 4, c]
            attention(8, 1, 128, qT, KTv(), lambda kt, h: Va[:, kt, h // 4, 0:128], sgT, units,
                      128.0 ** -0.5, 0, False)

        def final_norm(xg, ntiles, o_y):
            fg = gate_bc
            k.dma("sp", fg[:], d_fg.partition_broadcast(128))
            junk = [RV(i * 2048, [1024], BF16) for i in range(2)]
            k.memset("dve", ss[:], 0.0)
            for t in range(ntiles):
                k.act(junk[t % 2], xg[:, t, :], AF.Square, accum_out=ss[:, t:t + 1])
            k.act(rs[:, 0:ntiles], ss[:, 0:ntiles], AF.Ln, bias=epsc[:], scale=1.0 / D)
            k.act(rs[:, 0:ntiles], rs[:, 0:ntiles], AF.Exp, scale=-0.5)
            for t in range(ntiles):
                k.stt("dve", xg[:, t, :], xg[:, t, :], rs[:, t:t + 1], fg[:], ALU.mult, ALU.mult)
                k.dma("sp", o_y[t * 128:(t + 1) * 128, :], xg[:, t, :])

        try:
            mark("setup")
            if "l0" in stages:
                adaln_stats(xP, 8, "P0", 18432)
                for t in range(8):
                    k.dma("sp", xS[:, t, :], d_xs[t * 128:(t + 1) * 128, :])
                compute_mod(0)
                adaln_stats(xS, 8, "S0", 20480)
                mark("mod0")
                layer0_group(False)
                layer0_group(True)
            dbg_dump("x1P", xP[:], [128, 8, 1024])
            dbg_dump("x1S", xS[:], [128, 8, 1024])
            if "l1" in stages:
                if PREFETCH_MOD1 and "l0" in stages:
                    mod_finish(1)
                else:
                    compute_mod(1)
                cur["modT"], cur["gsT"] = modTs[1], gsTs[1]
                mark("mod1")
                layer1_group(False)
                mark("l1P")
                layer1_group(True)
            if "final" in stages:
                final_norm(xP, 8, o_yp)
                final_norm(xS, 4, o_ys)
        except _Stop:
            pass

        S.emit(st)
    build_program.last_sched = S
    return nc, dbg_outs


_bf = ml_dtypes.bfloat16


def _rope_tab(pos, head_dim):
    half = head_dim // 2
    inv = (np.float32(10000.0) ** (-np.arange(0, half, 2, dtype=np.float32) / np.float32(half))).astype(np.float32)
    row = (pos // 64).astype(np.float32)
    col = (pos % 64).astype(np.float32)
    ang = np.concatenate([row[:, None] * inv, col[:, None] * inv], axis=-1).astype(np.float32)
    cos, sin = np.cos(ang), np.sin(ang)
    d = np.arange(128) % head_dim
    i = d // 2
    C = cos[:, i].T
    Sg = sin[:, i].T * np.where(d % 2 == 0, -1.0, 1.0)[:, None]
    return np.stack([C, Sg]).astype(np.float32)


def _dft(L, pos):
    f = np.arange(L)
    theta = 2 * np.pi * (f + 0.5) / (2 * L)
    r = np.arange(2 * L)
    rt = r // 128
    j = rt // 2
    part = rt % 2
    fr = j * 128 + r % 128
    th = theta[fr]
    A = np.outer(pos.astype(np.float64), th)
    Ff = np.where(part[None, :] == 0, np.cos(A), -np.sin(A))
    Finv = np.where(part[:, None] == 0, np.cos(A.T), -np.sin(A.T)) / L
    return Ff, Finv


def _dft_half(L, tokpos):
    nj = L // 256
    kt_n = L // 128
    f = np.arange(L // 2)
    theta = 2 * np.pi * (f + 0.5) / (2 * L)
    A = np.outer(tokpos.astype(np.float64), theta)
    C = np.cos(A)
    Sn = -np.sin(A)
    Ff = np.concatenate([C.reshape(kt_n, 128, nj, 128), Sn.reshape(kt_n, 128, nj, 128)], axis=-1)
    FfH = np.ascontiguousarray(Ff.transpose(2, 1, 0, 3))
    Ci = (C / L).T
    Si = (Sn / L).T
    sgn = np.where(np.arange(L) < L // 2, 1.0, -1.0)[None, :]
    parts = [Ci, Si, Ci * sgn, -Si * sgn]
    Fi = np.stack([p_.reshape(nj, 128, kt_n, 128) for p_ in parts], axis=1)
    FinvH = np.ascontiguousarray(Fi.transpose(3, 2, 0, 1, 4).reshape(kt_n, 128, 4 * nj, 128))
    return FfH, FinvH


def _zpos(L, pos):
    t = np.linspace(0.0, 1.0, L, dtype=np.float32)[pos]
    tidx = pos.astype(np.float32)
    bands = np.linspace(1e-4, 15.0, 16, dtype=np.float32)
    ang = (np.float32(2.0 * math.pi) * tidx[:, None] * bands[None, :] / np.float32(L)).astype(np.float32)
    z = np.concatenate([t[:, None], np.cos(ang), -np.sin(ang)], axis=-1).astype(np.float32)
    return np.ascontiguousarray(z.T)


def _window(L, pos):
    t = np.linspace(0.0, 1.0, L, dtype=np.float32)[pos]
    mx = math.log(1e-2) / 0.3
    mn = math.log(1e-2) / 1.5
    deltas = np.abs(np.linspace(mn, mx, 512, dtype=np.float32))
    w = np.exp(-t[:, None] * deltas[None, :]).astype(np.float32)
    wb = w.copy()
    wb[pos == 0] = 0.0
    return np.stack([w, wb]).astype(np.float32)


def _core_consts(h):
    posS = (np.arange(LS) + h * 512) % LS
    posP = np.arange(LP)
    hyS = np.concatenate([posS[0::2], posS[1::2]])
    hyP = posP
    FfS, FinvS = _dft_half(LS, hyS)
    FfP, FinvP = _dft(LP, posP)
    c = {}
    c["FfS"] = FfS.astype(_bf)
    c["FinvS"] = FinvS.astype(_bf)
    c["FfP"] = np.ascontiguousarray(FfP.reshape(2, 128, 512).transpose(1, 0, 2)).astype(_bf)
    c["FinvP"] = np.ascontiguousarray(FinvP.reshape(4, 128, 256).transpose(1, 0, 2)).astype(_bf)
    c["ropeD"] = _rope_tab(posS, 64)
    c["ropeG"] = _rope_tab(posS, 128)
    c["zposP"] = _zpos(LP, hyP)
    c["zposS"] = _zpos(LS, hyS)
    c["winP"] = _window(LP, hyP)
    c["winS"] = _window(LS, hyS)
    a = np.zeros((128, 2), np.float32)
    a[:, 0] = float(h)
    a[:, 1] = 1.0 - float(h)
    c["acol"] = a
    c["ident_bf"] = np.eye(128, dtype=np.float32).astype(_bf)
    c["ident_f"] = np.eye(128, dtype=np.float32)
    pm = np.zeros((128, 128), np.float32)
    pm[np.arange(128) ^ 1, np.arange(128)] = 1.0
    c["perm_bf"] = pm.astype(_bf)
    return c


def make_in_maps(inputs):
    f = lambda a: np.ascontiguousarray(np.asarray(a, dtype=np.float32))
    I = {kk: f(v) for kk, v in inputs.items()}
    vecA = np.zeros((128, 128), np.float32)
    vecA[0:16] = I["norm_g"].reshape(16, 128)
    vecA[16:24] = I["final_g"].reshape(8, 128)
    vecA[24:72] = I["b_mod"].reshape(48, 128)
    vecA[80:88] = I["c_ctx"].reshape(8, 128)
    vecA[88:124] = I["hy_conv_w"][0].reshape(36, 128)
    vecB = np.zeros((128, 128), np.float32)
    vecB[0:12] = I["hy_conv_b"][0].reshape(12, 128)
    vecB[12, 0:64] = I["hy_b1"][0]
    vecB[13, 0:64] = I["hy_freq"][0]
    vecB[14, 0:64] = I["hy_b2"][0]
    vecB[15] = I["gq_q_g"][0]
    vecB[16] = I["gq_k_g"][0]
    vecB[19] = I["df_subln_g"][0]
    consts = [_core_consts(0), _core_consts(1)]
    shared = {
        "w_mod": I["w_mod"], "ev_w_in": I["ev_w_in"][0], "ev_w_out": I["ev_w_out"][0],
        "od_w_in": I["od_w_in"][0], "od_w_out": I["od_w_out"][0],
        "hy_w1": I["hy_w1"][0], "hy_w2": I["hy_w2"][0], "hy_w3": I["hy_w3"][0],
        "hy_skip": I["hy_skip"][0], "final_g": I["final_g"], "df_lambda": I["df_lambda"][0].reshape(1, 256),
        "gq_k_g": I["gq_k_g"][0], "vecB": vecB,
    }
    maps = []
    for c in range(8):
        b, h = c // 2, c % 2
        m = dict(shared)
        m.update(consts[h])
        va = vecA.copy()
        va[72:80] = I["c"][b].reshape(8, 128)
        m["vecA"] = va
        m["xp"] = np.ascontiguousarray(I["x_prompt"][4 * c:4 * c + 4].reshape(NPS * LP, D))
        m["xs"] = np.ascontiguousarray(np.roll(I["x_sample"][b], -h * 512, axis=0))
        m["cdk"] = np.ascontiguousarray(I["cache_diff_k"][b, 0].reshape(PAST, 512))
        m["cdv"] = np.ascontiguousarray(I["cache_diff_v"][b, 0].reshape(PAST, 512))
        m["cgk"] = np.ascontiguousarray(I["cache_gqa_k"][b, 0].reshape(PAST, 256))
        m["cgv"] = np.ascontiguousarray(I["cache_gqa_v"][b, 0].reshape(PAST, 256))
        maps.append(m)
    return maps


def assemble(results):
    yp = np.zeros((32, 256, D), np.float32)
    ys = np.zeros((4, 1024, D), np.float32)
    ndk = np.zeros((32, 1, 256, 4, 2, 64), np.float32)
    ndv = np.zeros((32, 1, 256, 4, 128), np.float32)
    ngk = np.zeros((32, 1, 256, 2, 128), np.float32)
    ngv = np.zeros((32, 1, 256, 2, 128), np.float32)
    for c, r in enumerate(results):
        b, h = c // 2, c % 2
        yp[4 * c:4 * c + 4] = np.asarray(r["y_p"]).reshape(4, 256, D)
        ys[b, h * 512:(h + 1) * 512] = np.asarray(r["y_s"])
        ndk[4 * c:4 * c + 4, 0] = np.asarray(r["ndk"]).reshape(4, 256, 4, 2, 64)
        ndv[4 * c:4 * c + 4, 0] = np.asarray(r["ndv"]).reshape(4, 256, 4, 128)
        ngk[4 * c:4 * c + 4, 0] = np.asarray(r["ngk"]).reshape(4, 256, 2, 128)
        ngv[4 * c:4 * c + 4, 0] = np.asarray(r["ngv"]).reshape(4, 256, 2, 128)
    return (yp, ys, ndk, ndv, ngk, ngv)


def kernel(**inputs):
    nc, _ = build_program()
    maps = make_in_maps(inputs)
    res = run_bass_kernel_spmd(nc, maps, core_ids=list(range(8)))
    return assemble(res.results)
```

```python
import math
from contextlib import ExitStack

import numpy as np
import ml_dtypes

import concourse.bass as bass
import concourse.mybir as mybir
from concourse.bass_utils import run_bass_kernel_spmd

F32 = mybir.dt.float32
BF16 = mybir.dt.bfloat16
ALU = mybir.AluOpType
AF = mybir.ActivationFunctionType
AX = mybir.AxisListType

_DTSIZE = {F32: 4, BF16: 2}
ENGS = ("pe", "act", "dve", "pool", "sp")
SKIP_SAME_ENGINE_WAR = False
PREFETCH_MOD1 = True


def _region(ap):
    t = ap.tensor
    name = t.name
    esz = _DTSIZE.get(ap.dtype, 4)
    dims = list(ap.ap)
    off = int(ap.offset)
    space = str(ap.space)
    if space == "PSUM":
        pstep = dims[0][0] if dims[0][0] else 1
        foff = off % pstep
        lo = hi = foff
        for st, cnt in dims[1:]:
            if cnt > 1:
                if st > 0:
                    hi += st * (cnt - 1)
                else:
                    lo += st * (cnt - 1)
        return (name, 0, 128, (lo * esz // 2048) * 2048, (hi * esz // 2048 + 1) * 2048)
    if space == "SB":
        pstep, pcnt = dims[0]
        if pstep == 0:
            row = int(np.prod(t.shape[1:]))
            p0 = off // row
            foff = off % row
            pcnt = 1
        else:
            p0 = off // pstep
            foff = off % pstep
        lo = hi = foff
        for st, cnt in dims[1:]:
            if cnt > 1:
                if st > 0:
                    hi += st * (cnt - 1)
                else:
                    lo += st * (cnt - 1)
        return (name, p0, p0 + pcnt, lo * esz, (hi + 1) * esz)
    lo = hi = off
    for st, cnt in dims:
        if cnt > 1:
            if st > 0:
                hi += st * (cnt - 1)
            else:
                lo += st * (cnt - 1)
    return (name, 0, 1, lo * esz, (hi + 1) * esz)


def _ovl(a, b):
    return a[1] < b[2] and b[1] < a[2] and a[3] < b[4] and b[3] < a[4]


def _covers(a, b):
    return a[1] <= b[1] and a[2] >= b[2] and a[3] <= b[3] and a[4] >= b[4]


class Op:
    __slots__ = ("eng", "fn", "idx", "deps", "signal", "semval", "is_dma", "dsem", "dval", "tag", "phase")

    def __init__(self, eng, fn, is_dma, tag=""):
        self.eng = eng
        self.fn = fn
        self.is_dma = is_dma
        self.deps = {}
        self.signal = False
        self.semval = None
        self.dsem = None
        self.dval = None
        self.tag = tag


class Sched:
    def __init__(self, nc, n_dma_sems=16):
        self.nc = nc
        self.ops = {e: [] for e in ENGS}
        self.writers = {}
        self.readers = {}
        self.n_dma_sems = n_dma_sems
        self.ro = set()
        self.phase = "setup"

    def _add_dep(self, op, d, raw=True):
        if d is op:
            return
        if SKIP_SAME_ENGINE_WAR and (not raw) and (not d.is_dma) and (not op.is_dma) and d.eng == op.eng:
            return
        if d.is_dma:
            op.deps[("dma", id(d))] = d
        else:
            cur = op.deps.get(d.eng)
            if cur is None or cur.idx < d.idx:
                op.deps[d.eng] = d

    def record(self, eng, fn, reads, writes, is_dma=False, tag=""):
        op = Op(eng, fn, is_dma, tag)
        op.phase = self.phase
        op.idx = len(self.ops[eng])
        rregs = [_region(a) for a in reads if a is not None and not isinstance(a, (int, float))]
        wregs = [_region(a) for a in writes if a is not None]
        for r in rregs:
            if r[0].startswith("pball") and r not in wregs:
                wregs.append(r)
        for r in rregs:
            if r[0] in self.ro:
                continue
            for wr, wop in self.writers.get(r[0], ()):
                if _ovl(wr, r):
                    self._add_dep(op, wop)
        for w in wregs:
            wl = self.writers.setdefault(w[0], [])
            for wr, wop in wl:
                if _ovl(wr, w):
                    self._add_dep(op, wop, raw=False)
            rl = self.readers.setdefault(w[0], [])
            for rr, rops in rl:
                if _ovl(rr, w):
                    for rop in rops.values():
                        self._add_dep(op, rop, raw=False)
        for w in wregs:
            wl = self.writers[w[0]]
            wl[:] = [e for e in wl if not _covers(w, e[0])]
            wl.append([w, op])
            rl = self.readers[w[0]]
            rl[:] = [e for e in rl if not _covers(w, e[0])]
        for r in rregs:
            if r[0] in self.ro:
                continue
            rl = self.readers.setdefault(r[0], [])
            key = ("dma", id(op)) if is_dma else eng
            for e in rl:
                if e[0] == r:
                    e[1][key] = op
                    break
            else:
                rl.append([r, {key: op}])
        for d in op.deps.values():
            d.signal = True
        self.ops[eng].append(op)
        return op

    def emit(self, stack):
        nc = self.nc
        sems = {e: stack.enter_context(nc.semaphore("s_" + e)) for e in ENGS}
        dsems = {e: [stack.enter_context(nc.semaphore(f"d_{e}_{i}")) for i in range(self.n_dma_sems)]
                 for e in ("sp", "pool", "act")}
        all_dma = []
        for e in ENGS:
            c = 0
            dcount = [0] * self.n_dma_sems
            kk = 0
            for op in self.ops[e]:
                if op.is_dma:
                    j = kk % self.n_dma_sems
                    kk += 1
                    dcount[j] += 16
                    op.dsem = dsems[e][j]
                    op.dval = dcount[j]
                    all_dma.append(op)
                elif op.signal:
                    c += 1
                    op.semval = c
        block = stack.enter_context(nc.Block())
        eng_obj = {"pe": "tensor", "act": "scalar", "dve": "vector", "pool": "gpsimd", "sp": "sync"}

        def make(e):
            def body(engine):
                waited = {}

                def wait(sem, val):
                    if waited.get(sem.num, 0) >= val:
                        return
                    waited[sem.num] = val
                    engine.wait_ge(sem, val)

                for op in self.ops[e]:
                    for d in op.deps.values():
                        if d.is_dma:
                            wait(d.dsem, d.dval)
                        else:
                            if d.eng == e and e == "pe":
                                continue
                            wait(sems[d.eng], d.semval)
                    if op.is_dma and op.dval > 16:
                        wait(op.dsem, op.dval - 16)
                    ins = op.fn(engine)
                    if op.is_dma:
                        ins.then_inc(op.dsem, 16)
                    elif op.signal:
                        ins.then_inc(sems[e], 1)
                if e == "sp":
                    last = {}
                    for op in all_dma:
                        last[op.dsem.num] = (op.dsem, op.dval)
                    for sem, val in last.values():
                        engine.wait_ge(sem, val)
            return body

        for e in ENGS:
            getattr(block, eng_obj[e])(make(e))


class K:
    def __init__(self, nc, sched):
        self.nc = nc
        self.s = sched

    def mm(self, out, lhsT, rhs, start=True, stop=True, sgc=False, tag="mm"):
        if sgc:
            fn = lambda e: e.matmul(out, lhsT, rhs, start=start, stop=stop, skip_group_check=True)
        else:
            fn = lambda e: e.matmul(out, lhsT, rhs, start=start, stop=stop)
        return self.s.record("pe", fn, [lhsT, rhs] + ([] if start else [out]), [out], tag=tag)

    def transpose(self, out, in_, ident, tag="tr"):
        return self.s.record("pe", lambda e: e.transpose(out, in_, ident), [in_, ident], [out], tag=tag)

    def act(self, out, in_, func, bias=None, scale=None, accum_out=None, tag="act"):
        kw = {}
        if bias is not None:
            kw["bias"] = bias
        if scale is not None:
            kw["scale"] = scale
        if accum_out is not None:
            kw["accum_out"] = accum_out
        rd = [in_] + [a for a in (bias, scale) if a is not None and not isinstance(a, (int, float))]
        wr = [out] + ([accum_out] if accum_out is not None else [])
        return self.s.record("act", lambda e: e.activation(out, in_, func, **kw), rd, wr, tag=tag)

    def tt(self, eng, out, in0, in1, op, tag="tt"):
        return self.s.record(eng, lambda e: e.tensor_tensor(out, in0, in1, op), [in0, in1], [out], tag=tag)

    def ts(self, eng, out, in0, s1, s2, op0, op1=None, tag="ts"):
        rd = [in0] + [a for a in (s1, s2) if a is not None and not isinstance(a, (int, float))]
        if op1 is None:
            return self.s.record(eng, lambda e: e.tensor_scalar(out, in0, s1, None, op0), rd, [out], tag=tag)
        return self.s.record(eng, lambda e: e.tensor_scalar(out, in0, s1, s2, op0, op1), rd, [out], tag=tag)

    def stt(self, eng, out, in0, scalar, in1, op0, op1, tag="stt"):
        rd = [in0, in1] + ([scalar] if not isinstance(scalar, (int, float)) else [])
        return self.s.record(eng, lambda e: e.scalar_tensor_tensor(out, in0, scalar, in1, op0, op1),
                             rd, [out], tag=tag)

    def copy(self, eng, out, in_, tag="copy"):
        if eng == "act":
            return self.s.record(eng, lambda e: e.copy(out, in_), [in_], [out], tag=tag)
        return self.s.record(eng, lambda e: e.tensor_copy(out, in_), [in_], [out], tag=tag)

    def memset(self, eng, out, val, tag="memset"):
        return self.s.record(eng, lambda e: e.memset(out, val), [], [out], tag=tag)

    def recip(self, out, in_, tag="recip"):
        return self.s.record("dve", lambda e: e.reciprocal(out, in_), [in_], [out], tag=tag)

    def rsum(self, out, in_, tag="rsum"):
        return self.s.record("dve", lambda e: e.reduce_sum(out, in_, AX.X), [in_], [out], tag=tag)

    def dma(self, q, out, in_, tag="dma"):
        return self.s.record(q, lambda e: e.dma_start(out=out, in_=in_), [in_], [out], is_dma=True, tag=tag)


D = 1024
NPS = 4
LP = 256
LS = 1024
PAST = 512
EPS = 1e-6
MAGIC = 12582912.0
TWO_PI = 2.0 * math.pi
LAM_INIT0 = 0.8 - 0.6 * math.exp(-0.3 * 0)

STAGES = ("l0", "l1", "final")
import os
EXPERIMENT = os.environ.get("KEXP", "")


class _Stop(Exception):
    pass


def build_program(stages=STAGES, dbg=None, stop_at=None):
    nc = bass.Bass("TRN2", target_bir_lowering=False)

    def din(name, shape, dt=F32):
        return nc.dram_tensor(name, list(shape), dt, kind="ExternalInput").ap()

    def dout(name, shape, dt=F32):
        return nc.dram_tensor(name, list(shape), dt, kind="ExternalOutput").ap()

    d_xp = din("xp", [NPS * LP, D])
    d_xs = din("xs", [LS, D])
    d_cdk = din("cdk", [PAST, 512])
    d_cdv = din("cdv", [PAST, 512])
    d_cgk = din("cgk", [PAST, 256])
    d_cgv = din("cgv", [PAST, 256])
    d_vecA = din("vecA", [128, 128])
    d_vecB = din("vecB", [128, 128])
    d_wmod = din("w_mod", [2, D, 3 * D])
    d_evin = din("ev_w_in", [D, 4096])
    d_evout = din("ev_w_out", [D, D])
    d_odin = din("od_w_in", [D, 2560])
    d_odout = din("od_w_out", [D, D])
    d_w1 = din("hy_w1", [33, 64])
    d_w2 = din("hy_w2", [64, 64])
    d_w3 = din("hy_w3", [64, 2048])
    d_skip = din("hy_skip", [2, 512])
    d_fg = din("final_g", [D])
    d_lam = din("df_lambda", [1, 256])
    d_gkg = din("gq_k_g", [128])
    d_identb = din("ident_bf", [128, 128], BF16)
    d_identf = din("ident_f", [128, 128])
    d_perm = din("perm_bf", [128, 128], BF16)
    d_ropeD = din("ropeD", [2, 128, LS])
    d_ropeG = din("ropeG", [2, 128, LS])
    d_zposP = din("zposP", [33, LP])
    d_zposS = din("zposS", [33, LS])
    d_winP = din("winP", [2, LP, 512])
    d_winS = din("winS", [2, LS, 512])
    d_FfP = din("FfP", [128, 2, 512], BF16)
    d_FinvP = din("FinvP", [128, 4, 256], BF16)
    d_FfS = din("FfS", [4, 128, 8, 256], BF16)
    d_FinvS = din("FinvS", [8, 128, 16, 128], BF16)
    d_acol = din("acol", [128, 2])

    o_yp = dout("y_p", [NPS * LP, D])
    o_ys = dout("y_s", [LS // 2, D])
    o_ndk = dout("ndk", [NPS * LP, 512])
    o_ndv = dout("ndv", [NPS * LP, 512])
    o_ngk = dout("ngk", [NPS * LP, 256])
    o_ngv = dout("ngv", [NPS * LP, 256])

    S = Sched(nc)
    k = K(nc, S)
    for a in (d_xp, d_xs, d_cdk, d_cdv, d_cgk, d_cgv, d_vecA, d_vecB, d_wmod, d_evin, d_evout, d_odin, d_odout,
              d_w1, d_w2, d_w3, d_skip, d_fg, d_lam, d_gkg, d_identb, d_identf, d_perm, d_ropeD, d_ropeG,
              d_zposP, d_zposS, d_winP, d_winS, d_FfP, d_FinvP, d_FfS, d_FinvS, d_acol):
        S.ro.add(a.tensor.name)

    dbg_outs = {}

    with ExitStack() as st:
        def sb(name, shape, dt):
            return st.enter_context(nc.sbuf_tensor("sb_" + name, list(shape), dt))

        xP = sb("xP", [128, 8, D], F32)
        xS = sb("xS", [128, 8, D], F32)
        hT = sb("hT", [128, 8, 1024], BF16)
        ybuf = sb("ybuf", [128, 8, 1024], BF16)
        wbuf = [sb(f"wbuf{i}", [128, 8, 256], BF16) for i in range(2)]
        REG_BYTES = 58 * 1024
        region = sb("region", [128, REG_BYTES // 2], BF16)
        ring = [sb(f"ring{i}", [128, 2048], BF16) for i in range(3)]
        spec = sb("spec", [128, 16, 256], BF16)
        gate_bc = sb("gate_bc", [128, D], F32)
        ident_b = sb("ident_b", [128, 128], BF16)
        ident_f = sb("ident_f", [128, 128], F32)
        perm_b = sb("perm_b", [128, 128], BF16)
        ones_f = sb("ones_f", [128, 128], F32)
        ones_b = sb("ones_b", [128, 128], BF16)
        epsc = sb("epsc", [128, 1], F32)
        acol = sb("acol", [128, 2], F32)
        vA = sb("vA", [128, 128], F32)
        vB = sb("vB", [128, 128], F32)
        vrow = sb("vrow", [128, 128], F32)
        sc = sb("sc", [128, 8, 2], BF16)
        modTs = [sb(f"modT{l}", [128, 24, 2], F32) for l in range(2)]
        gsTs = [sb(f"gsT{l}", [128, 2, 8], F32) for l in range(2)]
        modw = sb("modw", [128, 8, 256], BF16)
        cur = {"modT": modTs[0], "gsT": gsTs[0]}
        diag = [sb(f"diag{i}", [128, 128], F32) for i in range(2)]
        ss = sb("ss", [128, 8], F32)
        rs = sb("rs", [128, 8], F32)
        FfP = sb("FfP", [128, 2, 512], BF16)
        FinvP = sb("FinvP", [128, 4, 256], BF16)
        w1s = sb("w1s", [33, 64], F32)
        w2s = sb("w2s", [64, 64], F32)
        w3b = sb("w3b", [64, 2048], BF16)
        hid2b = sb("hid2b", [64, 1024], BF16)
        lamrow = sb("lamrow", [1, 256], F32)
        lamt = sb("lamt", [1, 8], F32)
        neglam = sb("neglam", [128, 1], F32)
        subg = sb("subg", [128, 1], F32)
        small = sb("small", [128, 64], F32)

        pball = st.enter_context(nc.psum_tensor("pball", [128, 4096], F32))
        pb = [pball[:, i * 512:(i + 1) * 512] for i in range(8)]

        def pbf(i):
            return pb[i].bitcast(BF16)

        def RV(off, shape, dt):
            n = int(np.prod(shape))
            esz = 2 if dt == BF16 else 4
            assert off % 4 == 0 and off + n * esz <= REG_BYTES, (off, shape)
            v = region[:, off // 2:(off + n * esz) // 2]
            if dt == F32:
                v = v.bitcast(F32)
            if len(shape) == 2:
                v = v.rearrange("p (a b) -> p a b", a=shape[0])
            elif len(shape) == 3:
                v = v.rearrange("p (a b c) -> p a b c", a=shape[0], b=shape[1])
            return v

        def dbg_dump(name, ap, shape, dt=F32):
            if dbg is not None and name in dbg:
                o = dout("dbg_" + name, shape, dt)
                k.dma("sp", o, ap)
                dbg_outs[name] = o

        rot = {"p": 0, "t": 0}

        def mark(label):
            S.phase = "after_" + label
            if stop_at is not None and label == stop_at:
                raise _Stop()

        pp_pool = {"banks": [0, 1]}

        def set_pp(banks):
            pp_pool["banks"] = list(banks)

        def next_pp():
            rot["p"] = (rot["p"] + 1) % len(pp_pool["banks"])
            return pb[pp_pool["banks"][rot["p"]]]

        pt_fixed = {"on": False}

        def next_pt():
            if pt_fixed["on"]:
                return 3
            rot["t"] ^= 1
            return 2 + rot["t"]

        k.dma("sp", ident_b[:], d_identb)
        k.dma("sp", ident_f[:], d_identf)
        k.dma("sp", perm_b[:], d_perm)
        k.dma("sp", acol[:], d_acol)
        k.dma("sp", vrow[:], d_vecA)
        k.memset("dve", ones_f[:], 1.0)
        k.memset("dve", ones_b[:], 1.0)
        k.memset("dve", epsc[:], EPS)
        k.transpose(pb[2][:, 0:128], vrow[:], ident_f[:])
        k.copy("dve", vA[:], pb[2][:, 0:128])
        k.dma("sp", vrow[:], d_vecB)
        k.transpose(pb[3][:, 0:128], vrow[:], ident_f[:])
        k.copy("dve", vB[:], pb[3][:, 0:128])
        k.dma("sp", FfP[:], d_FfP)
        k.dma("sp", FinvP[:], d_FinvP)
        k.dma("sp", w1s[:], d_w1)
        k.dma("sp", w2s[:], d_w2)
        k.dma("pool", w3b[:], d_w3)
        for t in range(8):
            k.dma("sp", xP[:, t, :], d_xp[t * 128:(t + 1) * 128, :])
        k.act(sc[:, :, 0], vA[:, 72:80], AF.Silu)
        k.act(sc[:, :, 1], vA[:, 80:88], AF.Silu)
        k.dma("sp", lamrow[:], d_lam)
        k.tt("dve", lamrow[:, 0:64], lamrow[:, 0:64], lamrow[:, 64:128], ALU.mult)
        k.tt("dve", lamrow[:, 128:192], lamrow[:, 128:192], lamrow[:, 192:256], ALU.mult)
        k.rsum(lamt[:, 0:1], lamrow[:, 0:64])
        k.rsum(lamt[:, 1:2], lamrow[:, 128:192])
        k.act(lamt[:, 2:4], lamt[:, 0:2], AF.Exp)
        k.tt("dve", lamt[:, 4:5], lamt[:, 3:4], lamt[:, 2:3], ALU.subtract)
        k.ts("dve", lamt[:, 5:6], lamt[:, 4:5], -LAM_INIT0, None, ALU.add)
        k.mm(pb[2][:, 0:1], ones_f[0:1, :], lamt[0:1, 5:6])
        k.copy("dve", neglam[:], pb[2][:, 0:1])
        k.ts("dve", subg[:], vB[:, 19:20], 1.0 - LAM_INIT0, None, ALU.mult)

        def wblock_dummy():
            pass

        def wblock(w2d, c0, ncols, buf):
            k.dma("pool", buf[:, :, 0:ncols],
                  w2d[:, c0:c0 + ncols].rearrange("(kk p) c -> p kk c", p=128))

        wrot = {"i": 0}

        def next_wbuf():
            wrot["i"] ^= 1
            return wbuf[wrot["i"]]

        def mod_block(l, blk, wb, pm):
            modT = modTs[l]
            wblock(d_wmod[l], blk * 256, 256, wb)
            for cc in range(2):
                for kc in range(8):
                    k.mm(pm[:, cc * 2:cc * 2 + 2], wb[:, kc, cc * 128:(cc + 1) * 128], sc[:, kc, :],
                         start=(kc == 0), stop=(kc == 7))
            c0 = blk * 2
            pm3 = pm[:, 0:4].rearrange("p (c n) -> p c n", n=2)
            for cond in range(2):
                k.tt("dve", modT[:, c0:c0 + 2, cond], pm3[:, :, cond],
                     vA[:, 24 + 24 * l + c0:24 + 24 * l + c0 + 2], ALU.add)

        def mod_finish(l):
            for cond in range(2):
                k.stt("dve", gsTs[l][:, cond, :], modTs[l][:, 8:16, cond], 1.0, vA[:, 8 * l:8 * l + 8],
                      ALU.add, ALU.mult)

        def compute_mod(l, blocks=range(12), finish=True):
            for blk in blocks:
                mod_block(l, blk, next_wbuf(), pb[2])
            if finish:
                mod_finish(l)

        def compute_gate_bc(cond):
            for j in range(8):
                dg = diag[j % 2]
                k.ts("dve", dg[:], ident_f[:], cur["modT"][:, 16 + j, cond:cond + 1], None, ALU.mult)
                k.mm(pb[4 + j // 4][:, (j % 4) * 128:(j % 4 + 1) * 128], ones_f[:], dg[:])
            k.copy("dve", gate_bc[:, 0:512], pb[4][:])
            k.copy("dve", gate_bc[:, 512:1024], pb[5][:])

        pre_rs = {}

        def adaln_stats(xg, ntiles, key, junk_off):
            ss_ = sb("ss_" + key, [128, 8], F32)
            rs_ = sb("rs_" + key, [128, 8], F32)
            junk = RV(junk_off, [1024], BF16)
            k.memset("dve", ss_[:], 0.0)
            for t in range(ntiles):
                k.act(junk, xg[:, t, :], AF.Square, accum_out=ss_[:, t:t + 1])
            k.act(rs_[:, 0:ntiles], ss_[:, 0:ntiles], AF.Ln, bias=epsc[:], scale=1.0 / D)
            k.act(rs_[:, 0:ntiles], rs_[:, 0:ntiles], AF.Exp, scale=-0.5)
            pre_rs[key] = rs_

        def adaln(xg, ntiles, cond, pre=None):
            xn = [RV(i * 2048, [1024], BF16) for i in range(8)]
            if pre is not None and pre in pre_rs:
                rs_u = pre_rs[pre]
            else:
                rs_u = rs
                k.memset("dve", ss[:], 0.0)
                for t in range(ntiles):
                    k.act(xn[t], xg[:, t, :], AF.Square, accum_out=ss[:, t:t + 1])
                k.act(rs[:, 0:ntiles], ss[:, 0:ntiles], AF.Ln, bias=epsc[:], scale=1.0 / D)
                k.act(rs[:, 0:ntiles], rs[:, 0:ntiles], AF.Exp, scale=-0.5)
            for t in range(ntiles):
                k.ts("dve", xn[t], xg[:, t, :], rs_u[:, t:t + 1], None, ALU.mult)
            for half in range(ntiles // 4):
                for j in range(8):
                    bank = pbf(next_pt())
                    for tt_ in range(4):
                        k.transpose(bank[:, tt_ * 128:(tt_ + 1) * 128],
                                    xn[half * 4 + tt_][:, j * 128:(j + 1) * 128], ident_b[:])
                    if j % 2 == 0:
                        k.act(hT[:, j, half * 512:(half + 1) * 512], bank[:, 0:512], AF.Identity,
                              bias=cur["modT"][:, j, cond:cond + 1], scale=cur["gsT"][:, cond, j:j + 1])
                    else:
                        k.ts("dve", hT[:, j, half * 512:(half + 1) * 512], bank[:, 0:512],
                             cur["gsT"][:, cond, j:j + 1], cur["modT"][:, j, cond:cond + 1], ALU.mult, ALU.add)

        deferred = []

        def defer(fn, delay=1):
            deferred.append([delay, fn])

        def run_deferred(flush=False):
            for e_ in deferred:
                e_[0] -= 1
            while deferred and (flush or deferred[0][0] <= 0):
                deferred.pop(0)[1]()

        def proj_fm(w2d, c0, nblk, ntok, evac):
            for b in range(nblk):
                wb = next_wbuf()
                wblock(w2d, c0 + b * 256, 256, wb)
                for cc in range(2):
                    for tb in range(ntok // 512):
                        ps = next_pp()
                        for kc in range(8):
                            k.mm(ps[:], wb[:, kc, cc * 128:(cc + 1) * 128], hT[:, kc, tb * 512:(tb + 1) * 512],
                                 start=(kc == 0), stop=(kc == 7))
                        run_deferred()
                        evac(b * 2 + cc, tb, ps)
            run_deferred(flush=True)

        def proj_tm(w2d, c0, nblk, ntiles, evac):
            for b in range(nblk):
                wb = next_wbuf()
                wblock(w2d, c0 + b * 256, 256, wb)
                for t in range(ntiles):
                    ps = next_pp()
                    for kc in range(8):
                        k.mm(ps[:, 0:256], hT[:, kc, t * 128:(t + 1) * 128], wb[:, kc, :],
                             start=(kc == 0), stop=(kc == 7))
                    run_deferred()
                    evac(b, t, ps)
            run_deferred(flush=True)

        def wout_residual(w2d, xg, ntiles):
            tmp = [RV(i * 1024, [256], F32) for i in range(4)]
            set_pp([0, 1, 4, 5, 6, 7])
            for b in range(4):
                wb = next_wbuf()
                wblock(w2d, b * 256, 256, wb)
                for t in range(ntiles):
                    ps = next_pp()
                    for kc in range(8):
                        k.mm(ps[:, 0:256], ybuf[:, kc, t * 128:(t + 1) * 128], wb[:, kc, :],
                             start=(kc == 0), stop=(kc == 7))
                    tm = tmp[t % 4]
                    k.tt("dve", tm, ps[:, 0:256], gate_bc[:, b * 256:(b + 1) * 256], ALU.mult)
                    k.tt("dve", xg[:, t, b * 256:(b + 1) * 256], xg[:, t, b * 256:(b + 1) * 256], tm, ALU.add)
            set_pp([0, 1])

        def sin_rr(out, ps_in, bcol, fcol, tmp_a, tmp_b, npart):
            k.ts("dve", tmp_a, ps_in, bcol, fcol, ALU.add, ALU.mult)
            k.ts("dve", tmp_b, tmp_a, 1.0 / TWO_PI, MAGIC, ALU.mult, ALU.add)
            k.ts("dve", tmp_b, tmp_b, MAGIC, None, ALU.subtract)
            k.stt("dve", tmp_a, tmp_b, -TWO_PI, tmp_a, ALU.mult, ALU.add)
            k.act(out, tmp_a, AF.Sin)

        OFF_ZT = 0
        OFF_G1 = 8192
        OFF_GG = 16384
        OFF_Y = 24576
        OFF_RAW = 32768
        OFF_T = 41472
        OFF_SKIP = 51712

        def hyena_group(xg_cond, L, nseq, d_zpos, d_win, ff_piece, finv_piece):
            kt_n = L // 128
            rt_n = 2 * kt_n
            nsl = nseq * 256
            seglen = 256 if nseq == 4 else 512
            nseg = 1024 // seglen
            zT = RV(OFF_ZT, [kt_n, 2, nsl], BF16)
            g1T = RV(OFF_G1, [kt_n, 2, nsl], BF16)
            gg = RV(OFF_GG, [4, 1024], BF16)
            Y = RV(OFF_Y, [rt_n, nsl], BF16)
            fa = RV(OFF_RAW, [kt_n, 256], BF16)
            fb = RV(OFF_RAW + 4096, [kt_n, 256], BF16)
            raw = [RV(OFF_RAW + i * 4352, [nseg, seglen + 2], F32) for i in range(2)]
            skipbc = RV(OFF_SKIP, [2, 512], F32)
            U = RV(OFF_T, [nseg, seglen], F32)
            ubfs = [RV(OFF_T + 4096, [1024], BF16), RV(OFF_T + 8192, [1024], BF16)]
            sgt = [RV(OFF_T + 6144 + i * 1024, [512], BF16) for i in range(2)]
            zp = RV(OFF_T, [1024], F32)
            ha = RV(OFF_T + 4096, [512], F32)
            hb = RV(OFF_T + 6144, [512], F32)
            h1 = RV(OFF_RAW, [1024], F32)
            winf = [RV(OFF_T + i * 2048, [256], F32) for i in range(2)]
            winb = [RV(OFF_T + 2048 + i * 2048, [256], F32) for i in range(2)]
            winf = [RV(OFF_T + 0, [256], F32), RV(OFF_T + 2048, [256], F32)]
            winb = [RV(OFF_T + 1024, [256], F32), RV(OFF_T + 3072, [256], F32)]
            ft1 = RV(OFF_T + 4096, [256], F32)
            ft2 = RV(OFF_T + 5120, [256], F32)
            xr = [RV(OFF_T + i * 512, [256], BF16) for i in range(2)]
            xi = [RV(OFF_T + 1024 + i * 512, [256], BF16) for i in range(2)]
            t1 = RV(OFF_T + 2048, [256], BF16)
            t2 = RV(OFF_T + 2560, [256], BF16)
            t3 = RV(OFF_T + 3072, [256], BF16)
            t4 = RV(OFF_T + 3584, [256], BF16)
            ytm = [RV(OFF_T + 8192 + i * 1024, [512], BF16) for i in range(2)]

            for o in range(2):
                k.dma("sp", skipbc[:, o, :], d_skip[o].partition_broadcast(128))

            k.dma("sp", zp[0:33, 0:L], d_zpos)
            nb = max(1, L // 512)
            bw = min(L, 512)
            for tb in range(nb):
                ps = next_pp()
                k.mm(ps[0:64, 0:bw], w1s[0:33, :], zp[0:33, tb * bw:(tb + 1) * bw])
                sin_rr(h1[0:64, tb * bw:(tb + 1) * bw], ps[0:64, 0:bw], vB[0:64, 12:13], vB[0:64, 13:14],
                       ha[0:64, 0:bw], hb[0:64, 0:bw], 64)
            for tb in range(nb):
                ps = next_pp()
                k.mm(ps[0:64, 0:bw], w2s[0:64, :], h1[0:64, tb * bw:(tb + 1) * bw])
                sin_rr(hid2b[0:64, tb * bw:(tb + 1) * bw], ps[0:64, 0:bw], vB[0:64, 14:15], vB[0:64, 13:14],
                       ha[0:64, 0:bw], hb[0:64, 0:bw], 64)

            mark("filt" + ("P" if nseq == 4 else "S"))
            for i in range(2):
                k.memset("dve", raw[i], 0.0)
            conv_state = {}

            def evacA(cc, tb, ps):
                if cc < 12:
                    rw = raw[cc % 2]
                    nsb = 512 // seglen
                    k.copy("act", rw[:, tb * nsb:(tb + 1) * nsb, 1:1 + seglen],
                           ps[:].rearrange("p (s t) -> p s t", s=nsb))
                    if tb == 1:
                        if nseq == 1:
                            a_ = acol[:, 0:1]
                            na_ = acol[:, 1:2]
                            k.ts("dve", rw[:, 0, 0:1], rw[:, 1, 512:513], a_, None, ALU.mult)
                            k.ts("dve", rw[:, 0, 513:514], rw[:, 1, 1:2], na_, None, ALU.mult)
                            k.ts("dve", rw[:, 1, 0:1], rw[:, 0, 512:513], na_, None, ALU.mult)
                            k.ts("dve", rw[:, 1, 513:514], rw[:, 0, 1:2], a_, None, ALU.mult)
                        w0 = vA[:, 88 + cc:89 + cc]
                        w1_ = vA[:, 100 + cc:101 + cc]
                        w2_ = vA[:, 112 + cc:113 + cc]
                        k.act(U, rw[:, :, 1:1 + seglen], AF.Identity, bias=vB[:, cc:cc + 1], scale=w1_)
                        k.stt("dve", U, rw[:, :, 0:seglen], w0, U, ALU.mult, ALU.add)
                        ubf = ubfs[cc % 2]
                        if cc < 8:
                            dst = ubf.rearrange("p (s t) -> p s t", s=nseg)
                        else:
                            dst = gg[:, cc - 8, :].rearrange("p (s t) -> p s t", s=nseg)
                        k.stt("dve", dst, rw[:, :, 2:2 + seglen], w2_, U, ALU.mult, ALU.add)
                        if cc < 8:
                            def tail(cc=cc, ubf=ubf):
                                bank = pbf(next_pt())
                                uv = ubf.rearrange("p (s t two) -> p s t two", s=nseq, two=2)
                                khh = kt_n // 2
                                for t in range(8):
                                    if nseq == 1:
                                        s_i, k_i = t // kt_n, t % kt_n
                                        par_, kk_i = k_i // khh, k_i % khh
                                        k.transpose(bank[:, t * 128:(t + 1) * 128],
                                                    uv[:, s_i, kk_i * 128:(kk_i + 1) * 128, par_], ident_b[:])
                                    else:
                                        k.transpose(bank[:, t * 128:(t + 1) * 128], ubf[:, t * 128:(t + 1) * 128],
                                                    ident_b[:])
                                dstT = zT if cc < 4 else g1T
                                c4 = cc % 4
                                hc, off = c4 // 2, (c4 % 2) * 128
                                src = bank[:, 0:1024].rearrange("p (s kk c) -> p kk s c", s=nseq, kk=kt_n)
                                dd = dstT[:, :, hc, :].rearrange("p kk (s c) -> p kk s c", s=nseq)[:, :, :,
                                                                                                   off:off + 128]
                                k.copy("dve", dd, src)
                            defer(tail, delay=2)
                else:
                    sg = sgt[tb]
                    k.act(sg, ps[:], AF.Silu)
                    k.tt("dve", gg[:, cc - 12, tb * 512:(tb + 1) * 512], gg[:, cc - 12, tb * 512:(tb + 1) * 512],
                         sg, ALU.mult)

            set_pp([0, 1, 4, 5, 6, 7])
            proj_fm(d_evin, 0, 8, 1024, evacA)
            set_pp([0, 1])
            dbg_dump("zT" + ("P" if nseq == 4 else "S"), region[:, OFF_ZT // 2:OFF_ZT // 2 + 4096], [128, 4096], BF16)
            dbg_dump("gg" + ("P" if nseq == 4 else "S"), region[:, OFF_GG // 2:OFF_GG // 2 + 4096], [128, 4096], BF16)

            mark("projA" + ("P" if nseq == 4 else "S"))
            sgn = 2 if nseq == 4 else 1
            ngrp = nseq // sgn
            ncol = sgn * 256
            ft1s = [RV(OFF_T + 4096, [256], F32), RV(OFF_T + 6144, [256], F32)]
            ft2s = [RV(OFF_T + 5120, [256], F32), RV(OFF_T + 7168, [256], F32)]

            def emit_taps(o, hc, kt2):
                pss_ = []
                for u in range(2):
                    kt = kt2 + u
                    ps = next_pp()
                    k.mm(ps[:, 0:256], hid2b[0:64, kt * 128:(kt + 1) * 128],
                         w3b[0:64, (o * 2) * 512 + hc * 256:(o * 2) * 512 + hc * 256 + 256])
                    k.mm(ps[:, 256:512], hid2b[0:64, kt * 128:(kt + 1) * 128],
                         w3b[0:64, (o * 2 + 1) * 512 + hc * 256:(o * 2 + 1) * 512 + hc * 256 + 256])
                    k.dma("sp", winf[u], d_win[0, kt * 128:(kt + 1) * 128, hc * 256:(hc + 1) * 256])
                    k.dma("sp", winb[u], d_win[1, kt * 128:(kt + 1) * 128, hc * 256:(hc + 1) * 256])
                    pss_.append(ps)
                for u in range(2):
                    k.tt("dve", ft1s[u], pss_[u][:, 0:256], winf[u], ALU.mult)
                for u in range(2):
                    k.tt("dve", ft2s[u], pss_[u][:, 256:512], winb[u], ALU.mult)
                for u in range(2):
                    k.tt("dve", fa[:, kt2 + u, :], ft1s[u], ft2s[u], ALU.add)
                for u in range(2):
                    k.tt("dve", fb[:, kt2 + u, :], ft1s[u], ft2s[u], ALU.subtract)

            eo = (nseq == 1)
            if eo:
                kh = kt_n // 2
                nj = kt_n // 2
                tb_ = [RV(OFF_T + i * 512, [256], BF16) for i in range(14)]
                Ao_sb, Bo_sb, Xc, Xs, Xcm, Xsm, p1, p2, p3, p4, p5, p6, p7, p8 = tb_
                xrot = {"i": 0}
                combos = [(o, hc) for o in range(2) for hc in range(2)]

                def Ytile(ci, r):
                    if ci % 2 == 0:
                        return Y[:, r, 0:256]
                    return wbuf[r // 8][:, r % 8, :]

                def stage_A(ci, j):
                    o, hc = combos[ci]
                    pc = ff_piece(j)
                    pa = next_pp()
                    for kk in range(kh):
                        k.mm(pa[:, 0:256], pc[:, kk, 0:128], fa[:, kk, :], start=(kk == 0), stop=(kk == kh - 1))
                    for kk in range(kh):
                        k.mm(pa[:, 256:512], pc[:, kh + kk, 0:128], fa[:, kh + kk, :],
                             start=(kk == 0), stop=(kk == kh - 1))
                    pbk = next_pp()
                    for kk in range(kh):
                        k.mm(pbk[:, 0:256], pc[:, kk, 128:256], fb[:, kk, :], start=(kk == 0), stop=(kk == kh - 1))
                    for kk in range(kh):
                        k.mm(pbk[:, 256:512], pc[:, kh + kk, 128:256], fb[:, kh + kk, :],
                             start=(kk == 0), stop=(kk == kh - 1))
                    k.copy("act", Ao_sb, pa[:, 256:512])
                    k.copy("act", Bo_sb, pbk[:, 256:512])
                    k.tt("dve", p1, pa[:, 0:256], skipbc[:, o, hc * 256:(hc + 1) * 256], ALU.add)
                    k.tt("dve", spec[:, 4 * j + 1, :], pbk[:, 0:256], Bo_sb, ALU.add)
                    k.tt("dve", spec[:, 4 * j + 3, :], Bo_sb, pbk[:, 0:256], ALU.subtract)
                    k.tt("dve", spec[:, 4 * j + 0, :], p1, Ao_sb, ALU.add)
                    k.tt("dve", spec[:, 4 * j + 2, :], p1, Ao_sb, ALU.subtract)
                    Kc, Ks, Kcm, Ksm = (spec[:, 4 * j + q_, :] for q_ in range(4))
                    xb_ = xrot["i"] % 2
                    xrot["i"] += 1
                    bA, bB = pb[4 + xb_ * 2], pb[5 + xb_ * 2]
                    zc = slice(0, 256)
                    for kk in range(kh):
                        k.mm(bA[:, 0:256], pc[:, kk, 0:128], zT[:, kk, hc, zc], start=(kk == 0), stop=(kk == kh - 1))
                    for kk in range(kh):
                        k.mm(bA[:, 256:512], pc[:, kh + kk, 0:128], zT[:, kh + kk, hc, zc],
                             start=(kk == 0), stop=(kk == kh - 1))
                    for kk in range(kh):
                        k.mm(bB[:, 0:256], pc[:, kk, 128:256], zT[:, kk, hc, zc], start=(kk == 0), stop=(kk == kh - 1))
                    for kk in range(kh):
                        k.mm(bB[:, 256:512], pc[:, kh + kk, 128:256], zT[:, kh + kk, hc, zc],
                             start=(kk == 0), stop=(kk == kh - 1))
                    k.copy("act", Ao_sb, bA[:, 256:512])
                    k.copy("act", Bo_sb, bB[:, 256:512])
                    k.tt("dve", Xc, bA[:, 0:256], Ao_sb, ALU.add)
                    k.tt("dve", Xs, bB[:, 0:256], Bo_sb, ALU.add)
                    k.tt("dve", Xcm, bA[:, 0:256], Ao_sb, ALU.subtract)
                    k.tt("dve", Xsm, Bo_sb, bB[:, 0:256], ALU.subtract)
                    k.tt("dve", p1, Xc, Kc, ALU.mult)
                    k.tt("dve", p2, Xs, Ks, ALU.mult)
                    k.tt("dve", p3, Xc, Ks, ALU.mult)
                    k.tt("dve", p4, Xs, Kc, ALU.mult)
                    k.tt("dve", p5, Xcm, Kcm, ALU.mult)
                    k.tt("dve", p6, Xsm, Ksm, ALU.mult)
                    k.tt("dve", p7, Xcm, Ksm, ALU.mult)
                    k.tt("dve", p8, Xsm, Kcm, ALU.mult)
                    k.tt("dve", Ytile(ci, 4 * j + 0), p1, p2, ALU.subtract)
                    k.tt("dve", Ytile(ci, 4 * j + 1), p3, p4, ALU.add)
                    k.tt("dve", Ytile(ci, 4 * j + 2), p5, p6, ALU.subtract)
                    k.tt("dve", Ytile(ci, 4 * j + 3), p7, p8, ALU.add)

                def stage_B(ci, i):
                    o, hc = combos[ci]
                    pc = finv_piece(i)
                    par, kk_ = i // kh, i % kh
                    ps = next_pp()
                    nmm = 4 * nj
                    for jf in range(nj):
                        for part in range(4):
                            idx = jf * 4 + part
                            k.mm(ps[:, 0:256], pc[:, idx, :], Ytile(ci, 4 * jf + part),
                                 start=(idx == 0), stop=(idx == nmm - 1))
                    cs = slice(0, 256)
                    if o == 0:
                        k.tt("dve", zT[:, i, hc, cs], ps[:, 0:256], g1T[:, i, hc, cs], ALU.mult)
                    else:
                        yt = ytm[i % 2]
                        k.copy("act", yt[:, 0:256], ps[:, 0:256])
                        bank = pbf(next_pt())
                        for c2 in range(2):
                            k.transpose(bank[:, c2 * 128:(c2 + 1) * 128], yt[:, c2 * 128:(c2 + 1) * 128], ident_b[:])
                        for c2 in range(2):
                            ch = hc * 2 + c2
                            yv = ybuf[:, ch, :].rearrange("p (s t two) -> p s t two", s=nseq, two=2)
                            gv = gg[:, ch, :].rearrange("p (s t two) -> p s t two", s=nseq, two=2)
                            k.tt("dve", yv[:, 0, kk_ * 128:(kk_ + 1) * 128, par], bank[:, c2 * 128:(c2 + 1) * 128],
                                 gv[:, 0, kk_ * 128:(kk_ + 1) * 128, par], ALU.mult)

                ncomb = len(combos)
                set_pp([0, 1, 2])
                pt_fixed["on"] = True
                for kt2 in range(0, kt_n, 2):
                    emit_taps(combos[0][0], combos[0][1], kt2)
                for j in range(nj):
                    stage_A(0, j)
                for ci in range(ncomb):
                    if ci + 1 < ncomb:
                        no_, nhc_ = combos[ci + 1]
                        taps_l = list(range(0, kt_n, 2))
                        assert len(taps_l) == kt_n // 2 and nj == kt_n // 2
                        for t_ in range(kt_n // 2):
                            emit_taps(no_, nhc_, taps_l[t_])
                            stage_B(ci, t_)
                        for j in range(nj):
                            stage_A(ci + 1, j)
                            stage_B(ci, kt_n // 2 + j)
                    else:
                        for i in range(kt_n):
                            stage_B(ci, i)
                set_pp([0, 1])
                pt_fixed["on"] = False

            else:
                combos = [(o, hc) for o in range(2) for hc in range(2)]
                for kt2 in range(0, kt_n, 2):
                    emit_taps(combos[0][0], combos[0][1], kt2)
                for ci, (o, hc) in enumerate(combos):
                    if True:
                        nxt = combos[ci + 1] if ci + 1 < len(combos) else None
                        pairs_left = list(range(0, kt_n, 2)) if nxt is not None else []
                        npairs = kt_n // 2
                        for j in range(kt_n):
                            pc = ff_piece(j)
                            psc = next_pp()
                            for kk in range(kt_n):
                                k.mm(psc[:, 0:256], pc[:, kk, 0:128], fa[:, kk, :], start=(kk == 0), stop=(kk == kt_n - 1))
                            pss = next_pp()
                            for kk in range(kt_n):
                                k.mm(pss[:, 0:256], pc[:, kk, 128:256], fb[:, kk, :], start=(kk == 0), stop=(kk == kt_n - 1))
                            k.tt("dve", spec[:, 2 * j, :], psc[:, 0:256], skipbc[:, o, hc * 256:(hc + 1) * 256], ALU.add)
                            k.copy("act", spec[:, 2 * j + 1, :], pss[:, 0:256])
                            for g in range(ngrp):
                                xb_ = (g if ngrp > 1 else j) % 2
                                xc = pb[4 + xb_ * 2]
                                xs_ = pb[5 + xb_ * 2]
                                for kk in range(kt_n):
                                    k.mm(xc[:, 0:ncol], pc[:, kk, 0:128], zT[:, kk, hc, g * ncol:(g + 1) * ncol],
                                         start=(kk == 0), stop=(kk == kt_n - 1))
                                for kk in range(kt_n):
                                    k.mm(xs_[:, 0:ncol], pc[:, kk, 128:256], zT[:, kk, hc, g * ncol:(g + 1) * ncol],
                                         start=(kk == 0), stop=(kk == kt_n - 1))
                                for s_ in range(sgn):
                                    sq = g * sgn + s_
                                    a_, b_ = xr[s_ % 2], xi[s_ % 2]
                                    k.copy("act", a_, xc[:, s_ * 256:(s_ + 1) * 256])
                                    k.copy("act", b_, xs_[:, s_ * 256:(s_ + 1) * 256])
                                    Kr, Ki = spec[:, 2 * j, :], spec[:, 2 * j + 1, :]
                                    k.tt("dve", t1, a_, Kr, ALU.mult)
                                    k.tt("dve", t2, b_, Ki, ALU.mult)
                                    k.tt("dve", t3, a_, Ki, ALU.mult)
                                    k.tt("dve", t4, b_, Kr, ALU.mult)
                                    k.tt("dve", Y[:, 2 * j, sq * 256:(sq + 1) * 256], t1, t2, ALU.subtract)
                                    k.tt("dve", Y[:, 2 * j + 1, sq * 256:(sq + 1) * 256], t3, t4, ALU.add)
                        for i in range(kt_n):
                            if pairs_left and ((i + 1) * npairs) // kt_n > npairs - len(pairs_left):
                                emit_taps(nxt[0], nxt[1], pairs_left.pop(0))
                            pc = finv_piece(i)
                            for g in range(ngrp):
                                ps = next_pp()
                                for r in range(rt_n):
                                    k.mm(ps[:, 0:ncol], pc[:, r, :], Y[:, r, g * ncol:(g + 1) * ncol],
                                         start=(r == 0), stop=(r == rt_n - 1))
                                if o == 0:
                                    k.tt("dve", zT[:, i, hc, g * ncol:(g + 1) * ncol], ps[:, 0:ncol],
                                         g1T[:, i, hc, g * ncol:(g + 1) * ncol], ALU.mult)
                                else:
                                    yt = ytm[(i * ngrp + g) % 2]
                                    k.copy("act", yt[:, 0:ncol], ps[:, 0:ncol])
                                    bank = pbf(next_pt())
                                    for s_ in range(sgn):
                                        for c2 in range(2):
                                            slot = s_ * 2 + c2
                                            k.transpose(bank[:, slot * 128:(slot + 1) * 128],
                                                        yt[:, s_ * 256 + c2 * 128:s_ * 256 + (c2 + 1) * 128], ident_b[:])
                                    for s_ in range(sgn):
                                        sq = g * sgn + s_
                                        tok0 = sq * L + i * 128
                                        for c2 in range(2):
                                            slot = s_ * 2 + c2
                                            ch = hc * 2 + c2
                                            k.tt("dve", ybuf[:, ch, tok0:tok0 + 128], bank[:, slot * 128:(slot + 1) * 128],
                                                 gg[:, ch, tok0:tok0 + 128], ALU.mult)
                        if o == 0 and hc == 0:
                            dbg_dump("z2" + ("P" if nseq == 4 else "S"), region[:, OFF_ZT // 2:OFF_ZT // 2 + 4096], [128, 4096], BF16)

        ring_rot = {"i": 0}

        def ffS_piece(j):
            r = ring[ring_rot["i"] % 3]
            ring_rot["i"] += 1
            v = r[:, 0:2048].rearrange("p (kk c) -> p kk c", kk=8)
            k.dma("sp", v, d_FfS[j])
            return v

        def finvS_piece(i):
            r = ring[ring_rot["i"] % 3]
            ring_rot["i"] += 1
            v = r[:, 0:2048].rearrange("p (r c) -> p r c", r=16)
            k.dma("sp", v, d_FinvS[i])
            return v

        def ffP_piece(j):
            return FfP[:, :, j * 256:(j + 1) * 256]

        def finvP_piece(i):
            return FinvP[:, :, i * 128:(i + 1) * 128]

        OFF_Q = 0
        OFF_SG0 = 8192
        OFF_K0 = 16384
        OFF_V0 = 28672
        OFF_SG1 = 16384
        OFF_K1 = 32768
        OFF_V1P = 36864
        OFF_V1S = 24576
        ropeC = RV(41216, [LS], F32)
        ropeS = RV(45312, [LS], F32)
        att_tmp = RV(49408, [2048], F32)

        def attention(nheads, ncomp, dk, qT, kT, Vaug, sgT, units, scale, ych0, finalize_diff, extra=None):
            Eb = att_tmp[:].bitcast(BF16)
            E = [Eb[:, i * 512:(i + 1) * 512] for i in range(8)]
            fin_sq = RV(57600, [512], BF16)
            pair = all(u_[1] <= 256 for u_ in units)

            steps = []
            for (q0, nq, ktl, tok0) in units:
                for h in range(nheads):
                    hh = (h % 2) if pair else 0
                    for ki, (ktile, kc0) in enumerate(ktl):
                        for m in range(ncomp):
                            steps.append((q0, nq, tok0, h, m, ki, ktile, kc0, ki == 0, ki == len(ktl) - 1, hh))
            n = len(steps)

            def banks(h, m):
                if ncomp == 2:
                    g = m
                else:
                    g = ((h // 2) % 2) if pair else (h % 2)
                return pb[4 + 2 * g], pb[5 + 2 * g]

            def issue_S(i):
                q0, nq, tok0, h, m, ki, ktile, kc0, first, last, hh = steps[i]
                ps = pb[i % 3]
                if ncomp == 2:
                    k.mm(ps[:, 0:nq], kT[:, h, kc0:kc0 + 128], qT[m][:, h, q0:q0 + nq])
                else:
                    k.mm(ps[:, 0:nq], kT[0:128, h, kc0:kc0 + 128], qT[0:128, h, q0:q0 + nq])
                k.act(E[i % 8][:, 0:nq], ps[:, 0:nq], AF.Exp, scale=scale)

            def finalize_stages(st_):
                q0, nq, tok0, h, m, ki, ktile, kc0, first, last, hh = st_
                W = 2 * nq if pair else nq
                A, B = ropeC[:, 0:W], ropeC[:, 512:512 + W]
                C, Dd = ropeS[:, 0:W], ropeS[:, 512:512 + W]
                sq_ = fin_sq[:, 0:W]
                if pair:
                    h0 = h - 1
                    dst = ybuf[:, ych0 + h0:ych0 + h0 + 2, tok0:tok0 + nq]
                    sg_ = sgT[:, h0:h0 + 2, tok0:tok0 + nq]
                    C_ = C.rearrange("p (a c) -> p a c", a=2)
                else:
                    dst = ybuf[:, ych0 + h, tok0:tok0 + nq]
                    sg_ = sgT[:, h, tok0:tok0 + nq]
                    C_ = C
                if ncomp == 2:
                    O0, S0 = banks(h, 0)
                    O1, S1 = banks(h, 1)

                    def s0a():
                        k.copy("dve", C, O0[:, 0:W])
                        k.act(A, S0[:, 0:W], AF.Ln)

                    def s0():
                        k.copy("dve", Dd, O1[:, 0:W])
                        k.act(B, S1[:, 0:W], AF.Ln)

                    def s1():
                        k.act(A, A, AF.Exp, scale=-1.0)
                        k.act(B, B, AF.Exp, scale=-1.0)

                    def s2():
                        k.tt("dve", C, C, A, ALU.mult)
                        k.tt("dve", Dd, Dd, B, ALU.mult)
                        k.stt("dve", C, Dd, neglam[:], C, ALU.mult, ALU.add)
                        k.tt("dve", sq_, C, C, ALU.mult)
                        k.mm(pb[3][:, 0:W], ones_b[:], sq_)

                    def s3():
                        k.act(A, pb[3][:, 0:W], AF.Ln, bias=epsc[:], scale=1.0 / 128)
                        k.act(A, A, AF.Exp, scale=-0.5)

                    def s4():
                        k.tt("dve", C, C, A, ALU.mult)
                        k.stt("dve", dst, C_, subg[:], sg_, ALU.mult, ALU.mult)
                    return [s0a, s0, s1, s2, s3, s4]
                O0, S0 = banks(h, 0)

                def g0():
                    k.copy("dve", C, O0[:, 0:W])
                    k.act(A, S0[:, 0:W], AF.Ln)

                def g1():
                    k.act(A, A, AF.Exp, scale=-1.0)

                def g2():
                    k.tt("dve", C, C, A, ALU.mult)
                    k.tt("dve", dst, C_, sg_, ALU.mult)
                return [g0, g1, g2]

            pending = []
            cur_fin = {}

            def issue_PV(i):
                q0, nq, tok0, h, m, ki, ktile, kc0, first, last, hh = steps[i]
                Ob, Sb = banks(h, m)
                e = E[i % 8]
                c0 = hh * 256
                k.mm(Ob[:, c0:c0 + nq], Vaug(ktile, h), e[:, 0:nq], start=first, stop=last)
                k.mm(Sb[:, c0:c0 + nq], ones_b[:], e[:, 0:nq], start=first, stop=last)
                for p_ in list(pending):
                    p_.pop(0)()
                    if not p_:
                        pending.remove(p_)
                fin_now = last and (hh == 1 or not pair)
                if ncomp == 2:
                    if fin_now and m == 0:
                        cur_fin["s"] = finalize_stages(steps[i])
                        cur_fin["s"].pop(0)()
                    elif fin_now and m == 1:
                        stg_ = cur_fin["s"]
                        stg_.pop(0)()
                        pending.append(stg_)
                elif fin_now:
                    stg_ = finalize_stages(steps[i])
                    stg_.pop(0)()
                    pending.append(stg_)

            LA = 2
            for j in range(min(LA, n)):
                issue_S(j)
            for i in range(n):
                if i + LA < n:
                    issue_S(i + LA)
                issue_PV(i)
                if extra is not None:
                    extra(i)
            while pending:
                for p_ in list(pending):
                    p_.pop(0)()
                    if not p_:
                        pending.remove(p_)

        rope_cnt = {"i": 0}

        def rope_evac(ps, dst, col0, n, gcol=None, post=None):
            par = rope_cnt["i"] % 2
            rope_cnt["i"] += 1
            qb = att_tmp[:, par * 256:(par + 1) * 256].bitcast(BF16)
            rt1 = att_tmp[:, 512 + par * 512:1024 + par * 512]
            rt2 = att_tmp[:, 1536:2048]
            if gcol is None:
                k.copy("act", qb[:, 0:n], ps[:, 0:n])
            else:
                k.ts("dve", qb[:, 0:n], ps[:, 0:n], gcol, None, ALU.mult)

            def tail():
                pq = pb[next_pt()]
                k.mm(pq[:, 0:n], perm_b[:], qb[:, 0:n])
                k.tt("dve", rt1[:, 0:n], qb[:, 0:n], ropeC[:, col0:col0 + n], ALU.mult)
                k.tt("dve", rt2[:, 0:n], pq[:, 0:n], ropeS[:, col0:col0 + n], ALU.mult)
                if post is None:
                    k.tt("dve", dst, rt1[:, 0:n], rt2[:, 0:n], ALU.add)
                else:
                    k.tt("dve", rt1[:, 0:n], rt1[:, 0:n], rt2[:, 0:n], ALU.add)
                    k.tt("dve", dst, rt1[:, 0:n], post, ALU.mult)
            defer(tail)

        def layer0_attn(is_sample):
            Tk = 1536 if is_sample else 1024
            nkt = Tk // 128
            qT = RV(OFF_Q, [4, 1024], BF16)
            sgT = RV(OFF_SG0, [4, 1024], BF16)
            kT = RV(OFF_K0, [4, Tk], BF16)
            Va = RV(OFF_V0, [nkt, 4, 130], BF16)
            k.memset("dve", Va[:, :, :, 128:130], 1.0)
            kst = [ring[0][:, i * 512:(i + 1) * 512].bitcast(F32) for i in range(4)]
            kbf = [ring[1][:, i * 256:(i + 1) * 256] for i in range(4)]
            koff = 512 if is_sample else 0
            if is_sample:
                k.dma("sp", ropeC[:], d_ropeD[0])
                k.dma("sp", ropeS[:], d_ropeD[1])
                stg = ring[2][:, 0:2048].rearrange("p (t c) -> p t c", t=4)
                k.dma("pool", stg, d_cdk.rearrange("(t p) c -> p t c", p=128))
                for ct in range(4):
                    bank = pbf(next_pt())
                    for h in range(4):
                        k.transpose(bank[:, h * 128:(h + 1) * 128], stg[:, ct, h * 128:(h + 1) * 128], ident_b[:])
                    k.copy("dve", kT[:, :, ct * 128:(ct + 1) * 128],
                           bank[:, 0:512].rearrange("p (h c) -> p h c", h=4))
                for ct in range(4):
                    k.dma("pool", Va[:, ct, :, 0:128],
                          d_cdv[ct * 128:(ct + 1) * 128, :].rearrange("p (h d) -> p h d", h=4))

            def evac_q(cc, tb, ps):
                if is_sample:
                    rope_evac(ps, qT[:, cc, tb * 512:(tb + 1) * 512], tb * 512, 512)
                else:
                    k.copy("act", qT[:, cc, tb * 512:(tb + 1) * 512], ps[:])

            def evac_k_fm(cc, tb, ps):
                rope_evac(ps, kT[:, cc, koff + tb * 512:koff + (tb + 1) * 512], tb * 512, 512)

            cnt = {"i": 0}

            def evac_k_tm(b, t, ps):
                i = cnt["i"] % 4
                cnt["i"] += 1
                if EXPERIMENT == "B":
                    return
                k.copy("act", kst[i], ps[:, 0:256])
                if EXPERIMENT != "A":
                    k.dma("sp", o_ndk[t * 128:(t + 1) * 128, b * 256:(b + 1) * 256], kst[i])
                k.copy("dve", kbf[i], ps[:, 0:256])

                def tail(i=i, b=b, t=t):
                    bank = pbf(next_pt())
                    for h2 in range(2):
                        k.transpose(bank[:, h2 * 128:(h2 + 1) * 128], kbf[i][:, h2 * 128:(h2 + 1) * 128], ident_b[:])
                    k.copy("dve", kT[:, 2 * b:2 * b + 2, t * 128:(t + 1) * 128],
                           bank[:, 0:256].rearrange("p (h c) -> p h c", h=2))
                defer(tail, delay=2)

            def evac_v(b, t, ps):
                if not is_sample:
                    i = cnt["i"] % 4
                    cnt["i"] += 1
                    k.copy("act", kst[i], ps[:, 0:256])
                    k.dma("sp", o_ndv[t * 128:(t + 1) * 128, b * 256:(b + 1) * 256], kst[i])
                vt = t + (4 if is_sample else 0)
                k.copy("dve", Va[:, vt, 2 * b:2 * b + 2, 0:128], ps[:, 0:256].rearrange("p (h c) -> p h c", h=2))

            def evac_g(cc, tb, ps):
                k.act(sgT[:, cc, tb * 512:(tb + 1) * 512], ps[:], AF.Silu)

            sfx = "S" if is_sample else "P"
            set_pp([0, 1, 4, 5, 6, 7])
            mark("pre" + sfx)
            proj_fm(d_evin, 2048, 2, 1024, evac_q)
            mark("pq" + sfx)
            if is_sample:
                proj_fm(d_evin, 2560, 2, 1024, evac_k_fm)
            else:
                proj_tm(d_evin, 2560, 2, 8, evac_k_tm)
            mark("pk" + sfx)
            proj_tm(d_evin, 3072, 2, 8, evac_v)
            mark("pv" + sfx)
            proj_fm(d_evin, 3584, 2, 1024, evac_g)
            mark("pg" + sfx)
            set_pp([0, 1])
            if is_sample:
                units = [(qb * 512, 512, [(kt, kt * 128) for kt in range(12)], qb * 512) for qb in range(2)]
            else:
                units = [(s_ * 256, 256, [(2 * s_ + i, (2 * s_ + i) * 128) for i in range(2)], s_ * 256)
                         for s_ in range(4)]
            dbg_dump("qT" + ("S" if is_sample else "P"), region[:, OFF_Q // 2:OFF_Q // 2 + 4096], [128, 4096], BF16)
            dbg_dump("kT" + ("S" if is_sample else "P"), region[:, OFF_K0 // 2:OFF_K0 // 2 + 4 * Tk], [128, 4 * Tk], BF16)
            qpad = [hT[:, 0:4, :], hT[:, 4:8, :]]
            k.memset("dve", qpad[0][64:128, :, :], 0.0)
            k.memset("dve", qpad[1][0:64, :, :], 0.0)
            k.copy("dve", qpad[0][0:64, :, :], qT[0:64, :, :])
            k.copy("dve", qpad[1][64:128, :, :], qT[64:128, :, :])
            extra = None
            if (not is_sample) and PREFETCH_MOD1 and "l1" in stages:
                def extra(i):
                    if i % 5 == 0 and i // 5 < 12:
                        mod_block(1, i // 5, wbuf[(i // 5) % 2], pb[2][:, 256:512])
            attention(4, 2, 64, qpad, kT, lambda kt, h: Va[:, kt, h, 0:128], sgT, units, 0.125, 4, True, extra=extra)

        def layer0_group(is_sample):
            xg = xS if is_sample else xP
            cond = 0 if is_sample else 1
            adaln(xg, 8, cond, pre=("S0" if is_sample else "P0"))
            if not is_sample:
                compute_mod(0, blocks=range(8, 12), finish=False)
            mark("adaln" + ("S" if is_sample else "P"))
            dbg_dump("hT" + ("S" if is_sample else "P"), hT[:], [128, 8, 1024], BF16)
            if is_sample:
                hyena_group(cond, LS, 1, d_zposS, d_winS, ffS_piece, finvS_piece)
            else:
                hyena_group(cond, LP, 4, d_zposP, d_winP, ffP_piece, finvP_piece)
            mark("hy" + ("S" if is_sample else "P"))
            layer0_attn(is_sample)
            mark("att" + ("S" if is_sample else "P"))
            dbg_dump("y" + ("S" if is_sample else "P"), ybuf[:], [128, 8, 1024], BF16)
            compute_gate_bc(cond)
            wout_residual(d_evout, xg, 8)
            mark("wout" + ("S" if is_sample else "P"))

        def layer1_group(is_sample):
            xg = xS if is_sample else xP
            cond = 0 if is_sample else 1
            Tq = 512 if is_sample else 1024
            Tk = 1536 if is_sample else 1024
            nkt = Tk // 128
            adaln(xg, 8, cond)
            qT = RV(OFF_Q, [8, Tq], BF16)
            sgT = RV(OFF_SG1, [8, Tq], BF16)
            kT = RV(OFF_K1, [2, Tk], BF16)
            Va = RV(OFF_V1S if is_sample else OFF_V1P, [nkt, 2, 130], BF16)
            k.memset("dve", Va[:, :, :, 128:130], 1.0)
            kst = [ring[0][:, i * 512:(i + 1) * 512].bitcast(F32) for i in range(4)]
            kbf = [ring[1][:, i * 256:(i + 1) * 256] for i in range(4)]
            gkbc = spec[:, 0, :].bitcast(F32)
            k.dma("sp", gkbc, d_gkg.partition_broadcast(128))
            koff = 512 if is_sample else 0
            if is_sample:
                k.dma("sp", ropeC[:], d_ropeG[0])
                k.dma("sp", ropeS[:], d_ropeG[1])
                stg = ring[2][:, 0:1024].rearrange("p (t c) -> p t c", t=4)
                k.dma("pool", stg, d_cgk.rearrange("(t p) c -> p t c", p=128))
                for ct in range(4):
                    bank = pbf(next_pt())
                    for h in range(2):
                        k.transpose(bank[:, h * 128:(h + 1) * 128], stg[:, ct, h * 128:(h + 1) * 128], ident_b[:])
                    k.copy("dve", kT[:, :, ct * 128:(ct + 1) * 128],
                           bank[:, 0:256].rearrange("p (h c) -> p h c", h=2))
                for ct in range(4):
                    k.dma("pool", Va[:, ct, :, 0:128],
                          d_cgv[ct * 128:(ct + 1) * 128, :].rearrange("p (h d) -> p h d", h=2))

            if is_sample:
                sqbs = [RV(8192 + i * 1024, [512], BF16) for i in range(2)]
                rstds = [RV(10240 + i * 2048, [512], F32) for i in range(2)]
            else:
                sqbs = [att_tmp[:, i * 256:(i + 1) * 256].bitcast(BF16) for i in range(2)]
                rstds = [att_tmp[:, 512 + i * 512:1024 + i * 512] for i in range(2)]
            fm_cnt = {"i": 0}

            def fm_norm(ps, n, gcol, dst, rope_col0):
                par = fm_cnt["i"] % 2
                fm_cnt["i"] += 1
                sqb, rstd = sqbs[par], rstds[par]
                k.act(sqb[:, 0:n], ps[:, 0:n], AF.Square)

                def tail():
                    pn = pb[next_pt()]
                    k.mm(pn[:, 0:n], ones_b[:], sqb[:, 0:n])
                    k.act(rstd[:, 0:n], pn[:, 0:n], AF.Ln, bias=epsc[:], scale=1.0 / 128)
                    k.act(rstd[:, 0:n], rstd[:, 0:n], AF.Exp, scale=-0.5)
                    if rope_col0 is None:
                        k.stt("dve", dst, ps[:, 0:n], gcol, rstd[:, 0:n], ALU.mult, ALU.mult)
                if rope_col0 is None:
                    defer(tail)
                else:
                    defer(tail)
                    rope_evac(ps, dst, rope_col0, n, gcol=gcol, post=rstd[:, 0:n])

            qg = vB[:, 15:16]
            kg = vB[:, 16:17]

            def evac_q(cc, tb, ps):
                fm_norm(ps, 512, qg, qT[:, cc, tb * 512:(tb + 1) * 512], (tb * 512) if is_sample else None)

            def evac_k_fm(cc, tb, ps):
                fm_norm(ps, 512, kg, kT[:, cc, koff + tb * 512:koff + (tb + 1) * 512], tb * 512)

            cnt = {"i": 0}

            def evac_k_tm(b, t, ps):
                i = cnt["i"] % 4
                cnt["i"] += 1
                rr = small[:, 16 + 4 * i:20 + 4 * i]
                k.memset("dve", rr[:, 0:2], 0.0)
                for h2 in range(2):
                    k.act(kbf[i][:, h2 * 128:(h2 + 1) * 128], ps[:, h2 * 128:(h2 + 1) * 128], AF.Square,
                          accum_out=rr[:, h2:h2 + 1])

                def tail2(i=i, t=t):
                    bank = pbf(next_pt())
                    for h2 in range(2):
                        k.transpose(bank[:, h2 * 128:(h2 + 1) * 128], kbf[i][:, h2 * 128:(h2 + 1) * 128], ident_b[:])
                    k.copy("dve", kT[:, :, t * 128:(t + 1) * 128], bank[:, 0:256].rearrange("p (h c) -> p h c", h=2))

                def tail1(i=i, t=t, ps=ps, rr=rr):
                    k.act(rr[:, 2:4], rr[:, 0:2], AF.Ln, bias=epsc[:], scale=1.0 / 128)
                    k.act(rr[:, 2:4], rr[:, 2:4], AF.Exp, scale=-0.5)
                    for h2 in range(2):
                        k.stt("dve", kst[i][:, h2 * 128:(h2 + 1) * 128], ps[:, h2 * 128:(h2 + 1) * 128],
                              rr[:, 2 + h2:3 + h2], gkbc, ALU.mult, ALU.mult)
                    k.dma("sp", o_ngk[t * 128:(t + 1) * 128, :], kst[i])
                    k.copy("dve", kbf[i], kst[i])
                    defer(tail2, delay=2)
                defer(tail1, delay=1)

            def evac_v(b, t, ps):
                if not is_sample:
                    i = cnt["i"] % 4
                    cnt["i"] += 1
                    k.copy("act", kst[i], ps[:, 0:256])
                    k.dma("sp", o_ngv[t * 128:(t + 1) * 128, :], kst[i])
                vt = t + (4 if is_sample else 0)
                k.copy("dve", Va[:, vt, :, 0:128], ps[:, 0:256].rearrange("p (h c) -> p h c", h=2))

            def evac_g(cc, tb, ps):
                k.act(sgT[:, cc, tb * 512:(tb + 1) * 512], ps[:], AF.Silu)

            sfx1 = "S" if is_sample else "P"
            mark("l1pre" + sfx1)
            set_pp([0, 1, 4, 5, 6, 7])
            proj_fm(d_odin, 0, 4, Tq, evac_q)
            mark("l1q" + sfx1)
            if is_sample:
                proj_fm(d_odin, 1024, 1, 1024, evac_k_fm)
            else:
                proj_tm(d_odin, 1024, 1, 8, evac_k_tm)
            proj_tm(d_odin, 1280, 1, 8, evac_v)
            mark("l1kv" + sfx1)
            proj_fm(d_odin, 1536, 4, Tq, evac_g)
            mark("l1g" + sfx1)
            set_pp([0, 1])
            if is_sample:
                units = [(0, 512, [(kt, kt * 128) for kt in range(12)], 0)]
            else:
                units = [(s_ * 256, 256, [(2 * s_ + i, (2 * s_ + i) * 128) for i in range(2)], s_ * 256)
                         for s_ in range(4)]
            dbg_dump("q1" + ("S" if is_sample else "P"), region[:, OFF_Q // 2:OFF_Q // 2 + 8 * Tq], [128, 8 * Tq], BF16)
            dbg_dump("k1" + ("S" if is_sample else "P"), region[:, OFF_K1 // 2:OFF_K1 // 2 + 2 * Tk], [128, 2 * Tk], BF16)

            class KV:
                pass
            attention_gqa(qT, kT, Va, sgT, units)
            mark("l1att" + sfx1)
            compute_gate_bc(cond)
            wout_residual(d_odout, xg, Tq // 128)
            mark("l1wout" + sfx1)

        def attention_gqa(qT, kT, Va, sgT, units):
            class KTv:
                def __getitem__(self, idx):
                    p, h, c = idx
                    return kT[p, h // 4, c]
            attention(8, 1, 128, qT, KTv(), lambda kt, h: Va[:, kt, h // 4, 0:128], sgT, units,
                      128.0 ** -0.5, 0, False)

        def final_norm(xg, ntiles, o_y):
            fg = gate_bc
            k.dma("sp", fg[:], d_fg.partition_broadcast(128))
            junk = [RV(i * 2048, [1024], BF16) for i in range(2)]
            k.memset("dve", ss[:], 0.0)
            for t in range(ntiles):
                k.act(junk[t % 2], xg[:, t, :], AF.Square, accum_out=ss[:, t:t + 1])
            k.act(rs[:, 0:ntiles], ss[:, 0:ntiles], AF.Ln, bias=epsc[:], scale=1.0 / D)
            k.act(rs[:, 0:ntiles], rs[:, 0:ntiles], AF.Exp, scale=-0.5)
            for t in range(ntiles):
                k.stt("dve", xg[:, t, :], xg[:, t, :], rs[:, t:t + 1], fg[:], ALU.mult, ALU.mult)
                k.dma("sp", o_y[t * 128:(t + 1) * 128, :], xg[:, t, :])

        try:
            mark("setup")
            if "l0" in stages:
                adaln_stats(xP, 8, "P0", 18432)
                for t in range(8):
                    k.dma("sp", xS[:, t, :], d_xs[t * 128:(t + 1) * 128, :])
                compute_mod(0, blocks=range(8))
                adaln_stats(xS, 8, "S0", 20480)
                mark("mod0")
                layer0_group(False)
                layer0_group(True)
            dbg_dump("x1P", xP[:], [128, 8, 1024])
            dbg_dump("x1S", xS[:], [128, 8, 1024])
            if "l1" in stages:
                if PREFETCH_MOD1 and "l0" in stages:
                    mod_finish(1)
                else:
                    compute_mod(1)
                cur["modT"], cur["gsT"] = modTs[1], gsTs[1]
                mark("mod1")
                layer1_group(False)
                mark("l1P")
                layer1_group(True)
            if "final" in stages:
                final_norm(xP, 8, o_yp)
                final_norm(xS, 4, o_ys)
        except _Stop:
            pass

        S.emit(st)
    build_program.last_sched = S
    return nc, dbg_outs


_bf = ml_dtypes.bfloat16


def _rope_tab(pos, head_dim):
    half = head_dim // 2
    inv = (np.float32(10000.0) ** (-np.arange(0, half, 2, dtype=np.float32) / np.float32(half))).astype(np.float32)
    row = (pos // 64).astype(np.float32)
    col = (pos % 64).astype(np.float32)
    ang = np.concatenate([row[:, None] * inv, col[:, None] * inv], axis=-1).astype(np.float32)
    cos, sin = np.cos(ang), np.sin(ang)
    d = np.arange(128) % head_dim
    i = d // 2
    C = cos[:, i].T
    Sg = sin[:, i].T * np.where(d % 2 == 0, -1.0, 1.0)[:, None]
    return np.stack([C, Sg]).astype(np.float32)


def _dft(L, pos):
    f = np.arange(L)
    theta = 2 * np.pi * (f + 0.5) / (2 * L)
    r = np.arange(2 * L)
    rt = r // 128
    j = rt // 2
    part = rt % 2
    fr = j * 128 + r % 128
    th = theta[fr]
    A = np.outer(pos.astype(np.float64), th)
    Ff = np.where(part[None, :] == 0, np.cos(A), -np.sin(A))
    Finv = np.where(part[:, None] == 0, np.cos(A.T), -np.sin(A.T)) / L
    return Ff, Finv


def _dft_half(L, tokpos):
    nj = L // 256
    kt_n = L // 128
    f = np.arange(L // 2)
    theta = 2 * np.pi * (f + 0.5) / (2 * L)
    A = np.outer(tokpos.astype(np.float64), theta)
    C = np.cos(A)
    Sn = -np.sin(A)
    Ff = np.concatenate([C.reshape(kt_n, 128, nj, 128), Sn.reshape(kt_n, 128, nj, 128)], axis=-1)
    FfH = np.ascontiguousarray(Ff.transpose(2, 1, 0, 3))
    Ci = (C / L).T
    Si = (Sn / L).T
    sgn = np.where(np.arange(L) < L // 2, 1.0, -1.0)[None, :]
    parts = [Ci, Si, Ci * sgn, -Si * sgn]
    Fi = np.stack([p_.reshape(nj, 128, kt_n, 128) for p_ in parts], axis=1)
    FinvH = np.ascontiguousarray(Fi.transpose(3, 2, 0, 1, 4).reshape(kt_n, 128, 4 * nj, 128))
    return FfH, FinvH


def _zpos(L, pos):
    t = np.linspace(0.0, 1.0, L, dtype=np.float32)[pos]
    tidx = pos.astype(np.float32)
    bands = np.linspace(1e-4, 15.0, 16, dtype=np.float32)
    ang = (np.float32(2.0 * math.pi) * tidx[:, None] * bands[None, :] / np.float32(L)).astype(np.float32)
    z = np.concatenate([t[:, None], np.cos(ang), -np.sin(ang)], axis=-1).astype(np.float32)
    return np.ascontiguousarray(z.T)


def _window(L, pos):
    t = np.linspace(0.0, 1.0, L, dtype=np.float32)[pos]
    mx = math.log(1e-2) / 0.3
    mn = math.log(1e-2) / 1.5
    deltas = np.abs(np.linspace(mn, mx, 512, dtype=np.float32))
    w = np.exp(-t[:, None] * deltas[None, :]).astype(np.float32)
    wb = w.copy()
    wb[pos == 0] = 0.0
    return np.stack([w, wb]).astype(np.float32)


def _core_consts(h):
    posS = (np.arange(LS) + h * 512) % LS
    posP = np.arange(LP)
    hyS = np.concatenate([posS[0::2], posS[1::2]])
    hyP = posP
    FfS, FinvS = _dft_half(LS, hyS)
    FfP, FinvP = _dft(LP, posP)
    c = {}
    c["FfS"] = FfS.astype(_bf)
    c["FinvS"] = FinvS.astype(_bf)
    c["FfP"] = np.ascontiguousarray(FfP.reshape(2, 128, 512).transpose(1, 0, 2)).astype(_bf)
    c["FinvP"] = np.ascontiguousarray(FinvP.reshape(4, 128, 256).transpose(1, 0, 2)).astype(_bf)
    c["ropeD"] = _rope_tab(posS, 64)
    c["ropeG"] = _rope_tab(posS, 128)
    c["zposP"] = _zpos(LP, hyP)
    c["zposS"] = _zpos(LS, hyS)
    c["winP"] = _window(LP, hyP)
    c["winS"] = _window(LS, hyS)
    a = np.zeros((128, 2), np.float32)
    a[:, 0] = float(h)
    a[:, 1] = 1.0 - float(h)
    c["acol"] = a
    c["ident_bf"] = np.eye(128, dtype=np.float32).astype(_bf)
    c["ident_f"] = np.eye(128, dtype=np.float32)
    pm = np.zeros((128, 128), np.float32)
    pm[np.arange(128) ^ 1, np.arange(128)] = 1.0
    c["perm_bf"] = pm.astype(_bf)
    return c


def make_in_maps(inputs):
    f = lambda a: np.ascontiguousarray(np.asarray(a, dtype=np.float32))
    I = {kk: f(v) for kk, v in inputs.items()}
    vecA = np.zeros((128, 128), np.float32)
    vecA[0:16] = I["norm_g"].reshape(16, 128)
    vecA[16:24] = I["final_g"].reshape(8, 128)
    vecA[24:72] = I["b_mod"].reshape(48, 128)
    vecA[80:88] = I["c_ctx"].reshape(8, 128)
    vecA[88:124] = I["hy_conv_w"][0].reshape(36, 128)
    vecB = np.zeros((128, 128), np.float32)
    vecB[0:12] = I["hy_conv_b"][0].reshape(12, 128)
    vecB[12, 0:64] = I["hy_b1"][0]
    vecB[13, 0:64] = I["hy_freq"][0]
    vecB[14, 0:64] = I["hy_b2"][0]
    vecB[15] = I["gq_q_g"][0]
    vecB[16] = I["gq_k_g"][0]
    vecB[19] = I["df_subln_g"][0]
    consts = [_core_consts(0), _core_consts(1)]
    shared = {
        "w_mod": I["w_mod"], "ev_w_in": I["ev_w_in"][0], "ev_w_out": I["ev_w_out"][0],
        "od_w_in": I["od_w_in"][0], "od_w_out": I["od_w_out"][0],
        "hy_w1": I["hy_w1"][0], "hy_w2": I["hy_w2"][0], "hy_w3": I["hy_w3"][0],
        "hy_skip": I["hy_skip"][0], "final_g": I["final_g"], "df_lambda": I["df_lambda"][0].reshape(1, 256),
        "gq_k_g": I["gq_k_g"][0], "vecB": vecB,
    }
    maps = []
    for c in range(8):
        b, h = c // 2, c % 2
        m = dict(shared)
        m.update(consts[h])
        va = vecA.copy()
        va[72:80] = I["c"][b].reshape(8, 128)
        m["vecA"] = va
        m["xp"] = np.ascontiguousarray(I["x_prompt"][4 * c:4 * c + 4].reshape(NPS * LP, D))
        m["xs"] = np.ascontiguousarray(np.roll(I["x_sample"][b], -h * 512, axis=0))
        m["cdk"] = np.ascontiguousarray(I["cache_diff_k"][b, 0].reshape(PAST, 512))
        m["cdv"] = np.ascontiguousarray(I["cache_diff_v"][b, 0].reshape(PAST, 512))
        m["cgk"] = np.ascontiguousarray(I["cache_gqa_k"][b, 0].reshape(PAST, 256))
        m["cgv"] = np.ascontiguousarray(I["cache_gqa_v"][b, 0].reshape(PAST, 256))
        maps.append(m)
    return maps


def assemble(results):
    yp = np.zeros((32, 256, D), np.float32)
    ys = np.zeros((4, 1024, D), np.float32)
    ndk = np.zeros((32, 1, 256, 4, 2, 64), np.float32)
    ndv = np.zeros((32, 1, 256, 4, 128), np.float32)
    ngk = np.zeros((32, 1, 256, 2, 128), np.float32)
    ngv = np.zeros((32, 1, 256, 2, 128), np.float32)
    for c, r in enumerate(results):
        b, h = c // 2, c % 2
        yp[4 * c:4 * c + 4] = np.asarray(r["y_p"]).reshape(4, 256, D)
        ys[b, h * 512:(h + 1) * 512] = np.asarray(r["y_s"])
        ndk[4 * c:4 * c + 4, 0] = np.asarray(r["ndk"]).reshape(4, 256, 4, 2, 64)
        ndv[4 * c:4 * c + 4, 0] = np.asarray(r["ndv"]).reshape(4, 256, 4, 128)
        ngk[4 * c:4 * c + 4, 0] = np.asarray(r["ngk"]).reshape(4, 256, 2, 128)
        ngv[4 * c:4 * c + 4, 0] = np.asarray(r["ngv"]).reshape(4, 256, 2, 128)
    return (yp, ys, ndk, ndv, ngk, ngv)


def kernel(**inputs):
    nc, _ = build_program()
    maps = make_in_maps(inputs)
    res = run_bass_kernel_spmd(nc, maps, core_ids=list(range(8)))
    return assemble(res.results)
```

```python
import math
from contextlib import ExitStack

import numpy as np
import ml_dtypes

import concourse.bass as bass
import concourse.mybir as mybir
from concourse.bass_utils import run_bass_kernel_spmd

F32 = mybir.dt.float32
BF16 = mybir.dt.bfloat16
ALU = mybir.AluOpType
AF = mybir.ActivationFunctionType
AX = mybir.AxisListType

_DTSIZE = {F32: 4, BF16: 2}
ENGS = ("pe", "act", "dve", "pool", "sp")
SKIP_SAME_ENGINE_WAR = False
PREFETCH_MOD1 = True


def _region(ap):
    t = ap.tensor
    name = t.name
    esz = _DTSIZE.get(ap.dtype, 4)
    dims = list(ap.ap)
    off = int(ap.offset)
    space = str(ap.space)
    if space == "PSUM":
        pstep = dims[0][0] if dims[0][0] else 1
        foff = off % pstep
        lo = hi = foff
        for st, cnt in dims[1:]:
            if cnt > 1:
                if st > 0:
                    hi += st * (cnt - 1)
                else:
                    lo += st * (cnt - 1)
        return (name, 0, 128, (lo * esz // 2048) * 2048, (hi * esz // 2048 + 1) * 2048)
    if space == "SB":
        pstep, pcnt = dims[0]
        if pstep == 0:
            row = int(np.prod(t.shape[1:]))
            p0 = off // row
            foff = off % row
            pcnt = 1
        else:
            p0 = off // pstep
            foff = off % pstep
        lo = hi = foff
        for st, cnt in dims[1:]:
            if cnt > 1:
                if st > 0:
                    hi += st * (cnt - 1)
                else:
                    lo += st * (cnt - 1)
        return (name, p0, p0 + pcnt, lo * esz, (hi + 1) * esz)
    lo = hi = off
    for st, cnt in dims:
        if cnt > 1:
            if st > 0:
                hi += st * (cnt - 1)
            else:
                lo += st * (cnt - 1)
    return (name, 0, 1, lo * esz, (hi + 1) * esz)


def _ovl(a, b):
    return a[1] < b[2] and b[1] < a[2] and a[3] < b[4] and b[3] < a[4]


def _covers(a, b):
    return a[1] <= b[1] and a[2] >= b[2] and a[3] <= b[3] and a[4] >= b[4]


class Op:
    __slots__ = ("eng", "fn", "idx", "deps", "signal", "semval", "is_dma", "dsem", "dval", "tag", "phase")

    def __init__(self, eng, fn, is_dma, tag=""):
        self.eng = eng
        self.fn = fn
        self.is_dma = is_dma
        self.deps = {}
        self.signal = False
        self.semval = None
        self.dsem = None
        self.dval = None
        self.tag = tag


class Sched:
    def __init__(self, nc, n_dma_sems=16):
        self.nc = nc
        self.ops = {e: [] for e in ENGS}
        self.writers = {}
        self.readers = {}
        self.n_dma_sems = n_dma_sems
        self.ro = set()
        self.phase = "setup"

    def _add_dep(self, op, d, raw=True):
        if d is op:
            return
        if SKIP_SAME_ENGINE_WAR and (not raw) and (not d.is_dma) and (not op.is_dma) and d.eng == op.eng:
            return
        if d.is_dma:
            op.deps[("dma", id(d))] = d
        else:
            cur = op.deps.get(d.eng)
            if cur is None or cur.idx < d.idx:
                op.deps[d.eng] = d

    def record(self, eng, fn, reads, writes, is_dma=False, tag=""):
        op = Op(eng, fn, is_dma, tag)
        op.phase = self.phase
        op.idx = len(self.ops[eng])
        rregs = [_region(a) for a in reads if a is not None and not isinstance(a, (int, float))]
        wregs = [_region(a) for a in writes if a is not None]
        for r in rregs:
            if r[0].startswith("pball") and r not in wregs:
                wregs.append(r)
        for r in rregs:
            if r[0] in self.ro:
                continue
            for wr, wop in self.writers.get(r[0], ()):
                if _ovl(wr, r):
                    self._add_dep(op, wop)
        for w in wregs:
            wl = self.writers.setdefault(w[0], [])
            for wr, wop in wl:
                if _ovl(wr, w):
                    self._add_dep(op, wop, raw=False)
            rl = self.readers.setdefault(w[0], [])
            for rr, rops in rl:
                if _ovl(rr, w):
                    for rop in rops.values():
                        self._add_dep(op, rop, raw=False)
        for w in wregs:
            wl = self.writers[w[0]]
            wl[:] = [e for e in wl if not _covers(w, e[0])]
            wl.append([w, op])
            rl = self.readers[w[0]]
            rl[:] = [e for e in rl if not _covers(w, e[0])]
        for r in rregs:
            if r[0] in self.ro:
                continue
            rl = self.readers.setdefault(r[0], [])
            key = ("dma", id(op)) if is_dma else eng
            for e in rl:
                if e[0] == r:
                    e[1][key] = op
                    break
            else:
                rl.append([r, {key: op}])
        for d in op.deps.values():
            d.signal = True
        self.ops[eng].append(op)
        return op

    def emit(self, stack):
        nc = self.nc
        sems = {e: stack.enter_context(nc.semaphore("s_" + e)) for e in ENGS}
        dsems = {e: [stack.enter_context(nc.semaphore(f"d_{e}_{i}")) for i in range(self.n_dma_sems)]
                 for e in ("sp", "pool", "act")}
        all_dma = []
        for e in ENGS:
            c = 0
            dcount = [0] * self.n_dma_sems
            kk = 0
            for op in self.ops[e]:
                if op.is_dma:
                    j = kk % self.n_dma_sems
                    kk += 1
                    dcount[j] += 16
                    op.dsem = dsems[e][j]
                    op.dval = dcount[j]
                    all_dma.append(op)
                elif op.signal:
                    c += 1
                    op.semval = c
        block = stack.enter_context(nc.Block())
        eng_obj = {"pe": "tensor", "act": "scalar", "dve": "vector", "pool": "gpsimd", "sp": "sync"}

        def make(e):
            def body(engine):
                waited = {}

                def wait(sem, val):
                    if waited.get(sem.num, 0) >= val:
                        return
                    waited[sem.num] = val
                    engine.wait_ge(sem, val)

                for op in self.ops[e]:
                    for d in op.deps.values():
                        if d.is_dma:
                            wait(d.dsem, d.dval)
                        else:
                            if d.eng == e and e == "pe":
                                continue
                            wait(sems[d.eng], d.semval)
                    if op.is_dma and op.dval > 16:
                        wait(op.dsem, op.dval - 16)
                    ins = op.fn(engine)
                    if op.is_dma:
                        ins.then_inc(op.dsem, 16)
                    elif op.signal:
                        ins.then_inc(sems[e], 1)
                if e == "sp":
                    last = {}
                    for op in all_dma:
                        last[op.dsem.num] = (op.dsem, op.dval)
                    for sem, val in last.values():
                        engine.wait_ge(sem, val)
            return body

        for e in ENGS:
            getattr(block, eng_obj[e])(make(e))


class K:
    def __init__(self, nc, sched):
        self.nc = nc
        self.s = sched

    def mm(self, out, lhsT, rhs, start=True, stop=True, sgc=False, tag="mm"):
        if sgc:
            fn = lambda e: e.matmul(out, lhsT, rhs, start=start, stop=stop, skip_group_check=True)
        else:
            fn = lambda e: e.matmul(out, lhsT, rhs, start=start, stop=stop)
        return self.s.record("pe", fn, [lhsT, rhs] + ([] if start else [out]), [out], tag=tag)

    def transpose(self, out, in_, ident, tag="tr"):
        return self.s.record("pe", lambda e: e.transpose(out, in_, ident), [in_, ident], [out], tag=tag)

    def act(self, out, in_, func, bias=None, scale=None, accum_out=None, tag="act"):
        kw = {}
        if bias is not None:
            kw["bias"] = bias
        if scale is not None:
            kw["scale"] = scale
        if accum_out is not None:
            kw["accum_out"] = accum_out
        rd = [in_] + [a for a in (bias, scale) if a is not None and not isinstance(a, (int, float))]
        wr = [out] + ([accum_out] if accum_out is not None else [])
        return self.s.record("act", lambda e: e.activation(out, in_, func, **kw), rd, wr, tag=tag)

    def tt(self, eng, out, in0, in1, op, tag="tt"):
        return self.s.record(eng, lambda e: e.tensor_tensor(out, in0, in1, op), [in0, in1], [out], tag=tag)

    def ts(self, eng, out, in0, s1, s2, op0, op1=None, tag="ts"):
        rd = [in0] + [a for a in (s1, s2) if a is not None and not isinstance(a, (int, float))]
        if op1 is None:
            return self.s.record(eng, lambda e: e.tensor_scalar(out, in0, s1, None, op0), rd, [out], tag=tag)
        return self.s.record(eng, lambda e: e.tensor_scalar(out, in0, s1, s2, op0, op1), rd, [out], tag=tag)

    def stt(self, eng, out, in0, scalar, in1, op0, op1, tag="stt"):
        rd = [in0, in1] + ([scalar] if not isinstance(scalar, (int, float)) else [])
        return self.s.record(eng, lambda e: e.scalar_tensor_tensor(out, in0, scalar, in1, op0, op1),
                             rd, [out], tag=tag)

    def copy(self, eng, out, in_, tag="copy"):
        if eng == "act":
            return self.s.record(eng, lambda e: e.copy(out, in_), [in_], [out], tag=tag)
        return self.s.record(eng, lambda e: e.tensor_copy(out, in_), [in_], [out], tag=tag)

    def memset(self, eng, out, val, tag="memset"):
        return self.s.record(eng, lambda e: e.memset(out, val), [], [out], tag=tag)

    def recip(self, out, in_, tag="recip"):
        return self.s.record("dve", lambda e: e.reciprocal(out, in_), [in_], [out], tag=tag)

    def rsum(self, out, in_, tag="rsum"):
        return self.s.record("dve", lambda e: e.reduce_sum(out, in_, AX.X), [in_], [out], tag=tag)

    def dma(self, q, out, in_, tag="dma"):
        return self.s.record(q, lambda e: e.dma_start(out=out, in_=in_), [in_], [out], is_dma=True, tag=tag)


D = 1024
NPS = 4
LP = 256
LS = 1024
PAST = 512
EPS = 1e-6
MAGIC = 12582912.0
TWO_PI = 2.0 * math.pi
LAM_INIT0 = 0.8 - 0.6 * math.exp(-0.3 * 0)

STAGES = ("l0", "l1", "final")
import os
EXPERIMENT = os.environ.get("KEXP", "")


class _Stop(Exception):
    pass


def build_program(stages=STAGES, dbg=None, stop_at=None):
    nc = bass.Bass("TRN2", target_bir_lowering=False)

    def din(name, shape, dt=F32):
        return nc.dram_tensor(name, list(shape), dt, kind="ExternalInput").ap()

    def dout(name, shape, dt=F32):
        return nc.dram_tensor(name, list(shape), dt, kind="ExternalOutput").ap()

    d_xp = din("xp", [NPS * LP, D])
    d_xs = din("xs", [LS, D])
    d_cdk = din("cdk", [PAST, 512])
    d_cdv = din("cdv", [PAST, 512])
    d_cgk = din("cgk", [PAST, 256])
    d_cgv = din("cgv", [PAST, 256])
    d_vecA = din("vecA", [128, 128])
    d_vecB = din("vecB", [128, 128])
    d_wmod = din("w_mod", [2, D, 3 * D])
    d_evin = din("ev_w_in", [D, 4096])
    d_evout = din("ev_w_out", [D, D])
    d_odin = din("od_w_in", [D, 2560])
    d_odout = din("od_w_out", [D, D])
    d_w1 = din("hy_w1", [33, 64])
    d_w2 = din("hy_w2", [64, 64])
    d_w3 = din("hy_w3", [64, 2048])
    d_skip = din("hy_skip", [2, 512])
    d_fg = din("final_g", [D])
    d_lam = din("df_lambda", [1, 256])
    d_gkg = din("gq_k_g", [128])
    d_identb = din("ident_bf", [128, 128], BF16)
    d_identf = din("ident_f", [128, 128])
    d_perm = din("perm_bf", [128, 128], BF16)
    d_ropeD = din("ropeD", [2, 128, LS])
    d_ropeG = din("ropeG", [2, 128, LS])
    d_zposP = din("zposP", [33, LP])
    d_zposS = din("zposS", [33, LS])
    d_winP = din("winP", [2, LP, 512])
    d_winS = din("winS", [2, LS, 512])
    d_FfP = din("FfP", [128, 2, 512], BF16)
    d_FinvP = din("FinvP", [128, 4, 256], BF16)
    d_FfS = din("FfS", [4, 128, 8, 256], BF16)
    d_FinvS = din("FinvS", [8, 128, 16, 128], BF16)
    d_acol = din("acol", [128, 2])

    o_yp = dout("y_p", [NPS * LP, D])
    o_ys = dout("y_s", [LS // 2, D])
    o_ndk = dout("ndk", [NPS * LP, 512])
    o_ndv = dout("ndv", [NPS * LP, 512])
    o_ngk = dout("ngk", [NPS * LP, 256])
    o_ngv = dout("ngv", [NPS * LP, 256])

    S = Sched(nc)
    k = K(nc, S)
    for a in (d_xp, d_xs, d_cdk, d_cdv, d_cgk, d_cgv, d_vecA, d_vecB, d_wmod, d_evin, d_evout, d_odin, d_odout,
              d_w1, d_w2, d_w3, d_skip, d_fg, d_lam, d_gkg, d_identb, d_identf, d_perm, d_ropeD, d_ropeG,
              d_zposP, d_zposS, d_winP, d_winS, d_FfP, d_FinvP, d_FfS, d_FinvS, d_acol):
        S.ro.add(a.tensor.name)

    dbg_outs = {}

    with ExitStack() as st:
        def sb(name, shape, dt):
            return st.enter_context(nc.sbuf_tensor("sb_" + name, list(shape), dt))

        xP = sb("xP", [128, 8, D], F32)
        xS = sb("xS", [128, 8, D], F32)
        hT = sb("hT", [128, 8, 1024], BF16)
        ybuf = sb("ybuf", [128, 8, 1024], BF16)
        wbuf = [sb(f"wbuf{i}", [128, 8, 256], BF16) for i in range(2)]
        REG_BYTES = 58 * 1024
        region = sb("region", [128, REG_BYTES // 2], BF16)
        ring = [sb(f"ring{i}", [128, 2048], BF16) for i in range(3)]
        spec = sb("spec", [128, 16, 256], BF16)
        gate_bc = sb("gate_bc", [128, D], F32)
        ident_b = sb("ident_b", [128, 128], BF16)
        ident_f = sb("ident_f", [128, 128], F32)
        perm_b = sb("perm_b", [128, 128], BF16)
        ones_f = sb("ones_f", [128, 128], F32)
        ones_b = sb("ones_b", [128, 128], BF16)
        epsc = sb("epsc", [128, 1], F32)
        acol = sb("acol", [128, 2], F32)
        vA = sb("vA", [128, 128], F32)
        vB = sb("vB", [128, 128], F32)
        vrow = sb("vrow", [128, 128], F32)
        sc = sb("sc", [128, 8, 2], BF16)
        modTs = [sb(f"modT{l}", [128, 24, 2], F32) for l in range(2)]
        gsTs = [sb(f"gsT{l}", [128, 2, 8], F32) for l in range(2)]
        fgb = sb("fgb", [128, D], F32)
        cur = {"modT": modTs[0], "gsT": gsTs[0]}
        diag = [sb(f"diag{i}", [128, 128], F32) for i in range(2)]
        ss = sb("ss", [128, 8], F32)
        rs = sb("rs", [128, 8], F32)
        FfP = sb("FfP", [128, 2, 512], BF16)
        FinvP = sb("FinvP", [128, 4, 256], BF16)
        w1s = sb("w1s", [33, 64], F32)
        w2s = sb("w2s", [64, 64], F32)
        w3b = sb("w3b", [64, 2048], BF16)
        hid2b = sb("hid2b", [64, 1024], BF16)
        lamrow = sb("lamrow", [1, 256], F32)
        lamt = sb("lamt", [1, 8], F32)
        neglam = sb("neglam", [128, 1], F32)
        subg = sb("subg", [128, 1], F32)
        small = sb("small", [128, 64], F32)

        pball = st.enter_context(nc.psum_tensor("pball", [128, 4096], F32))
        pb = [pball[:, i * 512:(i + 1) * 512] for i in range(8)]

        def pbf(i):
            return pb[i].bitcast(BF16)

        def RV(off, shape, dt):
            n = int(np.prod(shape))
            esz = 2 if dt == BF16 else 4
            assert off % 4 == 0 and off + n * esz <= REG_BYTES, (off, shape)
            v = region[:, off // 2:(off + n * esz) // 2]
            if dt == F32:
                v = v.bitcast(F32)
            if len(shape) == 2:
                v = v.rearrange("p (a b) -> p a b", a=shape[0])
            elif len(shape) == 3:
                v = v.rearrange("p (a b c) -> p a b c", a=shape[0], b=shape[1])
            return v

        def dbg_dump(name, ap, shape, dt=F32):
            if dbg is not None and name in dbg:
                o = dout("dbg_" + name, shape, dt)
                k.dma("sp", o, ap)
                dbg_outs[name] = o

        rot = {"p": 0, "t": 0}

        def mark(label):
            S.phase = "after_" + label
            if stop_at is not None and label == stop_at:
                raise _Stop()

        pp_pool = {"banks": [0, 1]}

        def set_pp(banks):
            pp_pool["banks"] = list(banks)

        def next_pp():
            rot["p"] = (rot["p"] + 1) % len(pp_pool["banks"])
            return pb[pp_pool["banks"][rot["p"]]]

        pt_fixed = {"on": False}

        def next_pt():
            if pt_fixed["on"]:
                return 3
            rot["t"] ^= 1
            return 2 + rot["t"]

        k.dma("sp", ident_b[:], d_identb)
        k.dma("sp", ident_f[:], d_identf)
        k.dma("sp", perm_b[:], d_perm)
        k.dma("sp", acol[:], d_acol)
        k.dma("sp", vrow[:], d_vecA)
        k.memset("dve", ones_f[:], 1.0)
        k.memset("dve", ones_b[:], 1.0)
        k.memset("dve", epsc[:], EPS)
        k.transpose(pb[2][:, 0:128], vrow[:], ident_f[:])
        k.copy("dve", vA[:], pb[2][:, 0:128])
        k.dma("sp", vrow[:], d_vecB)
        k.transpose(pb[3][:, 0:128], vrow[:], ident_f[:])
        k.copy("dve", vB[:], pb[3][:, 0:128])
        k.dma("sp", FfP[:], d_FfP)
        k.dma("sp", FinvP[:], d_FinvP)
        k.dma("sp", w1s[:], d_w1)
        k.dma("sp", w2s[:], d_w2)
        k.dma("pool", w3b[:], d_w3)
        for t in range(8):
            k.dma("sp", xP[:, t, :], d_xp[t * 128:(t + 1) * 128, :])
        k.act(sc[:, :, 0], vA[:, 72:80], AF.Silu)
        k.act(sc[:, :, 1], vA[:, 80:88], AF.Silu)
        k.dma("sp", lamrow[:], d_lam)
        k.tt("dve", lamrow[:, 0:64], lamrow[:, 0:64], lamrow[:, 64:128], ALU.mult)
        k.tt("dve", lamrow[:, 128:192], lamrow[:, 128:192], lamrow[:, 192:256], ALU.mult)
        k.rsum(lamt[:, 0:1], lamrow[:, 0:64])
        k.rsum(lamt[:, 1:2], lamrow[:, 128:192])
        k.act(lamt[:, 2:4], lamt[:, 0:2], AF.Exp)
        k.tt("dve", lamt[:, 4:5], lamt[:, 3:4], lamt[:, 2:3], ALU.subtract)
        k.ts("dve", lamt[:, 5:6], lamt[:, 4:5], -LAM_INIT0, None, ALU.add)
        k.mm(pb[2][:, 0:1], ones_f[0:1, :], lamt[0:1, 5:6])
        k.copy("dve", neglam[:], pb[2][:, 0:1])
        k.ts("dve", subg[:], vB[:, 19:20], 1.0 - LAM_INIT0, None, ALU.mult)

        def wblock_dummy():
            pass

        def wblock(w2d, c0, ncols, buf):
            k.dma("pool", buf[:, :, 0:ncols],
                  w2d[:, c0:c0 + ncols].rearrange("(kk p) c -> p kk c", p=128))

        wrot = {"i": 0}

        def next_wbuf():
            wrot["i"] ^= 1
            return wbuf[wrot["i"]]

        def mod_block(l, blk, wb, pm):
            modT = modTs[l]
            wblock(d_wmod[l], blk * 256, 256, wb)
            for cc in range(2):
                for kc in range(8):
                    k.mm(pm[:, cc * 2:cc * 2 + 2], wb[:, kc, cc * 128:(cc + 1) * 128], sc[:, kc, :],
                         start=(kc == 0), stop=(kc == 7))
            c0 = blk * 2
            pm3 = pm[:, 0:4].rearrange("p (c n) -> p c n", n=2)
            for cond in range(2):
                k.tt("dve", modT[:, c0:c0 + 2, cond], pm3[:, :, cond],
                     vA[:, 24 + 24 * l + c0:24 + 24 * l + c0 + 2], ALU.add)

        def mod_finish(l):
            for cond in range(2):
                k.stt("dve", gsTs[l][:, cond, :], modTs[l][:, 8:16, cond], 1.0, vA[:, 8 * l:8 * l + 8],
                      ALU.add, ALU.mult)

        def compute_mod(l, blocks=range(12), finish=True):
            for blk in blocks:
                mod_block(l, blk, next_wbuf(), pb[2])
            if finish:
                mod_finish(l)

        def compute_gate_bc(cond):
            for j in range(8):
                dg = diag[j % 2]
                k.ts("dve", dg[:], ident_f[:], cur["modT"][:, 16 + j, cond:cond + 1], None, ALU.mult)
                k.mm(pb[4 + j // 4][:, (j % 4) * 128:(j % 4 + 1) * 128], ones_f[:], dg[:])
            k.copy("dve", gate_bc[:, 0:512], pb[4][:])
            k.copy("dve", gate_bc[:, 512:1024], pb[5][:])

        pre_rs = {}

        def adaln_stats(xg, ntiles, key, junk_off):
            ss_ = sb("ss_" + key, [128, 8], F32)
            rs_ = sb("rs_" + key, [128, 8], F32)
            junk = RV(junk_off, [1024], BF16)
            k.memset("dve", ss_[:], 0.0)
            for t in range(ntiles):
                k.act(junk, xg[:, t, :], AF.Square, accum_out=ss_[:, t:t + 1])
            k.act(rs_[:, 0:ntiles], ss_[:, 0:ntiles], AF.Ln, bias=epsc[:], scale=1.0 / D)
            k.act(rs_[:, 0:ntiles], rs_[:, 0:ntiles], AF.Exp, scale=-0.5)
            pre_rs[key] = rs_

        def adaln(xg, ntiles, cond, pre=None):
            xn = [RV(i * 2048, [1024], BF16) for i in range(8)]
            if pre is not None and pre in pre_rs:
                rs_u = pre_rs[pre]
            else:
                rs_u = rs
                k.memset("dve", ss[:], 0.0)
                for t in range(ntiles):
                    k.act(xn[t], xg[:, t, :], AF.Square, accum_out=ss[:, t:t + 1])
                k.act(rs[:, 0:ntiles], ss[:, 0:ntiles], AF.Ln, bias=epsc[:], scale=1.0 / D)
                k.act(rs[:, 0:ntiles], rs[:, 0:ntiles], AF.Exp, scale=-0.5)
            for t in range(ntiles):
                k.ts("dve", xn[t], xg[:, t, :], rs_u[:, t:t + 1], None, ALU.mult)
            for half in range(ntiles // 4):
                for j in range(8):
                    bank = pbf(next_pt())
                    for tt_ in range(4):
                        k.transpose(bank[:, tt_ * 128:(tt_ + 1) * 128],
                                    xn[half * 4 + tt_][:, j * 128:(j + 1) * 128], ident_b[:])
                    if j % 2 == 0:
                        k.act(hT[:, j, half * 512:(half + 1) * 512], bank[:, 0:512], AF.Identity,
                              bias=cur["modT"][:, j, cond:cond + 1], scale=cur["gsT"][:, cond, j:j + 1])
                    else:
                        k.ts("dve", hT[:, j, half * 512:(half + 1) * 512], bank[:, 0:512],
                             cur["gsT"][:, cond, j:j + 1], cur["modT"][:, j, cond:cond + 1], ALU.mult, ALU.add)

        deferred = []

        def defer(fn, delay=1):
            deferred.append([delay, fn])

        def run_deferred(flush=False):
            for e_ in deferred:
                e_[0] -= 1
            while deferred and (flush or deferred[0][0] <= 0):
                deferred.pop(0)[1]()

        def proj_fm(w2d, c0, nblk, ntok, evac):
            for b in range(nblk):
                wb = next_wbuf()
                wblock(w2d, c0 + b * 256, 256, wb)
                for cc in range(2):
                    for tb in range(ntok // 512):
                        ps = next_pp()
                        for kc in range(8):
                            k.mm(ps[:], wb[:, kc, cc * 128:(cc + 1) * 128], hT[:, kc, tb * 512:(tb + 1) * 512],
                                 start=(kc == 0), stop=(kc == 7))
                        run_deferred()
                        evac(b * 2 + cc, tb, ps)
            run_deferred(flush=True)

        def proj_tm(w2d, c0, nblk, ntiles, evac):
            for b in range(nblk):
                wb = next_wbuf()
                wblock(w2d, c0 + b * 256, 256, wb)
                for t in range(ntiles):
                    ps = next_pp()
                    for kc in range(8):
                        k.mm(ps[:, 0:256], hT[:, kc, t * 128:(t + 1) * 128], wb[:, kc, :],
                             start=(kc == 0), stop=(kc == 7))
                    run_deferred()
                    evac(b, t, ps)
            run_deferred(flush=True)

        def wout_residual(w2d, xg, ntiles):
            tmp = [RV(i * 1024, [256], F32) for i in range(4)]
            set_pp([0, 1, 4, 5, 6, 7])
            for b in range(4):
                wb = next_wbuf()
                wblock(w2d, b * 256, 256, wb)
                for t in range(ntiles):
                    ps = next_pp()
                    for kc in range(8):
                        k.mm(ps[:, 0:256], ybuf[:, kc, t * 128:(t + 1) * 128], wb[:, kc, :],
                             start=(kc == 0), stop=(kc == 7))
                    tm = tmp[t % 4]
                    k.tt("dve", tm, ps[:, 0:256], gate_bc[:, b * 256:(b + 1) * 256], ALU.mult)
                    k.tt("dve", xg[:, t, b * 256:(b + 1) * 256], xg[:, t, b * 256:(b + 1) * 256], tm, ALU.add)
            set_pp([0, 1])

        def sin_rr(out, ps_in, bcol, fcol, tmp_a, tmp_b, npart):
            k.ts("dve", tmp_a, ps_in, bcol, fcol, ALU.add, ALU.mult)
            k.ts("dve", tmp_b, tmp_a, 1.0 / TWO_PI, MAGIC, ALU.mult, ALU.add)
            k.ts("dve", tmp_b, tmp_b, MAGIC, None, ALU.subtract)
            k.stt("dve", tmp_a, tmp_b, -TWO_PI, tmp_a, ALU.mult, ALU.add)
            k.act(out, tmp_a, AF.Sin)

        OFF_ZT = 0
        OFF_G1 = 8192
        OFF_GG = 16384
        OFF_Y = 24576
        OFF_RAW = 32768
        OFF_T = 41472
        OFF_SKIP = 51712

        def hyena_group(xg_cond, L, nseq, d_zpos, d_win, ff_piece, finv_piece):
            kt_n = L // 128
            rt_n = 2 * kt_n
            nsl = nseq * 256
            seglen = 256 if nseq == 4 else 512
            nseg = 1024 // seglen
            zT = RV(OFF_ZT, [kt_n, 2, nsl], BF16)
            g1T = RV(OFF_G1, [kt_n, 2, nsl], BF16)
            gg = RV(OFF_GG, [4, 1024], BF16)
            Y = RV(OFF_Y, [rt_n, nsl], BF16)
            fa = RV(OFF_RAW, [kt_n, 256], BF16)
            fb = RV(OFF_RAW + 4096, [kt_n, 256], BF16)
            raw = [RV(OFF_RAW + i * 4352, [nseg, seglen + 2], F32) for i in range(2)]
            skipbc = RV(OFF_SKIP, [2, 512], F32)
            U = RV(OFF_T, [nseg, seglen], F32)
            ubfs = [RV(OFF_T + 4096, [1024], BF16), RV(OFF_T + 8192, [1024], BF16)]
            sgt = [RV(OFF_T + 6144 + i * 1024, [512], BF16) for i in range(2)]
            zp = RV(OFF_T, [1024], F32)
            ha = RV(OFF_T + 4096, [512], F32)
            hb = RV(OFF_T + 6144, [512], F32)
            h1 = RV(OFF_RAW, [1024], F32)
            winf = [RV(OFF_T + i * 2048, [256], F32) for i in range(2)]
            winb = [RV(OFF_T + 2048 + i * 2048, [256], F32) for i in range(2)]
            winf = [RV(OFF_T + 0, [256], F32), RV(OFF_T + 2048, [256], F32)]
            winb = [RV(OFF_T + 1024, [256], F32), RV(OFF_T + 3072, [256], F32)]
            ft1 = RV(OFF_T + 4096, [256], F32)
            ft2 = RV(OFF_T + 5120, [256], F32)
            xr = [RV(OFF_T + i * 512, [256], BF16) for i in range(2)]
            xi = [RV(OFF_T + 1024 + i * 512, [256], BF16) for i in range(2)]
            t1 = RV(OFF_T + 2048, [256], BF16)
            t2 = RV(OFF_T + 2560, [256], BF16)
            t3 = RV(OFF_T + 3072, [256], BF16)
            t4 = RV(OFF_T + 3584, [256], BF16)
            ytm = [RV(OFF_T + 8192 + i * 1024, [512], BF16) for i in range(2)]

            for o in range(2):
                k.dma("sp", skipbc[:, o, :], d_skip[o].partition_broadcast(128))

            k.dma("sp", zp[0:33, 0:L], d_zpos)
            nb = max(1, L // 512)
            bw = min(L, 512)
            for tb in range(nb):
                ps = next_pp()
                k.mm(ps[0:64, 0:bw], w1s[0:33, :], zp[0:33, tb * bw:(tb + 1) * bw])
                sin_rr(h1[0:64, tb * bw:(tb + 1) * bw], ps[0:64, 0:bw], vB[0:64, 12:13], vB[0:64, 13:14],
                       ha[0:64, 0:bw], hb[0:64, 0:bw], 64)
            for tb in range(nb):
                ps = next_pp()
                k.mm(ps[0:64, 0:bw], w2s[0:64, :], h1[0:64, tb * bw:(tb + 1) * bw])
                sin_rr(hid2b[0:64, tb * bw:(tb + 1) * bw], ps[0:64, 0:bw], vB[0:64, 14:15], vB[0:64, 13:14],
                       ha[0:64, 0:bw], hb[0:64, 0:bw], 64)

            mark("filt" + ("P" if nseq == 4 else "S"))
            for i in range(2):
                k.memset("dve", raw[i], 0.0)
            conv_state = {}

            def evacA(cc, tb, ps):
                if cc < 12:
                    rw = raw[cc % 2]
                    nsb = 512 // seglen
                    k.copy("act", rw[:, tb * nsb:(tb + 1) * nsb, 1:1 + seglen],
                           ps[:].rearrange("p (s t) -> p s t", s=nsb))
                    if tb == 1:
                        if nseq == 1:
                            a_ = acol[:, 0:1]
                            na_ = acol[:, 1:2]
                            k.ts("dve", rw[:, 0, 0:1], rw[:, 1, 512:513], a_, None, ALU.mult)
                            k.ts("dve", rw[:, 0, 513:514], rw[:, 1, 1:2], na_, None, ALU.mult)
                            k.ts("dve", rw[:, 1, 0:1], rw[:, 0, 512:513], na_, None, ALU.mult)
                            k.ts("dve", rw[:, 1, 513:514], rw[:, 0, 1:2], a_, None, ALU.mult)
                        w0 = vA[:, 88 + cc:89 + cc]
                        w1_ = vA[:, 100 + cc:101 + cc]
                        w2_ = vA[:, 112 + cc:113 + cc]
                        k.act(U, rw[:, :, 1:1 + seglen], AF.Identity, bias=vB[:, cc:cc + 1], scale=w1_)
                        k.stt("dve", U, rw[:, :, 0:seglen], w0, U, ALU.mult, ALU.add)
                        ubf = ubfs[cc % 2]
                        if cc < 8:
                            dst = ubf.rearrange("p (s t) -> p s t", s=nseg)
                        else:
                            dst = gg[:, cc - 8, :].rearrange("p (s t) -> p s t", s=nseg)
                        k.stt("dve", dst, rw[:, :, 2:2 + seglen], w2_, U, ALU.mult, ALU.add)
                        if cc < 8:
                            def tail(cc=cc, ubf=ubf):
                                bank = pbf(next_pt())
                                uv = ubf.rearrange("p (s t two) -> p s t two", s=nseq, two=2)
                                khh = kt_n // 2
                                for t in range(8):
                                    if nseq == 1:
                                        s_i, k_i = t // kt_n, t % kt_n
                                        par_, kk_i = k_i // khh, k_i % khh
                                        k.transpose(bank[:, t * 128:(t + 1) * 128],
                                                    uv[:, s_i, kk_i * 128:(kk_i + 1) * 128, par_], ident_b[:])
                                    else:
                                        k.transpose(bank[:, t * 128:(t + 1) * 128], ubf[:, t * 128:(t + 1) * 128],
                                                    ident_b[:])
                                dstT = zT if cc < 4 else g1T
                                c4 = cc % 4
                                hc, off = c4 // 2, (c4 % 2) * 128
                                src = bank[:, 0:1024].rearrange("p (s kk c) -> p kk s c", s=nseq, kk=kt_n)
                                dd = dstT[:, :, hc, :].rearrange("p kk (s c) -> p kk s c", s=nseq)[:, :, :,
                                                                                                   off:off + 128]
                                k.copy("dve", dd, src)
                            defer(tail, delay=2)
                else:
                    sg = sgt[tb]
                    k.act(sg, ps[:], AF.Silu)
                    k.tt("dve", gg[:, cc - 12, tb * 512:(tb + 1) * 512], gg[:, cc - 12, tb * 512:(tb + 1) * 512],
                         sg, ALU.mult)

            set_pp([0, 1, 4, 5, 6, 7])
            proj_fm(d_evin, 0, 8, 1024, evacA)
            set_pp([0, 1])
            dbg_dump("zT" + ("P" if nseq == 4 else "S"), region[:, OFF_ZT // 2:OFF_ZT // 2 + 4096], [128, 4096], BF16)
            dbg_dump("gg" + ("P" if nseq == 4 else "S"), region[:, OFF_GG // 2:OFF_GG // 2 + 4096], [128, 4096], BF16)

            mark("projA" + ("P" if nseq == 4 else "S"))
            sgn = 2 if nseq == 4 else 1
            ngrp = nseq // sgn
            ncol = sgn * 256
            ft1s = [RV(OFF_T + 4096, [256], F32), RV(OFF_T + 6144, [256], F32)]
            ft2s = [RV(OFF_T + 5120, [256], F32), RV(OFF_T + 7168, [256], F32)]

            def emit_taps(o, hc, kt2):
                pss_ = []
                for u in range(2):
                    kt = kt2 + u
                    ps = next_pp()
                    k.mm(ps[:, 0:256], hid2b[0:64, kt * 128:(kt + 1) * 128],
                         w3b[0:64, (o * 2) * 512 + hc * 256:(o * 2) * 512 + hc * 256 + 256])
                    k.mm(ps[:, 256:512], hid2b[0:64, kt * 128:(kt + 1) * 128],
                         w3b[0:64, (o * 2 + 1) * 512 + hc * 256:(o * 2 + 1) * 512 + hc * 256 + 256])
                    k.dma("sp", winf[u], d_win[0, kt * 128:(kt + 1) * 128, hc * 256:(hc + 1) * 256])
                    k.dma("sp", winb[u], d_win[1, kt * 128:(kt + 1) * 128, hc * 256:(hc + 1) * 256])
                    pss_.append(ps)
                for u in range(2):
                    k.tt("dve", ft1s[u], pss_[u][:, 0:256], winf[u], ALU.mult)
                for u in range(2):
                    k.tt("dve", ft2s[u], pss_[u][:, 256:512], winb[u], ALU.mult)
                for u in range(2):
                    k.tt("dve", fa[:, kt2 + u, :], ft1s[u], ft2s[u], ALU.add)
                for u in range(2):
                    k.tt("dve", fb[:, kt2 + u, :], ft1s[u], ft2s[u], ALU.subtract)

            eo = (nseq == 1)
            if eo:
                kh = kt_n // 2
                nj = kt_n // 2
                tb_ = [RV(OFF_T + i * 512, [256], BF16) for i in range(14)]
                Ao_sb, Bo_sb, Xc, Xs, Xcm, Xsm, p1, p2, p3, p4, p5, p6, p7, p8 = tb_
                xrot = {"i": 0}
                combos = [(o, hc) for o in range(2) for hc in range(2)]

                def Ytile(ci, r):
                    if ci % 2 == 0:
                        return Y[:, r, 0:256]
                    return wbuf[r // 8][:, r % 8, :]

                def stage_A(ci, j):
                    o, hc = combos[ci]
                    pc = ff_piece(j)
                    pa = next_pp()
                    for kk in range(kh):
                        k.mm(pa[:, 0:256], pc[:, kk, 0:128], fa[:, kk, :], start=(kk == 0), stop=(kk == kh - 1))
                    for kk in range(kh):
                        k.mm(pa[:, 256:512], pc[:, kh + kk, 0:128], fa[:, kh + kk, :],
                             start=(kk == 0), stop=(kk == kh - 1))
                    pbk = next_pp()
                    for kk in range(kh):
                        k.mm(pbk[:, 0:256], pc[:, kk, 128:256], fb[:, kk, :], start=(kk == 0), stop=(kk == kh - 1))
                    for kk in range(kh):
                        k.mm(pbk[:, 256:512], pc[:, kh + kk, 128:256], fb[:, kh + kk, :],
                             start=(kk == 0), stop=(kk == kh - 1))
                    k.copy("act", Ao_sb, pa[:, 256:512])
                    k.copy("act", Bo_sb, pbk[:, 256:512])
                    k.tt("dve", p1, pa[:, 0:256], skipbc[:, o, hc * 256:(hc + 1) * 256], ALU.add)
                    k.tt("dve", spec[:, 4 * j + 1, :], pbk[:, 0:256], Bo_sb, ALU.add)
                    k.tt("dve", spec[:, 4 * j + 3, :], Bo_sb, pbk[:, 0:256], ALU.subtract)
                    k.tt("dve", spec[:, 4 * j + 0, :], p1, Ao_sb, ALU.add)
                    k.tt("dve", spec[:, 4 * j + 2, :], p1, Ao_sb, ALU.subtract)
                    Kc, Ks, Kcm, Ksm = (spec[:, 4 * j + q_, :] for q_ in range(4))
                    xb_ = xrot["i"] % 2
                    xrot["i"] += 1
                    bA, bB = pb[4 + xb_ * 2], pb[5 + xb_ * 2]
                    zc = slice(0, 256)
                    for kk in range(kh):
                        k.mm(bA[:, 0:256], pc[:, kk, 0:128], zT[:, kk, hc, zc], start=(kk == 0), stop=(kk == kh - 1))
                    for kk in range(kh):
                        k.mm(bA[:, 256:512], pc[:, kh + kk, 0:128], zT[:, kh + kk, hc, zc],
                             start=(kk == 0), stop=(kk == kh - 1))
                    for kk in range(kh):
                        k.mm(bB[:, 0:256], pc[:, kk, 128:256], zT[:, kk, hc, zc], start=(kk == 0), stop=(kk == kh - 1))
                    for kk in range(kh):
                        k.mm(bB[:, 256:512], pc[:, kh + kk, 128:256], zT[:, kh + kk, hc, zc],
                             start=(kk == 0), stop=(kk == kh - 1))
                    k.copy("act", Ao_sb, bA[:, 256:512])
                    k.copy("act", Bo_sb, bB[:, 256:512])
                    k.tt("dve", Xc, bA[:, 0:256], Ao_sb, ALU.add)
                    k.tt("dve", Xs, bB[:, 0:256], Bo_sb, ALU.add)
                    k.tt("dve", Xcm, bA[:, 0:256], Ao_sb, ALU.subtract)
                    k.tt("dve", Xsm, Bo_sb, bB[:, 0:256], ALU.subtract)
                    k.tt("dve", p1, Xc, Kc, ALU.mult)
                    k.tt("dve", p2, Xs, Ks, ALU.mult)
                    k.tt("dve", p3, Xc, Ks, ALU.mult)
                    k.tt("dve", p4, Xs, Kc, ALU.mult)
                    k.tt("dve", p5, Xcm, Kcm, ALU.mult)
                    k.tt("dve", p6, Xsm, Ksm, ALU.mult)
                    k.tt("dve", p7, Xcm, Ksm, ALU.mult)
                    k.tt("dve", p8, Xsm, Kcm, ALU.mult)
                    k.tt("dve", Ytile(ci, 4 * j + 0), p1, p2, ALU.subtract)
                    k.tt("dve", Ytile(ci, 4 * j + 1), p3, p4, ALU.add)
                    k.tt("dve", Ytile(ci, 4 * j + 2), p5, p6, ALU.subtract)
                    k.tt("dve", Ytile(ci, 4 * j + 3), p7, p8, ALU.add)

                def stage_B(ci, i):
                    o, hc = combos[ci]
                    pc = finv_piece(i)
                    par, kk_ = i // kh, i % kh
                    ps = next_pp()
                    nmm = 4 * nj
                    for jf in range(nj):
                        for part in range(4):
                            idx = jf * 4 + part
                            k.mm(ps[:, 0:256], pc[:, idx, :], Ytile(ci, 4 * jf + part),
                                 start=(idx == 0), stop=(idx == nmm - 1))
                    cs = slice(0, 256)
                    if o == 0:
                        k.tt("dve", zT[:, i, hc, cs], ps[:, 0:256], g1T[:, i, hc, cs], ALU.mult)
                    else:
                        yt = ytm[i % 2]
                        k.copy("act", yt[:, 0:256], ps[:, 0:256])
                        bank = pbf(next_pt())
                        for c2 in range(2):
                            k.transpose(bank[:, c2 * 128:(c2 + 1) * 128], yt[:, c2 * 128:(c2 + 1) * 128], ident_b[:])
                        for c2 in range(2):
                            ch = hc * 2 + c2
                            yv = ybuf[:, ch, :].rearrange("p (s t two) -> p s t two", s=nseq, two=2)
                            gv = gg[:, ch, :].rearrange("p (s t two) -> p s t two", s=nseq, two=2)
                            k.tt("dve", yv[:, 0, kk_ * 128:(kk_ + 1) * 128, par], bank[:, c2 * 128:(c2 + 1) * 128],
                                 gv[:, 0, kk_ * 128:(kk_ + 1) * 128, par], ALU.mult)

                ncomb = len(combos)
                set_pp([0, 1, 2])
                pt_fixed["on"] = True
                for kt2 in range(0, kt_n, 2):
                    emit_taps(combos[0][0], combos[0][1], kt2)
                for j in range(nj):
                    stage_A(0, j)
                for ci in range(ncomb):
                    if ci + 1 < ncomb:
                        no_, nhc_ = combos[ci + 1]
                        taps_l = list(range(0, kt_n, 2))
                        assert len(taps_l) == kt_n // 2 and nj == kt_n // 2
                        for t_ in range(kt_n // 2):
                            emit_taps(no_, nhc_, taps_l[t_])
                            stage_B(ci, t_)
                        for j in range(nj):
                            stage_A(ci + 1, j)
                            stage_B(ci, kt_n // 2 + j)
                    else:
                        for i in range(kt_n):
                            stage_B(ci, i)
                set_pp([0, 1])
                pt_fixed["on"] = False

            else:
                combos = [(o, hc) for o in range(2) for hc in range(2)]
                for kt2 in range(0, kt_n, 2):
                    emit_taps(combos[0][0], combos[0][1], kt2)
                for ci, (o, hc) in enumerate(combos):
                    if True:
                        nxt = combos[ci + 1] if ci + 1 < len(combos) else None
                        pairs_left = list(range(0, kt_n, 2)) if nxt is not None else []
                        npairs = kt_n // 2
                        for j in range(kt_n):
                            pc = ff_piece(j)
                            psc = next_pp()
                            for kk in range(kt_n):
                                k.mm(psc[:, 0:256], pc[:, kk, 0:128], fa[:, kk, :], start=(kk == 0), stop=(kk == kt_n - 1))
                            pss = next_pp()
                            for kk in range(kt_n):
                                k.mm(pss[:, 0:256], pc[:, kk, 128:256], fb[:, kk, :], start=(kk == 0), stop=(kk == kt_n - 1))
                            k.tt("dve", spec[:, 2 * j, :], psc[:, 0:256], skipbc[:, o, hc * 256:(hc + 1) * 256], ALU.add)
                            k.copy("act", spec[:, 2 * j + 1, :], pss[:, 0:256])
                            for g in range(ngrp):
                                xb_ = (g if ngrp > 1 else j) % 2
                                xc = pb[4 + xb_ * 2]
                                xs_ = pb[5 + xb_ * 2]
                                for kk in range(kt_n):
                                    k.mm(xc[:, 0:ncol], pc[:, kk, 0:128], zT[:, kk, hc, g * ncol:(g + 1) * ncol],
                                         start=(kk == 0), stop=(kk == kt_n - 1))
                                for kk in range(kt_n):
                                    k.mm(xs_[:, 0:ncol], pc[:, kk, 128:256], zT[:, kk, hc, g * ncol:(g + 1) * ncol],
                                         start=(kk == 0), stop=(kk == kt_n - 1))
                                for s_ in range(sgn):
                                    sq = g * sgn + s_
                                    a_, b_ = xr[s_ % 2], xi[s_ % 2]
                                    k.copy("act", a_, xc[:, s_ * 256:(s_ + 1) * 256])
                                    k.copy("act", b_, xs_[:, s_ * 256:(s_ + 1) * 256])
                                    Kr, Ki = spec[:, 2 * j, :], spec[:, 2 * j + 1, :]
                                    k.tt("dve", t1, a_, Kr, ALU.mult)
                                    k.tt("dve", t2, b_, Ki, ALU.mult)
                                    k.tt("dve", t3, a_, Ki, ALU.mult)
                                    k.tt("dve", t4, b_, Kr, ALU.mult)
                                    k.tt("dve", Y[:, 2 * j, sq * 256:(sq + 1) * 256], t1, t2, ALU.subtract)
                                    k.tt("dve", Y[:, 2 * j + 1, sq * 256:(sq + 1) * 256], t3, t4, ALU.add)
                        for i in range(kt_n):
                            if pairs_left and ((i + 1) * npairs) // kt_n > npairs - len(pairs_left):
                                emit_taps(nxt[0], nxt[1], pairs_left.pop(0))
                            pc = finv_piece(i)
                            for g in range(ngrp):
                                ps = next_pp()
                                for r in range(rt_n):
                                    k.mm(ps[:, 0:ncol], pc[:, r, :], Y[:, r, g * ncol:(g + 1) * ncol],
                                         start=(r == 0), stop=(r == rt_n - 1))
                                if o == 0:
                                    k.tt("dve", zT[:, i, hc, g * ncol:(g + 1) * ncol], ps[:, 0:ncol],
                                         g1T[:, i, hc, g * ncol:(g + 1) * ncol], ALU.mult)
                                else:
                                    yt = ytm[(i * ngrp + g) % 2]
                                    k.copy("act", yt[:, 0:ncol], ps[:, 0:ncol])
                                    bank = pbf(next_pt())
                                    for s_ in range(sgn):
                                        for c2 in range(2):
                                            slot = s_ * 2 + c2
                                            k.transpose(bank[:, slot * 128:(slot + 1) * 128],
                                                        yt[:, s_ * 256 + c2 * 128:s_ * 256 + (c2 + 1) * 128], ident_b[:])
                                    for s_ in range(sgn):
                                        sq = g * sgn + s_
                                        tok0 = sq * L + i * 128
                                        for c2 in range(2):
                                            slot = s_ * 2 + c2
                                            ch = hc * 2 + c2
                                            k.tt("dve", ybuf[:, ch, tok0:tok0 + 128], bank[:, slot * 128:(slot + 1) * 128],
                                                 gg[:, ch, tok0:tok0 + 128], ALU.mult)
                        if o == 0 and hc == 0:
                            dbg_dump("z2" + ("P" if nseq == 4 else "S"), region[:, OFF_ZT // 2:OFF_ZT // 2 + 4096], [128, 4096], BF16)

        ring_rot = {"i": 0}

        def ffS_piece(j):
            r = ring[ring_rot["i"] % 3]
            ring_rot["i"] += 1
            v = r[:, 0:2048].rearrange("p (kk c) -> p kk c", kk=8)
            k.dma("sp", v, d_FfS[j])
            return v

        def finvS_piece(i):
            r = ring[ring_rot["i"] % 3]
            ring_rot["i"] += 1
            v = r[:, 0:2048].rearrange("p (r c) -> p r c", r=16)
            k.dma("sp", v, d_FinvS[i])
            return v

        def ffP_piece(j):
            return FfP[:, :, j * 256:(j + 1) * 256]

        def finvP_piece(i):
            return FinvP[:, :, i * 128:(i + 1) * 128]

        OFF_Q = 0
        OFF_SG0 = 8192
        OFF_K0 = 16384
        OFF_V0 = 28672
        OFF_SG1 = 16384
        OFF_K1 = 32768
        OFF_V1P = 36864
        OFF_V1S = 24576
        ropeC = RV(41216, [LS], F32)
        ropeS = RV(45312, [LS], F32)
        att_tmp = RV(49408, [2048], F32)

        def attention(nheads, ncomp, dk, qT, kT, Vaug, sgT, units, scale, ych0, finalize_diff, extra=None):
            Eb = att_tmp[:].bitcast(BF16)
            E = [Eb[:, i * 512:(i + 1) * 512] for i in range(8)]
            fin_sq = RV(57600, [512], BF16)
            pair = all(u_[1] <= 256 for u_ in units)

            steps = []
            for (q0, nq, ktl, tok0) in units:
                for h in range(nheads):
                    hh = (h % 2) if pair else 0
                    for ki, (ktile, kc0) in enumerate(ktl):
                        for m in range(ncomp):
                            steps.append((q0, nq, tok0, h, m, ki, ktile, kc0, ki == 0, ki == len(ktl) - 1, hh))
            n = len(steps)

            def banks(h, m):
                if ncomp == 2:
                    g = m
                else:
                    g = ((h // 2) % 2) if pair else (h % 2)
                return pb[4 + 2 * g], pb[5 + 2 * g]

            def issue_S(i):
                q0, nq, tok0, h, m, ki, ktile, kc0, first, last, hh = steps[i]
                ps = pb[i % 3]
                if ncomp == 2:
                    k.mm(ps[:, 0:nq], kT[:, h, kc0:kc0 + 128], qT[m][:, h, q0:q0 + nq])
                else:
                    k.mm(ps[:, 0:nq], kT[0:128, h, kc0:kc0 + 128], qT[0:128, h, q0:q0 + nq])
                k.act(E[i % 8][:, 0:nq], ps[:, 0:nq], AF.Exp, scale=scale)

            def finalize_stages(st_):
                q0, nq, tok0, h, m, ki, ktile, kc0, first, last, hh = st_
                W = 2 * nq if pair else nq
                A, B = ropeC[:, 0:W], ropeC[:, 512:512 + W]
                C, Dd = ropeS[:, 0:W], ropeS[:, 512:512 + W]
                sq_ = fin_sq[:, 0:W]
                if pair:
                    h0 = h - 1
                    dst = ybuf[:, ych0 + h0:ych0 + h0 + 2, tok0:tok0 + nq]
                    sg_ = sgT[:, h0:h0 + 2, tok0:tok0 + nq]
                    C_ = C.rearrange("p (a c) -> p a c", a=2)
                else:
                    dst = ybuf[:, ych0 + h, tok0:tok0 + nq]
                    sg_ = sgT[:, h, tok0:tok0 + nq]
                    C_ = C
                if ncomp == 2:
                    O0, S0 = banks(h, 0)
                    O1, S1 = banks(h, 1)

                    def s0a():
                        k.copy("dve", C, O0[:, 0:W])
                        k.act(A, S0[:, 0:W], AF.Ln)

                    def s0():
                        k.copy("dve", Dd, O1[:, 0:W])
                        k.act(B, S1[:, 0:W], AF.Ln)

                    def s1():
                        k.act(A, A, AF.Exp, scale=-1.0)
                        k.act(B, B, AF.Exp, scale=-1.0)

                    def s2():
                        k.tt("dve", C, C, A, ALU.mult)
                        k.tt("dve", Dd, Dd, B, ALU.mult)
                        k.stt("dve", C, Dd, neglam[:], C, ALU.mult, ALU.add)
                        k.tt("dve", sq_, C, C, ALU.mult)
                        k.mm(pb[3][:, 0:W], ones_b[:], sq_)

                    def s3():
                        k.act(A, pb[3][:, 0:W], AF.Ln, bias=epsc[:], scale=1.0 / 128)
                        k.act(A, A, AF.Exp, scale=-0.5)

                    def s4():
                        k.tt("dve", C, C, A, ALU.mult)
                        k.stt("dve", dst, C_, subg[:], sg_, ALU.mult, ALU.mult)
                    return [s0a, s0, s1, s2, s3, s4]
                O0, S0 = banks(h, 0)

                def g0():
                    k.copy("dve", C, O0[:, 0:W])
                    k.act(A, S0[:, 0:W], AF.Ln)

                def g1():
                    k.act(A, A, AF.Exp, scale=-1.0)

                def g2():
                    k.tt("dve", C, C, A, ALU.mult)
                    k.tt("dve", dst, C_, sg_, ALU.mult)
                return [g0, g1, g2]

            pending = []
            cur_fin = {}

            def issue_PV(i):
                q0, nq, tok0, h, m, ki, ktile, kc0, first, last, hh = steps[i]
                Ob, Sb = banks(h, m)
                e = E[i % 8]
                c0 = hh * 256
                k.mm(Ob[:, c0:c0 + nq], Vaug(ktile, h), e[:, 0:nq], start=first, stop=last)
                k.mm(Sb[:, c0:c0 + nq], ones_b[:], e[:, 0:nq], start=first, stop=last)
                for p_ in list(pending):
                    p_.pop(0)()
                    if not p_:
                        pending.remove(p_)
                fin_now = last and (hh == 1 or not pair)
                if ncomp == 2:
                    if fin_now and m == 0:
                        cur_fin["s"] = finalize_stages(steps[i])
                        cur_fin["s"].pop(0)()
                    elif fin_now and m == 1:
                        stg_ = cur_fin["s"]
                        stg_.pop(0)()
                        pending.append(stg_)
                elif fin_now:
                    stg_ = finalize_stages(steps[i])
                    stg_.pop(0)()
                    pending.append(stg_)

            LA = 2
            for j in range(min(LA, n)):
                issue_S(j)
            for i in range(n):
                if i + LA < n:
                    issue_S(i + LA)
                issue_PV(i)
                if extra is not None:
                    extra(i)
            while pending:
                for p_ in list(pending):
                    p_.pop(0)()
                    if not p_:
                        pending.remove(p_)

        rope_cnt = {"i": 0}

        def rope_evac(ps, dst, col0, n, gcol=None, post=None):
            par = rope_cnt["i"] % 2
            rope_cnt["i"] += 1
            qb = att_tmp[:, par * 256:(par + 1) * 256].bitcast(BF16)
            rt1 = att_tmp[:, 512 + par * 512:1024 + par * 512]
            rt2 = att_tmp[:, 1536:2048]
            if gcol is None:
                k.copy("act", qb[:, 0:n], ps[:, 0:n])
            else:
                k.ts("dve", qb[:, 0:n], ps[:, 0:n], gcol, None, ALU.mult)

            def tail():
                pq = pb[next_pt()]
                k.mm(pq[:, 0:n], perm_b[:], qb[:, 0:n])
                k.tt("dve", rt1[:, 0:n], qb[:, 0:n], ropeC[:, col0:col0 + n], ALU.mult)
                k.tt("dve", rt2[:, 0:n], pq[:, 0:n], ropeS[:, col0:col0 + n], ALU.mult)
                if post is None:
                    k.tt("dve", dst, rt1[:, 0:n], rt2[:, 0:n], ALU.add)
                else:
                    k.tt("dve", rt1[:, 0:n], rt1[:, 0:n], rt2[:, 0:n], ALU.add)
                    k.tt("dve", dst, rt1[:, 0:n], post, ALU.mult)
            defer(tail)

        def layer0_attn(is_sample):
            Tk = 1536 if is_sample else 1024
            nkt = Tk // 128
            qT = RV(OFF_Q, [4, 1024], BF16)
            sgT = RV(OFF_SG0, [4, 1024], BF16)
            kT = RV(OFF_K0, [4, Tk], BF16)
            Va = RV(OFF_V0, [nkt, 4, 130], BF16)
            k.memset("dve", Va[:, :, :, 128:130], 1.0)
            kst = [ring[0][:, i * 512:(i + 1) * 512].bitcast(F32) for i in range(4)]
            kbf = [ring[1][:, i * 256:(i + 1) * 256] for i in range(4)]
            koff = 512 if is_sample else 0
            if is_sample:
                k.dma("sp", ropeC[:], d_ropeD[0])
                k.dma("sp", ropeS[:], d_ropeD[1])
                stg = ring[2][:, 0:2048].rearrange("p (t c) -> p t c", t=4)
                k.dma("pool", stg, d_cdk.rearrange("(t p) c -> p t c", p=128))
                for ct in range(4):
                    bank = pbf(next_pt())
                    for h in range(4):
                        k.transpose(bank[:, h * 128:(h + 1) * 128], stg[:, ct, h * 128:(h + 1) * 128], ident_b[:])
                    k.copy("dve", kT[:, :, ct * 128:(ct + 1) * 128],
                           bank[:, 0:512].rearrange("p (h c) -> p h c", h=4))
                for ct in range(4):
                    k.dma("pool", Va[:, ct, :, 0:128],
                          d_cdv[ct * 128:(ct + 1) * 128, :].rearrange("p (h d) -> p h d", h=4))

            def evac_q(cc, tb, ps):
                if is_sample:
                    rope_evac(ps, qT[:, cc, tb * 512:(tb + 1) * 512], tb * 512, 512)
                else:
                    k.copy("act", qT[:, cc, tb * 512:(tb + 1) * 512], ps[:])

            def evac_k_fm(cc, tb, ps):
                rope_evac(ps, kT[:, cc, koff + tb * 512:koff + (tb + 1) * 512], tb * 512, 512)

            cnt = {"i": 0}

            def evac_k_tm(b, t, ps):
                i = cnt["i"] % 4
                cnt["i"] += 1
                if EXPERIMENT == "B":
                    return
                k.copy("act", kst[i], ps[:, 0:256])
                if EXPERIMENT != "A":
                    k.dma("sp", o_ndk[t * 128:(t + 1) * 128, b * 256:(b + 1) * 256], kst[i])
                k.copy("dve", kbf[i], ps[:, 0:256])

                def tail(i=i, b=b, t=t):
                    bank = pbf(next_pt())
                    for h2 in range(2):
                        k.transpose(bank[:, h2 * 128:(h2 + 1) * 128], kbf[i][:, h2 * 128:(h2 + 1) * 128], ident_b[:])
                    k.copy("dve", kT[:, 2 * b:2 * b + 2, t * 128:(t + 1) * 128],
                           bank[:, 0:256].rearrange("p (h c) -> p h c", h=2))
                defer(tail, delay=2)

            def evac_v(b, t, ps):
                if not is_sample:
                    i = cnt["i"] % 4
                    cnt["i"] += 1
                    k.copy("act", kst[i], ps[:, 0:256])
                    k.dma("sp", o_ndv[t * 128:(t + 1) * 128, b * 256:(b + 1) * 256], kst[i])
                vt = t + (4 if is_sample else 0)
                k.copy("dve", Va[:, vt, 2 * b:2 * b + 2, 0:128], ps[:, 0:256].rearrange("p (h c) -> p h c", h=2))

            def evac_g(cc, tb, ps):
                k.act(sgT[:, cc, tb * 512:(tb + 1) * 512], ps[:], AF.Silu)

            sfx = "S" if is_sample else "P"
            set_pp([0, 1, 4, 5, 6, 7])
            mark("pre" + sfx)
            proj_fm(d_evin, 2048, 2, 1024, evac_q)
            mark("pq" + sfx)
            if is_sample:
                proj_fm(d_evin, 2560, 2, 1024, evac_k_fm)
            else:
                proj_tm(d_evin, 2560, 2, 8, evac_k_tm)
            mark("pk" + sfx)
            proj_tm(d_evin, 3072, 2, 8, evac_v)
            mark("pv" + sfx)
            proj_fm(d_evin, 3584, 2, 1024, evac_g)
            mark("pg" + sfx)
            set_pp([0, 1])
            if is_sample:
                units = [(qb * 512, 512, [(kt, kt * 128) for kt in range(12)], qb * 512) for qb in range(2)]
            else:
                units = [(s_ * 256, 256, [(2 * s_ + i, (2 * s_ + i) * 128) for i in range(2)], s_ * 256)
                         for s_ in range(4)]
            dbg_dump("qT" + ("S" if is_sample else "P"), region[:, OFF_Q // 2:OFF_Q // 2 + 4096], [128, 4096], BF16)
            dbg_dump("kT" + ("S" if is_sample else "P"), region[:, OFF_K0 // 2:OFF_K0 // 2 + 4 * Tk], [128, 4 * Tk], BF16)
            qpad = [hT[:, 0:4, :], hT[:, 4:8, :]]
            k.memset("dve", qpad[0][64:128, :, :], 0.0)
            k.memset("dve", qpad[1][0:64, :, :], 0.0)
            k.copy("dve", qpad[0][0:64, :, :], qT[0:64, :, :])
            k.copy("dve", qpad[1][64:128, :, :], qT[64:128, :, :])
            extra = None
            if (not is_sample) and PREFETCH_MOD1 and "l1" in stages:
                def extra(i):
                    if i % 5 == 0 and i // 5 < 12:
                        mod_block(1, i // 5, wbuf[(i // 5) % 2], pb[2][:, 256:512])
            attention(4, 2, 64, qpad, kT, lambda kt, h: Va[:, kt, h, 0:128], sgT, units, 0.125, 4, True, extra=extra)

        def layer0_group(is_sample):
            xg = xS if is_sample else xP
            cond = 0 if is_sample else 1
            adaln(xg, 8, cond, pre=("S0" if is_sample else "P0"))
            if not is_sample:
                compute_mod(0, blocks=range(8, 12), finish=False)
            mark("adaln" + ("S" if is_sample else "P"))
            dbg_dump("hT" + ("S" if is_sample else "P"), hT[:], [128, 8, 1024], BF16)
            if is_sample:
                hyena_group(cond, LS, 1, d_zposS, d_winS, ffS_piece, finvS_piece)
            else:
                hyena_group(cond, LP, 4, d_zposP, d_winP, ffP_piece, finvP_piece)
            mark("hy" + ("S" if is_sample else "P"))
            layer0_attn(is_sample)
            mark("att" + ("S" if is_sample else "P"))
            dbg_dump("y" + ("S" if is_sample else "P"), ybuf[:], [128, 8, 1024], BF16)
            compute_gate_bc(cond)
            wout_residual(d_evout, xg, 8)
            mark("wout" + ("S" if is_sample else "P"))

        def layer1_group(is_sample):
            xg = xS if is_sample else xP
            cond = 0 if is_sample else 1
            Tq = 512 if is_sample else 1024
            Tk = 1536 if is_sample else 1024
            nkt = Tk // 128
            adaln(xg, 8, cond)
            qT = RV(OFF_Q, [8, Tq], BF16)
            sgT = RV(OFF_SG1, [8, Tq], BF16)
            kT = RV(OFF_K1, [2, Tk], BF16)
            Va = RV(OFF_V1S if is_sample else OFF_V1P, [nkt, 2, 130], BF16)
            k.memset("dve", Va[:, :, :, 128:130], 1.0)
            kst = [ring[0][:, i * 512:(i + 1) * 512].bitcast(F32) for i in range(4)]
            kbf = [ring[1][:, i * 256:(i + 1) * 256] for i in range(4)]
            gkbc = spec[:, 0, :].bitcast(F32)
            k.dma("sp", gkbc, d_gkg.partition_broadcast(128))
            koff = 512 if is_sample else 0
            if is_sample:
                k.dma("sp", ropeC[:], d_ropeG[0])
                k.dma("sp", ropeS[:], d_ropeG[1])
                stg = ring[2][:, 0:1024].rearrange("p (t c) -> p t c", t=4)
                k.dma("pool", stg, d_cgk.rearrange("(t p) c -> p t c", p=128))
                for ct in range(4):
                    bank = pbf(next_pt())
                    for h in range(2):
                        k.transpose(bank[:, h * 128:(h + 1) * 128], stg[:, ct, h * 128:(h + 1) * 128], ident_b[:])
                    k.copy("dve", kT[:, :, ct * 128:(ct + 1) * 128],
                           bank[:, 0:256].rearrange("p (h c) -> p h c", h=2))
                for ct in range(4):
                    k.dma("pool", Va[:, ct, :, 0:128],
                          d_cgv[ct * 128:(ct + 1) * 128, :].rearrange("p (h d) -> p h d", h=2))

            if is_sample:
                sqbs = [RV(8192 + i * 1024, [512], BF16) for i in range(2)]
                rstds = [RV(10240 + i * 2048, [512], F32) for i in range(2)]
            else:
                sqbs = [att_tmp[:, i * 256:(i + 1) * 256].bitcast(BF16) for i in range(2)]
                rstds = [att_tmp[:, 512 + i * 512:1024 + i * 512] for i in range(2)]
            fm_cnt = {"i": 0}

            def fm_norm(ps, n, gcol, dst, rope_col0):
                par = fm_cnt["i"] % 2
                fm_cnt["i"] += 1
                sqb, rstd = sqbs[par], rstds[par]
                k.act(sqb[:, 0:n], ps[:, 0:n], AF.Square)

                def tail():
                    pn = pb[next_pt()]
                    k.mm(pn[:, 0:n], ones_b[:], sqb[:, 0:n])
                    k.act(rstd[:, 0:n], pn[:, 0:n], AF.Ln, bias=epsc[:], scale=1.0 / 128)
                    k.act(rstd[:, 0:n], rstd[:, 0:n], AF.Exp, scale=-0.5)
                    if rope_col0 is None:
                        k.stt("dve", dst, ps[:, 0:n], gcol, rstd[:, 0:n], ALU.mult, ALU.mult)
                if rope_col0 is None:
                    defer(tail)
                else:
                    defer(tail)
                    rope_evac(ps, dst, rope_col0, n, gcol=gcol, post=rstd[:, 0:n])

            qg = vB[:, 15:16]
            kg = vB[:, 16:17]

            def evac_q(cc, tb, ps):
                fm_norm(ps, 512, qg, qT[:, cc, tb * 512:(tb + 1) * 512], (tb * 512) if is_sample else None)

            def evac_k_fm(cc, tb, ps):
                fm_norm(ps, 512, kg, kT[:, cc, koff + tb * 512:koff + (tb + 1) * 512], tb * 512)

            cnt = {"i": 0}

            def evac_k_tm(b, t, ps):
                i = cnt["i"] % 4
                cnt["i"] += 1
                rr = small[:, 16 + 4 * i:20 + 4 * i]
                k.memset("dve", rr[:, 0:2], 0.0)
                for h2 in range(2):
                    k.act(kbf[i][:, h2 * 128:(h2 + 1) * 128], ps[:, h2 * 128:(h2 + 1) * 128], AF.Square,
                          accum_out=rr[:, h2:h2 + 1])

                def tail2(i=i, t=t):
                    bank = pbf(next_pt())
                    for h2 in range(2):
                        k.transpose(bank[:, h2 * 128:(h2 + 1) * 128], kbf[i][:, h2 * 128:(h2 + 1) * 128], ident_b[:])
                    k.copy("dve", kT[:, :, t * 128:(t + 1) * 128], bank[:, 0:256].rearrange("p (h c) -> p h c", h=2))

                def tail1(i=i, t=t, ps=ps, rr=rr):
                    k.act(rr[:, 2:4], rr[:, 0:2], AF.Ln, bias=epsc[:], scale=1.0 / 128)
                    k.act(rr[:, 2:4], rr[:, 2:4], AF.Exp, scale=-0.5)
                    for h2 in range(2):
                        k.stt("dve", kst[i][:, h2 * 128:(h2 + 1) * 128], ps[:, h2 * 128:(h2 + 1) * 128],
                              rr[:, 2 + h2:3 + h2], gkbc, ALU.mult, ALU.mult)
                    k.dma("sp", o_ngk[t * 128:(t + 1) * 128, :], kst[i])
                    k.copy("dve", kbf[i], kst[i])
                    defer(tail2, delay=2)
                defer(tail1, delay=1)

            def evac_v(b, t, ps):
                if not is_sample:
                    i = cnt["i"] % 4
                    cnt["i"] += 1
                    k.copy("act", kst[i], ps[:, 0:256])
                    k.dma("sp", o_ngv[t * 128:(t + 1) * 128, :], kst[i])
                vt = t + (4 if is_sample else 0)
                k.copy("dve", Va[:, vt, :, 0:128], ps[:, 0:256].rearrange("p (h c) -> p h c", h=2))

            def evac_g(cc, tb, ps):
                k.act(sgT[:, cc, tb * 512:(tb + 1) * 512], ps[:], AF.Silu)

            sfx1 = "S" if is_sample else "P"
            mark("l1pre" + sfx1)
            set_pp([0, 1, 4, 5, 6, 7])
            proj_fm(d_odin, 0, 4, Tq, evac_q)
            mark("l1q" + sfx1)
            if is_sample:
                proj_fm(d_odin, 1024, 1, 1024, evac_k_fm)
            else:
                proj_tm(d_odin, 1024, 1, 8, evac_k_tm)
            proj_tm(d_odin, 1280, 1, 8, evac_v)
            mark("l1kv" + sfx1)
            proj_fm(d_odin, 1536, 4, Tq, evac_g)
            mark("l1g" + sfx1)
            set_pp([0, 1])
            if is_sample:
                units = [(0, 512, [(kt, kt * 128) for kt in range(12)], 0)]
            else:
                units = [(s_ * 256, 256, [(2 * s_ + i, (2 * s_ + i) * 128) for i in range(2)], s_ * 256)
                         for s_ in range(4)]
            dbg_dump("q1" + ("S" if is_sample else "P"), region[:, OFF_Q // 2:OFF_Q // 2 + 8 * Tq], [128, 8 * Tq], BF16)
            dbg_dump("k1" + ("S" if is_sample else "P"), region[:, OFF_K1 // 2:OFF_K1 // 2 + 2 * Tk], [128, 2 * Tk], BF16)

            class KV:
                pass
            attention_gqa(qT, kT, Va, sgT, units)
            mark("l1att" + sfx1)
            compute_gate_bc(cond)
            wout_residual(d_odout, xg, Tq // 128)
            mark("l1wout" + sfx1)

        def attention_gqa(qT, kT, Va, sgT, units):
            class KTv:
                def __getitem__(self, idx):
                    p, h, c = idx
                    return kT[p, h // 4, c]
            attention(8, 1, 128, qT, KTv(), lambda kt, h: Va[:, kt, h // 4, 0:128], sgT, units,
                      128.0 ** -0.5, 0, False)

        fn_state = {"loaded": False}

        def final_norm(xg, ntiles, o_y, key):
            fg = fgb
            if not fn_state["loaded"]:
                k.dma("sp", fg[:], d_fg.partition_broadcast(128))
                fn_state["loaded"] = True
            ss_ = sb("ssf_" + key, [128, 8], F32)
            rs_ = sb("rsf_" + key, [128, 8], F32)
            j0 = 8 if key == "P" else 12
            junk = spec[:, j0:j0 + 4, :].rearrange("p a c -> p (a c)")
            k.memset("dve", ss_[:], 0.0)
            for t in range(ntiles):
                k.act(junk, xg[:, t, :], AF.Square, accum_out=ss_[:, t:t + 1])
            k.act(rs_[:, 0:ntiles], ss_[:, 0:ntiles], AF.Ln, bias=epsc[:], scale=1.0 / D)
            k.act(rs_[:, 0:ntiles], rs_[:, 0:ntiles], AF.Exp, scale=-0.5)
            for t in range(ntiles):
                k.stt("dve", xg[:, t, :], xg[:, t, :], rs_[:, t:t + 1], fg[:], ALU.mult, ALU.mult)
                k.dma("sp", o_y[t * 128:(t + 1) * 128, :], xg[:, t, :])

        try:
            mark("setup")
            if "l0" in stages:
                adaln_stats(xP, 8, "P0", 18432)
                for t in range(8):
                    k.dma("sp", xS[:, t, :], d_xs[t * 128:(t + 1) * 128, :])
                compute_mod(0, blocks=range(8))
                adaln_stats(xS, 8, "S0", 20480)
                mark("mod0")
                layer0_group(False)
                layer0_group(True)
            dbg_dump("x1P", xP[:], [128, 8, 1024])
            dbg_dump("x1S", xS[:], [128, 8, 1024])
            if "l1" in stages:
                if PREFETCH_MOD1 and "l0" in stages:
                    mod_finish(1)
                else:
                    compute_mod(1)
                cur["modT"], cur["gsT"] = modTs[1], gsTs[1]
                mark("mod1")
                layer1_group(False)
                mark("l1P")
                if "final" in stages:
                    final_norm(xP, 8, o_yp, "P")
                layer1_group(True)
            if "final" in stages:
                if "l1" not in stages:
                    final_norm(xP, 8, o_yp, "P")
                final_norm(xS, 4, o_ys, "S")
        except _Stop:
            pass

        S.emit(st)
    build_program.last_sched = S
    return nc, dbg_outs


_bf = ml_dtypes.bfloat16


def _rope_tab(pos, head_dim):
    half = head_dim // 2
    inv = (np.float32(10000.0) ** (-np.arange(0, half, 2, dtype=np.float32) / np.float32(half))).astype(np.float32)
    row = (pos // 64).astype(np.float32)
    col = (pos % 64).astype(np.float32)
    ang = np.concatenate([row[:, None] * inv, col[:, None] * inv], axis=-1).astype(np.float32)
    cos, sin = np.cos(ang), np.sin(ang)
    d = np.arange(128) % head_dim
    i = d // 2
    C = cos[:, i].T
    Sg = sin[:, i].T * np.where(d % 2 == 0, -1.0, 1.0)[:, None]
    return np.stack([C, Sg]).astype(np.float32)


def _dft(L, pos):
    f = np.arange(L)
    theta = 2 * np.pi * (f + 0.5) / (2 * L)
    r = np.arange(2 * L)
    rt = r // 128
    j = rt // 2
    part = rt % 2
    fr = j * 128 + r % 128
    th = theta[fr]
    A = np.outer(pos.astype(np.float64), th)
    Ff = np.where(part[None, :] == 0, np.cos(A), -np.sin(A))
    Finv = np.where(part[:, None] == 0, np.cos(A.T), -np.sin(A.T)) / L
    return Ff, Finv


def _dft_half(L, tokpos):
    nj = L // 256
    kt_n = L // 128
    f = np.arange(L // 2)
    theta = 2 * np.pi * (f + 0.5) / (2 * L)
    A = np.outer(tokpos.astype(np.float64), theta)
    C = np.cos(A)
    Sn = -np.sin(A)
    Ff = np.concatenate([C.reshape(kt_n, 128, nj, 128), Sn.reshape(kt_n, 128, nj, 128)], axis=-1)
    FfH = np.ascontiguousarray(Ff.transpose(2, 1, 0, 3))
    Ci = (C / L).T
    Si = (Sn / L).T
    sgn = np.where(np.arange(L) < L // 2, 1.0, -1.0)[None, :]
    parts = [Ci, Si, Ci * sgn, -Si * sgn]
    Fi = np.stack([p_.reshape(nj, 128, kt_n, 128) for p_ in parts], axis=1)
    FinvH = np.ascontiguousarray(Fi.transpose(3, 2, 0, 1, 4).reshape(kt_n, 128, 4 * nj, 128))
    return FfH, FinvH


def _zpos(L, pos):
    t = np.linspace(0.0, 1.0, L, dtype=np.float32)[pos]
    tidx = pos.astype(np.float32)
    bands = np.linspace(1e-4, 15.0, 16, dtype=np.float32)
    ang = (np.float32(2.0 * math.pi) * tidx[:, None] * bands[None, :] / np.float32(L)).astype(np.float32)
    z = np.concatenate([t[:, None], np.cos(ang), -np.sin(ang)], axis=-1).astype(np.float32)
    return np.ascontiguousarray(z.T)


def _window(L, pos):
    t = np.linspace(0.0, 1.0, L, dtype=np.float32)[pos]
    mx = math.log(1e-2) / 0.3
    mn = math.log(1e-2) / 1.5
    deltas = np.abs(np.linspace(mn, mx, 512, dtype=np.float32))
    w = np.exp(-t[:, None] * deltas[None, :]).astype(np.float32)
    wb = w.copy()
    wb[pos == 0] = 0.0
    return np.stack([w, wb]).astype(np.float32)


def _core_consts(h):
    posS = (np.arange(LS) + h * 512) % LS
    posP = np.arange(LP)
    hyS = np.concatenate([posS[0::2], posS[1::2]])
    hyP = posP
    FfS, FinvS = _dft_half(LS, hyS)
    FfP, FinvP = _dft(LP, posP)
    c = {}
    c["FfS"] = FfS.astype(_bf)
    c["FinvS"] = FinvS.astype(_bf)
    c["FfP"] = np.ascontiguousarray(FfP.reshape(2, 128, 512).transpose(1, 0, 2)).astype(_bf)
    c["FinvP"] = np.ascontiguousarray(FinvP.reshape(4, 128, 256).transpose(1, 0, 2)).astype(_bf)
    c["ropeD"] = _rope_tab(posS, 64)
    c["ropeG"] = _rope_tab(posS, 128)
    c["zposP"] = _zpos(LP, hyP)
    c["zposS"] = _zpos(LS, hyS)
    c["winP"] = _window(LP, hyP)
    c["winS"] = _window(LS, hyS)
    a = np.zeros((128, 2), np.float32)
    a[:, 0] = float(h)
    a[:, 1] = 1.0 - float(h)
    c["acol"] = a
    c["ident_bf"] = np.eye(128, dtype=np.float32).astype(_bf)
    c["ident_f"] = np.eye(128, dtype=np.float32)
    pm = np.zeros((128, 128), np.float32)
    pm[np.arange(128) ^ 1, np.arange(128)] = 1.0
    c["perm_bf"] = pm.astype(_bf)
    return c


def make_in_maps(inputs):
    f = lambda a: np.ascontiguousarray(np.asarray(a, dtype=np.float32))
    I = {kk: f(v) for kk, v in inputs.items()}
    vecA = np.zeros((128, 128), np.float32)
    vecA[0:16] = I["norm_g"].reshape(16, 128)
    vecA[16:24] = I["final_g"].reshape(8, 128)
    vecA[24:72] = I["b_mod"].reshape(48, 128)
    vecA[80:88] = I["c_ctx"].reshape(8, 128)
    vecA[88:124] = I["hy_conv_w"][0].reshape(36, 128)
    vecB = np.zeros((128, 128), np.float32)
    vecB[0:12] = I["hy_conv_b"][0].reshape(12, 128)
    vecB[12, 0:64] = I["hy_b1"][0]
    vecB[13, 0:64] = I["hy_freq"][0]
    vecB[14, 0:64] = I["hy_b2"][0]
    vecB[15] = I["gq_q_g"][0]
    vecB[16] = I["gq_k_g"][0]
    vecB[19] = I["df_subln_g"][0]
    consts = [_core_consts(0), _core_consts(1)]
    shared = {
        "w_mod": I["w_mod"], "ev_w_in": I["ev_w_in"][0], "ev_w_out": I["ev_w_out"][0],
        "od_w_in": I["od_w_in"][0], "od_w_out": I["od_w_out"][0],
        "hy_w1": I["hy_w1"][0], "hy_w2": I["hy_w2"][0], "hy_w3": I["hy_w3"][0],
        "hy_skip": I["hy_skip"][0], "final_g": I["final_g"], "df_lambda": I["df_lambda"][0].reshape(1, 256),
        "gq_k_g": I["gq_k_g"][0], "vecB": vecB,
    }
    maps = []
    for c in range(8):
        b, h = c // 2, c % 2
        m = dict(shared)
        m.update(consts[h])
        va = vecA.copy()
        va[72:80] = I["c"][b].reshape(8, 128)
        m["vecA"] = va
        m["xp"] = np.ascontiguousarray(I["x_prompt"][4 * c:4 * c + 4].reshape(NPS * LP, D))
        m["xs"] = np.ascontiguousarray(np.roll(I["x_sample"][b], -h * 512, axis=0))
        m["cdk"] = np.ascontiguousarray(I["cache_diff_k"][b, 0].reshape(PAST, 512))
        m["cdv"] = np.ascontiguousarray(I["cache_diff_v"][b, 0].reshape(PAST, 512))
        m["cgk"] = np.ascontiguousarray(I["cache_gqa_k"][b, 0].reshape(PAST, 256))
        m["cgv"] = np.ascontiguousarray(I["cache_gqa_v"][b, 0].reshape(PAST, 256))
        maps.append(m)
    return maps


def assemble(results):
    yp = np.zeros((32, 256, D), np.float32)
    ys = np.zeros((4, 1024, D), np.float32)
    ndk = np.zeros((32, 1, 256, 4, 2, 64), np.float32)
    ndv = np.zeros((32, 1, 256, 4, 128), np.float32)
    ngk = np.zeros((32, 1, 256, 2, 128), np.float32)
    ngv = np.zeros((32, 1, 256, 2, 128), np.float32)
    for c, r in enumerate(results):
        b, h = c // 2, c % 2
        yp[4 * c:4 * c + 4] = np.asarray(r["y_p"]).reshape(4, 256, D)
        ys[b, h * 512:(h + 1) * 512] = np.asarray(r["y_s"])
        ndk[4 * c:4 * c + 4, 0] = np.asarray(r["ndk"]).reshape(4, 256, 4, 2, 64)
        ndv[4 * c:4 * c + 4, 0] = np.asarray(r["ndv"]).reshape(4, 256, 4, 128)
        ngk[4 * c:4 * c + 4, 0] = np.asarray(r["ngk"]).reshape(4, 256, 2, 128)
        ngv[4 * c:4 * c + 4, 0] = np.asarray(r["ngv"]).reshape(4, 256, 2, 128)
    return (yp, ys, ndk, ndv, ngk, ngv)


def kernel(**inputs):
    nc, _ = build_program()
    maps = make_in_maps(inputs)
    res = run_bass_kernel_spmd(nc, maps, core_ids=list(range(8)))
    return assemble(res.results)
```

```python
import math
from contextlib import ExitStack

import numpy as np
import ml_dtypes

import concourse.bass as bass
import concourse.mybir as mybir
from concourse.bass_utils import run_bass_kernel_spmd

F32 = mybir.dt.float32
BF16 = mybir.dt.bfloat16
ALU = mybir.AluOpType
AF = mybir.ActivationFunctionType
AX = mybir.AxisListType

_DTSIZE = {F32: 4, BF16: 2}
ENGS = ("pe", "act", "dve", "pool", "sp")
SKIP_SAME_ENGINE_WAR = False
PREFETCH_MOD1 = True


def _region(ap):
    t = ap.tensor
    name = t.name
    esz = _DTSIZE.get(ap.dtype, 4)
    dims = list(ap.ap)
    off = int(ap.offset)
    space = str(ap.space)
    if space == "PSUM":
        pstep = dims[0][0] if dims[0][0] else 1
        foff = off % pstep
        lo = hi = foff
        for st, cnt in dims[1:]:
            if cnt > 1:
                if st > 0:
                    hi += st * (cnt - 1)
                else:
                    lo += st * (cnt - 1)
        return (name, 0, 128, (lo * esz // 2048) * 2048, (hi * esz // 2048 + 1) * 2048)
    if space == "SB":
        pstep, pcnt = dims[0]
        if pstep == 0:
            row = int(np.prod(t.shape[1:]))
            p0 = off // row
            foff = off % row
            pcnt = 1
        else:
            p0 = off // pstep
            foff = off % pstep
        lo = hi = foff
        for st, cnt in dims[1:]:
            if cnt > 1:
                if st > 0:
                    hi += st * (cnt - 1)
                else:
                    lo += st * (cnt - 1)
        return (name, p0, p0 + pcnt, lo * esz, (hi + 1) * esz)
    lo = hi = off
    for st, cnt in dims:
        if cnt > 1:
            if st > 0:
                hi += st * (cnt - 1)
            else:
                lo += st * (cnt - 1)
    return (name, 0, 1, lo * esz, (hi + 1) * esz)


def _ovl(a, b):
    return a[1] < b[2] and b[1] < a[2] and a[3] < b[4] and b[3] < a[4]


def _covers(a, b):
    return a[1] <= b[1] and a[2] >= b[2] and a[3] <= b[3] and a[4] >= b[4]


class Op:
    __slots__ = ("eng", "fn", "idx", "deps", "signal", "semval", "is_dma", "dsem", "dval", "tag", "phase")

    def __init__(self, eng, fn, is_dma, tag=""):
        self.eng = eng
        self.fn = fn
        self.is_dma = is_dma
        self.deps = {}
        self.signal = False
        self.semval = None
        self.dsem = None
        self.dval = None
        self.tag = tag


class Sched:
    def __init__(self, nc, n_dma_sems=16):
        self.nc = nc
        self.ops = {e: [] for e in ENGS}
        self.writers = {}
        self.readers = {}
        self.n_dma_sems = n_dma_sems
        self.ro = set()
        self.phase = "setup"

    def _add_dep(self, op, d, raw=True):
        if d is op:
            return
        if SKIP_SAME_ENGINE_WAR and (not raw) and (not d.is_dma) and (not op.is_dma) and d.eng == op.eng:
            return
        if d.is_dma:
            op.deps[("dma", id(d))] = d
        else:
            cur = op.deps.get(d.eng)
            if cur is None or cur.idx < d.idx:
                op.deps[d.eng] = d

    def record(self, eng, fn, reads, writes, is_dma=False, tag=""):
        op = Op(eng, fn, is_dma, tag)
        op.phase = self.phase
        op.idx = len(self.ops[eng])
        rregs = [_region(a) for a in reads if a is not None and not isinstance(a, (int, float))]
        wregs = [_region(a) for a in writes if a is not None]
        for r in rregs:
            if r[0].startswith("pball") and r not in wregs:
                wregs.append(r)
        for r in rregs:
            if r[0] in self.ro:
                continue
            for wr, wop in self.writers.get(r[0], ()):
                if _ovl(wr, r):
                    self._add_dep(op, wop)
        for w in wregs:
            wl = self.writers.setdefault(w[0], [])
            for wr, wop in wl:
                if _ovl(wr, w):
                    self._add_dep(op, wop, raw=False)
            rl = self.readers.setdefault(w[0], [])
            for rr, rops in rl:
                if _ovl(rr, w):
                    for rop in rops.values():
                        self._add_dep(op, rop, raw=False)
        for w in wregs:
            wl = self.writers[w[0]]
            wl[:] = [e for e in wl if not _covers(w, e[0])]
            wl.append([w, op])
            rl = self.readers[w[0]]
            rl[:] = [e for e in rl if not _covers(w, e[0])]
        for r in rregs:
            if r[0] in self.ro:
                continue
            rl = self.readers.setdefault(r[0], [])
            key = ("dma", id(op)) if is_dma else eng
            for e in rl:
                if e[0] == r:
                    e[1][key] = op
                    break
            else:
                rl.append([r, {key: op}])
        for d in op.deps.values():
            d.signal = True
        self.ops[eng].append(op)
        return op

    def emit(self, stack):
        nc = self.nc
        sems = {e: stack.enter_context(nc.semaphore("s_" + e)) for e in ENGS}
        dsems = {e: [stack.enter_context(nc.semaphore(f"d_{e}_{i}")) for i in range(self.n_dma_sems)]
                 for e in ("sp", "pool", "act")}
        all_dma = []
        for e in ENGS:
            c = 0
            dcount = [0] * self.n_dma_sems
            kk = 0
            for op in self.ops[e]:
                if op.is_dma:
                    j = kk % self.n_dma_sems
                    kk += 1
                    dcount[j] += 16
                    op.dsem = dsems[e][j]
                    op.dval = dcount[j]
                    all_dma.append(op)
                elif op.signal:
                    c += 1
                    op.semval = c
        block = stack.enter_context(nc.Block())
        eng_obj = {"pe": "tensor", "act": "scalar", "dve": "vector", "pool": "gpsimd", "sp": "sync"}

        def make(e):
            def body(engine):
                waited = {}

                def wait(sem, val):
                    if waited.get(sem.num, 0) >= val:
                        return
                    waited[sem.num] = val
                    engine.wait_ge(sem, val)

                for op in self.ops[e]:
                    for d in op.deps.values():
                        if d.is_dma:
                            wait(d.dsem, d.dval)
                        else:
                            if d.eng == e and e == "pe":
                                continue
                            wait(sems[d.eng], d.semval)
                    if op.is_dma and op.dval > 16:
                        wait(op.dsem, op.dval - 16)
                    ins = op.fn(engine)
                    if op.is_dma:
                        ins.then_inc(op.dsem, 16)
                    elif op.signal:
                        ins.then_inc(sems[e], 1)
                if e == "sp":
                    last = {}
                    for op in all_dma:
                        last[op.dsem.num] = (op.dsem, op.dval)
                    for sem, val in last.values():
                        engine.wait_ge(sem, val)
            return body

        for e in ENGS:
            getattr(block, eng_obj[e])(make(e))


class K:
    def __init__(self, nc, sched):
        self.nc = nc
        self.s = sched

    def mm(self, out, lhsT, rhs, start=True, stop=True, sgc=False, tag="mm"):
        if sgc:
            fn = lambda e: e.matmul(out, lhsT, rhs, start=start, stop=stop, skip_group_check=True)
        else:
            fn = lambda e: e.matmul(out, lhsT, rhs, start=start, stop=stop)
        return self.s.record("pe", fn, [lhsT, rhs] + ([] if start else [out]), [out], tag=tag)

    def transpose(self, out, in_, ident, tag="tr"):
        return self.s.record("pe", lambda e: e.transpose(out, in_, ident), [in_, ident], [out], tag=tag)

    def act(self, out, in_, func, bias=None, scale=None, accum_out=None, tag="act"):
        kw = {}
        if bias is not None:
            kw["bias"] = bias
        if scale is not None:
            kw["scale"] = scale
        if accum_out is not None:
            kw["accum_out"] = accum_out
        rd = [in_] + [a for a in (bias, scale) if a is not None and not isinstance(a, (int, float))]
        wr = [out] + ([accum_out] if accum_out is not None else [])
        return self.s.record("act", lambda e: e.activation(out, in_, func, **kw), rd, wr, tag=tag)

    def tt(self, eng, out, in0, in1, op, tag="tt"):
        return self.s.record(eng, lambda e: e.tensor_tensor(out, in0, in1, op), [in0, in1], [out], tag=tag)

    def ts(self, eng, out, in0, s1, s2, op0, op1=None, tag="ts"):
        rd = [in0] + [a for a in (s1, s2) if a is not None and not isinstance(a, (int, float))]
        if op1 is None:
            return self.s.record(eng, lambda e: e.tensor_scalar(out, in0, s1, None, op0), rd, [out], tag=tag)
        return self.s.record(eng, lambda e: e.tensor_scalar(out, in0, s1, s2, op0, op1), rd, [out], tag=tag)

    def stt(self, eng, out, in0, scalar, in1, op0, op1, tag="stt"):
        rd = [in0, in1] + ([scalar] if not isinstance(scalar, (int, float)) else [])
        return self.s.record(eng, lambda e: e.scalar_tensor_tensor(out, in0, scalar, in1, op0, op1),
                             rd, [out], tag=tag)

    def copy(self, eng, out, in_, tag="copy"):
        if eng == "act":
            return self.s.record(eng, lambda e: e.copy(out, in_), [in_], [out], tag=tag)
        return self.s.record(eng, lambda e: e.tensor_copy(out, in_), [in_], [out], tag=tag)

    def memset(self, eng, out, val, tag="memset"):
        return self.s.record(eng, lambda e: e.memset(out, val), [], [out], tag=tag)

    def recip(self, out, in_, tag="recip"):
        return self.s.record("dve", lambda e: e.reciprocal(out, in_), [in_], [out], tag=tag)

    def rsum(self, out, in_, tag="rsum"):
        return self.s.record("dve", lambda e: e.reduce_sum(out, in_, AX.X), [in_], [out], tag=tag)

    def dma(self, q, out, in_, tag="dma"):
        return self.s.record(q, lambda e: e.dma_start(out=out, in_=in_), [in_], [out], is_dma=True, tag=tag)


D = 1024
NPS = 4
LP = 256
LS = 1024
PAST = 512
EPS = 1e-6
MAGIC = 12582912.0
TWO_PI = 2.0 * math.pi
LAM_INIT0 = 0.8 - 0.6 * math.exp(-0.3 * 0)

STAGES = ("l0", "l1", "final")
import os
EXPERIMENT = os.environ.get("KEXP", "")


class _Stop(Exception):
    pass


def build_program(stages=STAGES, dbg=None, stop_at=None):
    nc = bass.Bass("TRN2", target_bir_lowering=False)

    def din(name, shape, dt=F32):
        return nc.dram_tensor(name, list(shape), dt, kind="ExternalInput").ap()

    def dout(name, shape, dt=F32):
        return nc.dram_tensor(name, list(shape), dt, kind="ExternalOutput").ap()

    d_xp = din("xp", [NPS * LP, D])
    d_xs = din("xs", [LS, D])
    d_cdk = din("cdk", [PAST, 512])
    d_cdv = din("cdv", [PAST, 512])
    d_cgk = din("cgk", [PAST, 256])
    d_cgv = din("cgv", [PAST, 256])
    d_vecA = din("vecA", [128, 128])
    d_vecB = din("vecB", [128, 128])
    d_wmod = din("w_mod", [2, D, 3 * D])
    d_evin = din("ev_w_in", [D, 4096])
    d_evout = din("ev_w_out", [D, D])
    d_odin = din("od_w_in", [D, 2560])
    d_odout = din("od_w_out", [D, D])
    d_w1 = din("hy_w1", [33, 64])
    d_w2 = din("hy_w2", [64, 64])
    d_w3 = din("hy_w3", [64, 2048])
    d_skip = din("hy_skip", [2, 512])
    d_fg = din("final_g", [D])
    d_lam = din("df_lambda", [1, 256])
    d_gkg = din("gq_k_g", [128])
    d_identb = din("ident_bf", [128, 128], BF16)
    d_identf = din("ident_f", [128, 128])
    d_perm = din("perm_bf", [128, 128], BF16)
    d_ropeD = din("ropeD", [2, 128, LS])
    d_ropeG = din("ropeG", [2, 128, LS])
    d_zposP = din("zposP", [33, LP])
    d_zposS = din("zposS", [33, LS])
    d_winP = din("winP", [2, LP, 512])
    d_winS = din("winS", [2, LS, 512])
    d_FfP = din("FfP", [128, 2, 512], BF16)
    d_FinvP = din("FinvP", [128, 4, 256], BF16)
    d_FfS = din("FfS", [4, 128, 8, 256], BF16)
    d_FinvS = din("FinvS", [8, 128, 16, 128], BF16)
    d_acol = din("acol", [128, 2])

    o_yp = dout("y_p", [NPS * LP, D])
    o_ys = dout("y_s", [LS // 2, D])
    o_ndk = dout("ndk", [NPS * LP, 512])
    o_ndv = dout("ndv", [NPS * LP, 512])
    o_ngk = dout("ngk", [NPS * LP, 256])
    o_ngv = dout("ngv", [NPS * LP, 256])

    S = Sched(nc)
    k = K(nc, S)
    for a in (d_xp, d_xs, d_cdk, d_cdv, d_cgk, d_cgv, d_vecA, d_vecB, d_wmod, d_evin, d_evout, d_odin, d_odout,
              d_w1, d_w2, d_w3, d_skip, d_fg, d_lam, d_gkg, d_identb, d_identf, d_perm, d_ropeD, d_ropeG,
              d_zposP, d_zposS, d_winP, d_winS, d_FfP, d_FinvP, d_FfS, d_FinvS, d_acol):
        S.ro.add(a.tensor.name)

    dbg_outs = {}

    with ExitStack() as st:
        def sb(name, shape, dt):
            return st.enter_context(nc.sbuf_tensor("sb_" + name, list(shape), dt))

        xP = sb("xP", [128, 8, D], F32)
        xS = sb("xS", [128, 8, D], F32)
        hT = sb("hT", [128, 8, 1024], BF16)
        ybuf = sb("ybuf", [128, 8, 1024], BF16)
        wbuf = [sb(f"wbuf{i}", [128, 8, 256], BF16) for i in range(2)]
        REG_BYTES = 58 * 1024
        region = sb("region", [128, REG_BYTES // 2], BF16)
        ring = [sb(f"ring{i}", [128, 2048], BF16) for i in range(3)]
        spec = sb("spec", [128, 16, 256], BF16)
        gate_bc = sb("gate_bc", [128, D], F32)
        ident_b = sb("ident_b", [128, 128], BF16)
        ident_f = sb("ident_f", [128, 128], F32)
        perm_b = sb("perm_b", [128, 128], BF16)
        ones_f = sb("ones_f", [128, 128], F32)
        ones_b = sb("ones_b", [128, 128], BF16)
        epsc = sb("epsc", [128, 1], F32)
        acol = sb("acol", [128, 2], F32)
        vA = sb("vA", [128, 128], F32)
        vB = sb("vB", [128, 128], F32)
        vrow = sb("vrow", [128, 128], F32)
        sc = sb("sc", [128, 8, 2], BF16)
        modTs = [sb(f"modT{l}", [128, 24, 2], F32) for l in range(2)]
        gsTs = [sb(f"gsT{l}", [128, 2, 8], F32) for l in range(2)]
        fgb = sb("fgb", [128, D], F32)
        cur = {"modT": modTs[0], "gsT": gsTs[0]}
        diag = [sb(f"diag{i}", [128, 128], F32) for i in range(2)]
        ss = sb("ss", [128, 8], F32)
        rs = sb("rs", [128, 8], F32)
        FfP = sb("FfP", [128, 2, 512], BF16)
        FinvP = sb("FinvP", [128, 4, 256], BF16)
        w1s = sb("w1s", [33, 64], F32)
        w2s = sb("w2s", [64, 64], F32)
        w3b = sb("w3b", [64, 2048], BF16)
        hid2b = sb("hid2b", [64, 1024], BF16)
        lamrow = sb("lamrow", [1, 256], F32)
        lamt = sb("lamt", [1, 8], F32)
        neglam = sb("neglam", [128, 1], F32)
        subg = sb("subg", [128, 1], F32)
        small = sb("small", [128, 64], F32)

        pball = st.enter_context(nc.psum_tensor("pball", [128, 4096], F32))
        pb = [pball[:, i * 512:(i + 1) * 512] for i in range(8)]

        def pbf(i):
            return pb[i].bitcast(BF16)

        def RV(off, shape, dt):
            n = int(np.prod(shape))
            esz = 2 if dt == BF16 else 4
            assert off % 4 == 0 and off + n * esz <= REG_BYTES, (off, shape)
            v = region[:, off // 2:(off + n * esz) // 2]
            if dt == F32:
                v = v.bitcast(F32)
            if len(shape) == 2:
                v = v.rearrange("p (a b) -> p a b", a=shape[0])
            elif len(shape) == 3:
                v = v.rearrange("p (a b c) -> p a b c", a=shape[0], b=shape[1])
            return v

        def dbg_dump(name, ap, shape, dt=F32):
            if dbg is not None and name in dbg:
                o = dout("dbg_" + name, shape, dt)
                k.dma("sp", o, ap)
                dbg_outs[name] = o

        rot = {"p": 0, "t": 0}

        def mark(label):
            S.phase = "after_" + label
            if stop_at is not None and label == stop_at:
                raise _Stop()

        pp_pool = {"banks": [0, 1]}

        def set_pp(banks):
            pp_pool["banks"] = list(banks)

        def next_pp():
            rot["p"] = (rot["p"] + 1) % len(pp_pool["banks"])
            return pb[pp_pool["banks"][rot["p"]]]

        pt_fixed = {"on": False}

        def next_pt():
            if pt_fixed["on"]:
                return 3
            rot["t"] ^= 1
            return 2 + rot["t"]

        k.dma("sp", ident_b[:], d_identb)
        k.dma("sp", ident_f[:], d_identf)
        k.dma("sp", perm_b[:], d_perm)
        k.dma("sp", acol[:], d_acol)
        k.dma("sp", vrow[:], d_vecA)
        k.memset("dve", ones_f[:], 1.0)
        k.memset("dve", ones_b[:], 1.0)
        k.memset("dve", epsc[:], EPS)
        k.transpose(pb[2][:, 0:128], vrow[:], ident_f[:])
        k.copy("dve", vA[:], pb[2][:, 0:128])
        k.dma("sp", vrow[:], d_vecB)
        k.transpose(pb[3][:, 0:128], vrow[:], ident_f[:])
        k.copy("dve", vB[:], pb[3][:, 0:128])
        k.dma("sp", FfP[:], d_FfP)
        k.dma("sp", FinvP[:], d_FinvP)
        k.dma("sp", w1s[:], d_w1)
        k.dma("sp", w2s[:], d_w2)
        k.dma("pool", w3b[:], d_w3)
        for t in range(8):
            k.dma("sp", xP[:, t, :], d_xp[t * 128:(t + 1) * 128, :])
        k.act(sc[:, :, 0], vA[:, 72:80], AF.Silu)
        k.act(sc[:, :, 1], vA[:, 80:88], AF.Silu)
        k.dma("sp", lamrow[:], d_lam)
        k.tt("dve", lamrow[:, 0:64], lamrow[:, 0:64], lamrow[:, 64:128], ALU.mult)
        k.tt("dve", lamrow[:, 128:192], lamrow[:, 128:192], lamrow[:, 192:256], ALU.mult)
        k.rsum(lamt[:, 0:1], lamrow[:, 0:64])
        k.rsum(lamt[:, 1:2], lamrow[:, 128:192])
        k.act(lamt[:, 2:4], lamt[:, 0:2], AF.Exp)
        k.tt("dve", lamt[:, 4:5], lamt[:, 3:4], lamt[:, 2:3], ALU.subtract)
        k.ts("dve", lamt[:, 5:6], lamt[:, 4:5], -LAM_INIT0, None, ALU.add)
        k.mm(pb[2][:, 0:1], ones_f[0:1, :], lamt[0:1, 5:6])
        k.copy("dve", neglam[:], pb[2][:, 0:1])
        k.ts("dve", subg[:], vB[:, 19:20], 1.0 - LAM_INIT0, None, ALU.mult)

        def wblock_dummy():
            pass

        def wblock(w2d, c0, ncols, buf):
            k.dma("pool", buf[:, :, 0:ncols],
                  w2d[:, c0:c0 + ncols].rearrange("(kk p) c -> p kk c", p=128))

        wrot = {"i": 0}

        def next_wbuf():
            wrot["i"] ^= 1
            return wbuf[wrot["i"]]

        def mod_block(l, blk, wb, pm):
            modT = modTs[l]
            wblock(d_wmod[l], blk * 256, 256, wb)
            for cc in range(2):
                for kc in range(8):
                    k.mm(pm[:, cc * 2:cc * 2 + 2], wb[:, kc, cc * 128:(cc + 1) * 128], sc[:, kc, :],
                         start=(kc == 0), stop=(kc == 7))
            c0 = blk * 2
            pm3 = pm[:, 0:4].rearrange("p (c n) -> p c n", n=2)
            for cond in range(2):
                k.tt("dve", modT[:, c0:c0 + 2, cond], pm3[:, :, cond],
                     vA[:, 24 + 24 * l + c0:24 + 24 * l + c0 + 2], ALU.add)

        def mod_finish(l):
            for cond in range(2):
                k.stt("dve", gsTs[l][:, cond, :], modTs[l][:, 8:16, cond], 1.0, vA[:, 8 * l:8 * l + 8],
                      ALU.add, ALU.mult)

        def compute_mod(l, blocks=range(12), finish=True):
            for blk in blocks:
                mod_block(l, blk, next_wbuf(), pb[2])
            if finish:
                mod_finish(l)

        def compute_gate_bc(cond):
            for j in range(8):
                dg = diag[j % 2]
                k.ts("dve", dg[:], ident_f[:], cur["modT"][:, 16 + j, cond:cond + 1], None, ALU.mult)
                k.mm(pb[4 + j // 4][:, (j % 4) * 128:(j % 4 + 1) * 128], ones_f[:], dg[:])
            k.copy("dve", gate_bc[:, 0:512], pb[4][:])
            k.copy("dve", gate_bc[:, 512:1024], pb[5][:])

        pre_rs = {}

        def adaln_stats(xg, ntiles, key, junk_off):
            ss_ = sb("ss_" + key, [128, 8], F32)
            rs_ = sb("rs_" + key, [128, 8], F32)
            junk = RV(junk_off, [1024], BF16)
            k.memset("dve", ss_[:], 0.0)
            for t in range(ntiles):
                k.act(junk, xg[:, t, :], AF.Square, accum_out=ss_[:, t:t + 1])
            k.act(rs_[:, 0:ntiles], ss_[:, 0:ntiles], AF.Ln, bias=epsc[:], scale=1.0 / D)
            k.act(rs_[:, 0:ntiles], rs_[:, 0:ntiles], AF.Exp, scale=-0.5)
            pre_rs[key] = rs_

        def adaln(xg, ntiles, cond, pre=None):
            xn = [RV(i * 2048, [1024], BF16) for i in range(8)]
            if pre is not None and pre in pre_rs:
                rs_u = pre_rs[pre]
            else:
                rs_u = rs
                k.memset("dve", ss[:], 0.0)
                for t in range(ntiles):
                    k.act(xn[t], xg[:, t, :], AF.Square, accum_out=ss[:, t:t + 1])
                k.act(rs[:, 0:ntiles], ss[:, 0:ntiles], AF.Ln, bias=epsc[:], scale=1.0 / D)
                k.act(rs[:, 0:ntiles], rs[:, 0:ntiles], AF.Exp, scale=-0.5)
            for t in range(ntiles):
                k.ts("dve", xn[t], xg[:, t, :], rs_u[:, t:t + 1], None, ALU.mult)
            for half in range(ntiles // 4):
                for j in range(8):
                    bank = pbf(next_pt())
                    for tt_ in range(4):
                        k.transpose(bank[:, tt_ * 128:(tt_ + 1) * 128],
                                    xn[half * 4 + tt_][:, j * 128:(j + 1) * 128], ident_b[:])
                    if j % 2 == 0:
                        k.act(hT[:, j, half * 512:(half + 1) * 512], bank[:, 0:512], AF.Identity,
                              bias=cur["modT"][:, j, cond:cond + 1], scale=cur["gsT"][:, cond, j:j + 1])
                    else:
                        k.ts("dve", hT[:, j, half * 512:(half + 1) * 512], bank[:, 0:512],
                             cur["gsT"][:, cond, j:j + 1], cur["modT"][:, j, cond:cond + 1], ALU.mult, ALU.add)

        deferred = []

        def defer(fn, delay=1):
            deferred.append([delay, fn])

        bg_tasks = []

        def run_deferred(flush=False):
            for e_ in deferred:
                e_[0] -= 1
            while deferred and (flush or deferred[0][0] <= 0):
                deferred.pop(0)[1]()
            if bg_tasks and not flush:
                bg_tasks.pop(0)()

        def proj_fm(w2d, c0, nblk, ntok, evac):
            for b in range(nblk):
                wb = next_wbuf()
                wblock(w2d, c0 + b * 256, 256, wb)
                for cc in range(2):
                    for tb in range(ntok // 512):
                        ps = next_pp()
                        for kc in range(8):
                            k.mm(ps[:], wb[:, kc, cc * 128:(cc + 1) * 128], hT[:, kc, tb * 512:(tb + 1) * 512],
                                 start=(kc == 0), stop=(kc == 7))
                        run_deferred()
                        evac(b * 2 + cc, tb, ps)
            run_deferred(flush=True)

        def proj_tm(w2d, c0, nblk, ntiles, evac):
            for b in range(nblk):
                wb = next_wbuf()
                wblock(w2d, c0 + b * 256, 256, wb)
                for t in range(ntiles):
                    ps = next_pp()
                    for kc in range(8):
                        k.mm(ps[:, 0:256], hT[:, kc, t * 128:(t + 1) * 128], wb[:, kc, :],
                             start=(kc == 0), stop=(kc == 7))
                    run_deferred()
                    evac(b, t, ps)
            run_deferred(flush=True)

        def wout_residual(w2d, xg, ntiles):
            tmp = [RV(i * 1024, [256], F32) for i in range(4)]
            set_pp([0, 1, 4, 5, 6, 7])
            for b in range(4):
                wb = next_wbuf()
                wblock(w2d, b * 256, 256, wb)
                for t in range(ntiles):
                    ps = next_pp()
                    for kc in range(8):
                        k.mm(ps[:, 0:256], ybuf[:, kc, t * 128:(t + 1) * 128], wb[:, kc, :],
                             start=(kc == 0), stop=(kc == 7))
                    tm = tmp[t % 4]
                    k.tt("dve", tm, ps[:, 0:256], gate_bc[:, b * 256:(b + 1) * 256], ALU.mult)
                    k.tt("dve", xg[:, t, b * 256:(b + 1) * 256], xg[:, t, b * 256:(b + 1) * 256], tm, ALU.add)
            set_pp([0, 1])

        def sin_rr(out, ps_in, bcol, fcol, tmp_a, tmp_b, npart):
            k.ts("dve", tmp_a, ps_in, bcol, fcol, ALU.add, ALU.mult)
            k.ts("dve", tmp_b, tmp_a, 1.0 / TWO_PI, MAGIC, ALU.mult, ALU.add)
            k.ts("dve", tmp_b, tmp_b, MAGIC, None, ALU.subtract)
            k.stt("dve", tmp_a, tmp_b, -TWO_PI, tmp_a, ALU.mult, ALU.add)
            k.act(out, tmp_a, AF.Sin)

        OFF_ZT = 0
        OFF_G1 = 8192
        OFF_GG = 16384
        OFF_Y = 24576
        OFF_RAW = 32768
        OFF_T = 41472
        OFF_SKIP = 51712

        def hyena_group(xg_cond, L, nseq, d_zpos, d_win, ff_piece, finv_piece):
            kt_n = L // 128
            rt_n = 2 * kt_n
            nsl = nseq * 256
            seglen = 256 if nseq == 4 else 512
            nseg = 1024 // seglen
            zT = RV(OFF_ZT, [kt_n, 2, nsl], BF16)
            g1T = RV(OFF_G1, [kt_n, 2, nsl], BF16)
            gg = RV(OFF_GG, [4, 1024], BF16)
            Y = RV(OFF_Y, [rt_n, nsl], BF16)
            fa = RV(OFF_RAW, [kt_n, 256], BF16)
            fb = RV(OFF_RAW + 4096, [kt_n, 256], BF16)
            raw = [RV(OFF_RAW + i * 4352, [nseg, seglen + 2], F32) for i in range(2)]
            skipbc = RV(OFF_SKIP, [2, 512], F32)
            U = RV(OFF_T, [nseg, seglen], F32)
            ubfs = [RV(OFF_T + 4096, [1024], BF16), RV(OFF_T + 8192, [1024], BF16)]
            sgt = [RV(OFF_T + 6144 + i * 1024, [512], BF16) for i in range(2)]
            zp = RV(OFF_T, [1024], F32)
            ha = RV(OFF_T + 4096, [512], F32)
            hb = RV(OFF_T + 6144, [512], F32)
            h1 = RV(OFF_RAW, [1024], F32)
            winf = [RV(OFF_T + i * 2048, [256], F32) for i in range(2)]
            winb = [RV(OFF_T + 2048 + i * 2048, [256], F32) for i in range(2)]
            winf = [RV(OFF_T + 0, [256], F32), RV(OFF_T + 2048, [256], F32)]
            winb = [RV(OFF_T + 1024, [256], F32), RV(OFF_T + 3072, [256], F32)]
            ft1 = RV(OFF_T + 4096, [256], F32)
            ft2 = RV(OFF_T + 5120, [256], F32)
            xr = [RV(OFF_T + i * 512, [256], BF16) for i in range(2)]
            xi = [RV(OFF_T + 1024 + i * 512, [256], BF16) for i in range(2)]
            t1 = RV(OFF_T + 2048, [256], BF16)
            t2 = RV(OFF_T + 2560, [256], BF16)
            t3 = RV(OFF_T + 3072, [256], BF16)
            t4 = RV(OFF_T + 3584, [256], BF16)
            ytm = [RV(OFF_T + 8192 + i * 1024, [512], BF16) for i in range(2)]

            for o in range(2):
                k.dma("sp", skipbc[:, o, :], d_skip[o].partition_broadcast(128))

            k.dma("sp", zp[0:33, 0:L], d_zpos)
            nb = max(1, L // 512)
            bw = min(L, 512)
            for tb in range(nb):
                ps = next_pp()
                k.mm(ps[0:64, 0:bw], w1s[0:33, :], zp[0:33, tb * bw:(tb + 1) * bw])
                sin_rr(h1[0:64, tb * bw:(tb + 1) * bw], ps[0:64, 0:bw], vB[0:64, 12:13], vB[0:64, 13:14],
                       ha[0:64, 0:bw], hb[0:64, 0:bw], 64)
            for tb in range(nb):
                ps = next_pp()
                k.mm(ps[0:64, 0:bw], w2s[0:64, :], h1[0:64, tb * bw:(tb + 1) * bw])
                sin_rr(hid2b[0:64, tb * bw:(tb + 1) * bw], ps[0:64, 0:bw], vB[0:64, 14:15], vB[0:64, 13:14],
                       ha[0:64, 0:bw], hb[0:64, 0:bw], 64)

            mark("filt" + ("P" if nseq == 4 else "S"))
            for i in range(2):
                k.memset("dve", raw[i], 0.0)
            conv_state = {}

            def evacA(cc, tb, ps):
                if cc < 12:
                    rw = raw[cc % 2]
                    nsb = 512 // seglen
                    k.copy("act", rw[:, tb * nsb:(tb + 1) * nsb, 1:1 + seglen],
                           ps[:].rearrange("p (s t) -> p s t", s=nsb))
                    if tb == 1:
                        if nseq == 1:
                            a_ = acol[:, 0:1]
                            na_ = acol[:, 1:2]
                            k.ts("dve", rw[:, 0, 0:1], rw[:, 1, 512:513], a_, None, ALU.mult)
                            k.ts("dve", rw[:, 0, 513:514], rw[:, 1, 1:2], na_, None, ALU.mult)
                            k.ts("dve", rw[:, 1, 0:1], rw[:, 0, 512:513], na_, None, ALU.mult)
                            k.ts("dve", rw[:, 1, 513:514], rw[:, 0, 1:2], a_, None, ALU.mult)
                        w0 = vA[:, 88 + cc:89 + cc]
                        w1_ = vA[:, 100 + cc:101 + cc]
                        w2_ = vA[:, 112 + cc:113 + cc]
                        k.act(U, rw[:, :, 1:1 + seglen], AF.Identity, bias=vB[:, cc:cc + 1], scale=w1_)
                        k.stt("dve", U, rw[:, :, 0:seglen], w0, U, ALU.mult, ALU.add)
                        ubf = ubfs[cc % 2]
                        if cc < 8:
                            dst = ubf.rearrange("p (s t) -> p s t", s=nseg)
                        else:
                            dst = gg[:, cc - 8, :].rearrange("p (s t) -> p s t", s=nseg)
                        k.stt("dve", dst, rw[:, :, 2:2 + seglen], w2_, U, ALU.mult, ALU.add)
                        if cc < 8:
                            def tail(cc=cc, ubf=ubf):
                                bank = pbf(next_pt())
                                uv = ubf.rearrange("p (s t two) -> p s t two", s=nseq, two=2)
                                khh = kt_n // 2
                                for t in range(8):
                                    if nseq == 1:
                                        s_i, k_i = t // kt_n, t % kt_n
                                        par_, kk_i = k_i // khh, k_i % khh
                                        k.transpose(bank[:, t * 128:(t + 1) * 128],
                                                    uv[:, s_i, kk_i * 128:(kk_i + 1) * 128, par_], ident_b[:])
                                    else:
                                        k.transpose(bank[:, t * 128:(t + 1) * 128], ubf[:, t * 128:(t + 1) * 128],
                                                    ident_b[:])
                                dstT = zT if cc < 4 else g1T
                                c4 = cc % 4
                                hc, off = c4 // 2, (c4 % 2) * 128
                                src = bank[:, 0:1024].rearrange("p (s kk c) -> p kk s c", s=nseq, kk=kt_n)
                                dd = dstT[:, :, hc, :].rearrange("p kk (s c) -> p kk s c", s=nseq)[:, :, :,
                                                                                                   off:off + 128]
                                k.copy("dve", dd, src)
                            defer(tail, delay=2)
                else:
                    sg = sgt[tb]
                    k.act(sg, ps[:], AF.Silu)
                    k.tt("dve", gg[:, cc - 12, tb * 512:(tb + 1) * 512], gg[:, cc - 12, tb * 512:(tb + 1) * 512],
                         sg, ALU.mult)

            set_pp([0, 1, 4, 5, 6, 7])
            proj_fm(d_evin, 0, 8, 1024, evacA)
            set_pp([0, 1])
            dbg_dump("zT" + ("P" if nseq == 4 else "S"), region[:, OFF_ZT // 2:OFF_ZT // 2 + 4096], [128, 4096], BF16)
            dbg_dump("gg" + ("P" if nseq == 4 else "S"), region[:, OFF_GG // 2:OFF_GG // 2 + 4096], [128, 4096], BF16)

            mark("projA" + ("P" if nseq == 4 else "S"))
            sgn = 2 if nseq == 4 else 1
            ngrp = nseq // sgn
            ncol = sgn * 256
            ft1s = [RV(OFF_T + 4096, [256], F32), RV(OFF_T + 6144, [256], F32)]
            ft2s = [RV(OFF_T + 5120, [256], F32), RV(OFF_T + 7168, [256], F32)]

            def emit_taps(o, hc, kt2):
                pss_ = []
                for u in range(2):
                    kt = kt2 + u
                    ps = next_pp()
                    k.mm(ps[:, 0:256], hid2b[0:64, kt * 128:(kt + 1) * 128],
                         w3b[0:64, (o * 2) * 512 + hc * 256:(o * 2) * 512 + hc * 256 + 256])
                    k.mm(ps[:, 256:512], hid2b[0:64, kt * 128:(kt + 1) * 128],
                         w3b[0:64, (o * 2 + 1) * 512 + hc * 256:(o * 2 + 1) * 512 + hc * 256 + 256])
                    k.dma("sp", winf[u], d_win[0, kt * 128:(kt + 1) * 128, hc * 256:(hc + 1) * 256])
                    k.dma("sp", winb[u], d_win[1, kt * 128:(kt + 1) * 128, hc * 256:(hc + 1) * 256])
                    pss_.append(ps)
                for u in range(2):
                    k.tt("dve", ft1s[u], pss_[u][:, 0:256], winf[u], ALU.mult)
                for u in range(2):
                    k.tt("dve", ft2s[u], pss_[u][:, 256:512], winb[u], ALU.mult)
                for u in range(2):
                    k.tt("dve", fa[:, kt2 + u, :], ft1s[u], ft2s[u], ALU.add)
                for u in range(2):
                    k.tt("dve", fb[:, kt2 + u, :], ft1s[u], ft2s[u], ALU.subtract)

            eo = (nseq == 1)
            if eo:
                kh = kt_n // 2
                nj = kt_n // 2
                tb_ = [RV(OFF_T + i * 512, [256], BF16) for i in range(14)]
                Ao_sb, Bo_sb, Xc, Xs, Xcm, Xsm, p1, p2, p3, p4, p5, p6, p7, p8 = tb_
                xrot = {"i": 0}
                combos = [(o, hc) for o in range(2) for hc in range(2)]

                def Ytile(ci, r):
                    if ci % 2 == 0:
                        return Y[:, r, 0:256]
                    return wbuf[r // 8][:, r % 8, :]

                def stage_A(ci, j):
                    o, hc = combos[ci]
                    pc = ff_piece(j)
                    pa = next_pp()
                    for kk in range(kh):
                        k.mm(pa[:, 0:256], pc[:, kk, 0:128], fa[:, kk, :], start=(kk == 0), stop=(kk == kh - 1))
                    for kk in range(kh):
                        k.mm(pa[:, 256:512], pc[:, kh + kk, 0:128], fa[:, kh + kk, :],
                             start=(kk == 0), stop=(kk == kh - 1))
                    pbk = next_pp()
                    for kk in range(kh):
                        k.mm(pbk[:, 0:256], pc[:, kk, 128:256], fb[:, kk, :], start=(kk == 0), stop=(kk == kh - 1))
                    for kk in range(kh):
                        k.mm(pbk[:, 256:512], pc[:, kh + kk, 128:256], fb[:, kh + kk, :],
                             start=(kk == 0), stop=(kk == kh - 1))
                    k.copy("act", Ao_sb, pa[:, 256:512])
                    k.copy("act", Bo_sb, pbk[:, 256:512])
                    k.tt("dve", p1, pa[:, 0:256], skipbc[:, o, hc * 256:(hc + 1) * 256], ALU.add)
                    k.tt("dve", spec[:, 4 * j + 1, :], pbk[:, 0:256], Bo_sb, ALU.add)
                    k.tt("dve", spec[:, 4 * j + 3, :], Bo_sb, pbk[:, 0:256], ALU.subtract)
                    k.tt("dve", spec[:, 4 * j + 0, :], p1, Ao_sb, ALU.add)
                    k.tt("dve", spec[:, 4 * j + 2, :], p1, Ao_sb, ALU.subtract)
                    Kc, Ks, Kcm, Ksm = (spec[:, 4 * j + q_, :] for q_ in range(4))
                    xb_ = xrot["i"] % 2
                    xrot["i"] += 1
                    bA, bB = pb[4 + xb_ * 2], pb[5 + xb_ * 2]
                    zc = slice(0, 256)
                    for kk in range(kh):
                        k.mm(bA[:, 0:256], pc[:, kk, 0:128], zT[:, kk, hc, zc], start=(kk == 0), stop=(kk == kh - 1))
                    for kk in range(kh):
                        k.mm(bA[:, 256:512], pc[:, kh + kk, 0:128], zT[:, kh + kk, hc, zc],
                             start=(kk == 0), stop=(kk == kh - 1))
                    for kk in range(kh):
                        k.mm(bB[:, 0:256], pc[:, kk, 128:256], zT[:, kk, hc, zc], start=(kk == 0), stop=(kk == kh - 1))
                    for kk in range(kh):
                        k.mm(bB[:, 256:512], pc[:, kh + kk, 128:256], zT[:, kh + kk, hc, zc],
                             start=(kk == 0), stop=(kk == kh - 1))
                    k.copy("act", Ao_sb, bA[:, 256:512])
                    k.copy("act", Bo_sb, bB[:, 256:512])
                    k.tt("dve", Xc, bA[:, 0:256], Ao_sb, ALU.add)
                    k.tt("dve", Xs, bB[:, 0:256], Bo_sb, ALU.add)
                    k.tt("dve", Xcm, bA[:, 0:256], Ao_sb, ALU.subtract)
                    k.tt("dve", Xsm, Bo_sb, bB[:, 0:256], ALU.subtract)
                    k.tt("dve", p1, Xc, Kc, ALU.mult)
                    k.tt("dve", p2, Xs, Ks, ALU.mult)
                    k.tt("dve", p3, Xc, Ks, ALU.mult)
                    k.tt("dve", p4, Xs, Kc, ALU.mult)
                    k.tt("dve", p5, Xcm, Kcm, ALU.mult)
                    k.tt("dve", p6, Xsm, Ksm, ALU.mult)
                    k.tt("dve", p7, Xcm, Ksm, ALU.mult)
                    k.tt("dve", p8, Xsm, Kcm, ALU.mult)
                    k.tt("dve", Ytile(ci, 4 * j + 0), p1, p2, ALU.subtract)
                    k.tt("dve", Ytile(ci, 4 * j + 1), p3, p4, ALU.add)
                    k.tt("dve", Ytile(ci, 4 * j + 2), p5, p6, ALU.subtract)
                    k.tt("dve", Ytile(ci, 4 * j + 3), p7, p8, ALU.add)

                def stage_B(ci, i):
                    o, hc = combos[ci]
                    pc = finv_piece(i)
                    par, kk_ = i // kh, i % kh
                    ps = next_pp()
                    nmm = 4 * nj
                    for jf in range(nj):
                        for part in range(4):
                            idx = jf * 4 + part
                            k.mm(ps[:, 0:256], pc[:, idx, :], Ytile(ci, 4 * jf + part),
                                 start=(idx == 0), stop=(idx == nmm - 1))
                    cs = slice(0, 256)
                    if o == 0:
                        k.tt("dve", zT[:, i, hc, cs], ps[:, 0:256], g1T[:, i, hc, cs], ALU.mult)
                    else:
                        yt = ytm[i % 2]
                        k.copy("act", yt[:, 0:256], ps[:, 0:256])
                        bank = pbf(next_pt())
                        for c2 in range(2):
                            k.transpose(bank[:, c2 * 128:(c2 + 1) * 128], yt[:, c2 * 128:(c2 + 1) * 128], ident_b[:])
                        for c2 in range(2):
                            ch = hc * 2 + c2
                            yv = ybuf[:, ch, :].rearrange("p (s t two) -> p s t two", s=nseq, two=2)
                            gv = gg[:, ch, :].rearrange("p (s t two) -> p s t two", s=nseq, two=2)
                            k.tt("dve", yv[:, 0, kk_ * 128:(kk_ + 1) * 128, par], bank[:, c2 * 128:(c2 + 1) * 128],
                                 gv[:, 0, kk_ * 128:(kk_ + 1) * 128, par], ALU.mult)

                ncomb = len(combos)
                set_pp([0, 1, 2])
                pt_fixed["on"] = True
                for kt2 in range(0, kt_n, 2):
                    emit_taps(combos[0][0], combos[0][1], kt2)
                for j in range(nj):
                    stage_A(0, j)
                for ci in range(ncomb):
                    if ci + 1 < ncomb:
                        no_, nhc_ = combos[ci + 1]
                        taps_l = list(range(0, kt_n, 2))
                        assert len(taps_l) == kt_n // 2 and nj == kt_n // 2
                        for t_ in range(kt_n // 2):
                            emit_taps(no_, nhc_, taps_l[t_])
                            stage_B(ci, t_)
                        for j in range(nj):
                            stage_A(ci + 1, j)
                            stage_B(ci, kt_n // 2 + j)
                    else:
                        for i in range(kt_n):
                            stage_B(ci, i)
                set_pp([0, 1])
                pt_fixed["on"] = False

            else:
                combos = [(o, hc) for o in range(2) for hc in range(2)]
                for kt2 in range(0, kt_n, 2):
                    emit_taps(combos[0][0], combos[0][1], kt2)
                for ci, (o, hc) in enumerate(combos):
                    if True:
                        nxt = combos[ci + 1] if ci + 1 < len(combos) else None
                        pairs_left = list(range(0, kt_n, 2)) if nxt is not None else []
                        npairs = kt_n // 2
                        for j in range(kt_n):
                            pc = ff_piece(j)
                            psc = next_pp()
                            for kk in range(kt_n):
                                k.mm(psc[:, 0:256], pc[:, kk, 0:128], fa[:, kk, :], start=(kk == 0), stop=(kk == kt_n - 1))
                            pss = next_pp()
                            for kk in range(kt_n):
                                k.mm(pss[:, 0:256], pc[:, kk, 128:256], fb[:, kk, :], start=(kk == 0), stop=(kk == kt_n - 1))
                            k.tt("dve", spec[:, 2 * j, :], psc[:, 0:256], skipbc[:, o, hc * 256:(hc + 1) * 256], ALU.add)
                            k.copy("act", spec[:, 2 * j + 1, :], pss[:, 0:256])
                            for g in range(ngrp):
                                xb_ = (g if ngrp > 1 else j) % 2
                                xc = pb[4 + xb_ * 2]
                                xs_ = pb[5 + xb_ * 2]
                                for kk in range(kt_n):
                                    k.mm(xc[:, 0:ncol], pc[:, kk, 0:128], zT[:, kk, hc, g * ncol:(g + 1) * ncol],
                                         start=(kk == 0), stop=(kk == kt_n - 1))
                                for kk in range(kt_n):
                                    k.mm(xs_[:, 0:ncol], pc[:, kk, 128:256], zT[:, kk, hc, g * ncol:(g + 1) * ncol],
                                         start=(kk == 0), stop=(kk == kt_n - 1))
                                for s_ in range(sgn):
                                    sq = g * sgn + s_
                                    a_, b_ = xr[s_ % 2], xi[s_ % 2]
                                    k.copy("act", a_, xc[:, s_ * 256:(s_ + 1) * 256])
                                    k.copy("act", b_, xs_[:, s_ * 256:(s_ + 1) * 256])
                                    Kr, Ki = spec[:, 2 * j, :], spec[:, 2 * j + 1, :]
                                    k.tt("dve", t1, a_, Kr, ALU.mult)
                                    k.tt("dve", t2, b_, Ki, ALU.mult)
                                    k.tt("dve", t3, a_, Ki, ALU.mult)
                                    k.tt("dve", t4, b_, Kr, ALU.mult)
                                    k.tt("dve", Y[:, 2 * j, sq * 256:(sq + 1) * 256], t1, t2, ALU.subtract)
                                    k.tt("dve", Y[:, 2 * j + 1, sq * 256:(sq + 1) * 256], t3, t4, ALU.add)
                        for i in range(kt_n):
                            if pairs_left and ((i + 1) * npairs) // kt_n > npairs - len(pairs_left):
                                emit_taps(nxt[0], nxt[1], pairs_left.pop(0))
                            pc = finv_piece(i)
                            for g in range(ngrp):
                                ps = next_pp()
                                for r in range(rt_n):
                                    k.mm(ps[:, 0:ncol], pc[:, r, :], Y[:, r, g * ncol:(g + 1) * ncol],
                                         start=(r == 0), stop=(r == rt_n - 1))
                                if o == 0:
                                    k.tt("dve", zT[:, i, hc, g * ncol:(g + 1) * ncol], ps[:, 0:ncol],
                                         g1T[:, i, hc, g * ncol:(g + 1) * ncol], ALU.mult)
                                else:
                                    yt = ytm[(i * ngrp + g) % 2]
                                    k.copy("act", yt[:, 0:ncol], ps[:, 0:ncol])
                                    bank = pbf(next_pt())
                                    for s_ in range(sgn):
                                        for c2 in range(2):
                                            slot = s_ * 2 + c2
                                            k.transpose(bank[:, slot * 128:(slot + 1) * 128],
                                                        yt[:, s_ * 256 + c2 * 128:s_ * 256 + (c2 + 1) * 128], ident_b[:])
                                    for s_ in range(sgn):
                                        sq = g * sgn + s_
                                        tok0 = sq * L + i * 128
                                        for c2 in range(2):
                                            slot = s_ * 2 + c2
                                            ch = hc * 2 + c2
                                            k.tt("dve", ybuf[:, ch, tok0:tok0 + 128], bank[:, slot * 128:(slot + 1) * 128],
                                                 gg[:, ch, tok0:tok0 + 128], ALU.mult)
                        if o == 0 and hc == 0:
                            dbg_dump("z2" + ("P" if nseq == 4 else "S"), region[:, OFF_ZT // 2:OFF_ZT // 2 + 4096], [128, 4096], BF16)

        ring_rot = {"i": 0}

        def ffS_piece(j):
            r = ring[ring_rot["i"] % 3]
            ring_rot["i"] += 1
            v = r[:, 0:2048].rearrange("p (kk c) -> p kk c", kk=8)
            k.dma("sp", v, d_FfS[j])
            return v

        def finvS_piece(i):
            r = ring[ring_rot["i"] % 3]
            ring_rot["i"] += 1
            v = r[:, 0:2048].rearrange("p (r c) -> p r c", r=16)
            k.dma("sp", v, d_FinvS[i])
            return v

        def ffP_piece(j):
            return FfP[:, :, j * 256:(j + 1) * 256]

        def finvP_piece(i):
            return FinvP[:, :, i * 128:(i + 1) * 128]

        OFF_Q = 0
        OFF_SG0 = 8192
        OFF_K0 = 16384
        OFF_V0 = 28672
        OFF_SG1 = 16384
        OFF_K1 = 32768
        OFF_V1P = 36864
        OFF_V1S = 24576
        ropeC = RV(41216, [LS], F32)
        ropeS = RV(45312, [LS], F32)
        att_tmp = RV(49408, [2048], F32)

        def attention(nheads, ncomp, dk, qT, kT, Vaug, sgT, units, scale, ych0, finalize_diff, extra=None):
            Eb = att_tmp[:].bitcast(BF16)
            E = [Eb[:, i * 512:(i + 1) * 512] for i in range(8)]
            fin_sq = RV(57600, [512], BF16)
            pair = all(u_[1] <= 256 for u_ in units)

            steps = []
            for (q0, nq, ktl, tok0) in units:
                for h in range(nheads):
                    hh = (h % 2) if pair else 0
                    for ki, (ktile, kc0) in enumerate(ktl):
                        for m in range(ncomp):
                            steps.append((q0, nq, tok0, h, m, ki, ktile, kc0, ki == 0, ki == len(ktl) - 1, hh))
            n = len(steps)

            def banks(h, m):
                if ncomp == 2:
                    g = m
                else:
                    g = ((h // 2) % 2) if pair else (h % 2)
                return pb[4 + 2 * g], pb[5 + 2 * g]

            def issue_S(i):
                q0, nq, tok0, h, m, ki, ktile, kc0, first, last, hh = steps[i]
                ps = pb[i % 3]
                if ncomp == 2:
                    k.mm(ps[:, 0:nq], kT[:, h, kc0:kc0 + 128], qT[m][:, h, q0:q0 + nq])
                else:
                    k.mm(ps[:, 0:nq], kT[0:128, h, kc0:kc0 + 128], qT[0:128, h, q0:q0 + nq])
                k.act(E[i % 8][:, 0:nq], ps[:, 0:nq], AF.Exp, scale=scale)

            def finalize_stages(st_):
                q0, nq, tok0, h, m, ki, ktile, kc0, first, last, hh = st_
                W = 2 * nq if pair else nq
                A, B = ropeC[:, 0:W], ropeC[:, 512:512 + W]
                C, Dd = ropeS[:, 0:W], ropeS[:, 512:512 + W]
                sq_ = fin_sq[:, 0:W]
                if pair:
                    h0 = h - 1
                    dst = ybuf[:, ych0 + h0:ych0 + h0 + 2, tok0:tok0 + nq]
                    sg_ = sgT[:, h0:h0 + 2, tok0:tok0 + nq]
                    C_ = C.rearrange("p (a c) -> p a c", a=2)
                else:
                    dst = ybuf[:, ych0 + h, tok0:tok0 + nq]
                    sg_ = sgT[:, h, tok0:tok0 + nq]
                    C_ = C
                if ncomp == 2:
                    O0, S0 = banks(h, 0)
                    O1, S1 = banks(h, 1)

                    def s0a():
                        k.copy("dve", C, O0[:, 0:W])
                        k.act(A, S0[:, 0:W], AF.Ln)

                    def s0():
                        k.copy("dve", Dd, O1[:, 0:W])
                        k.act(B, S1[:, 0:W], AF.Ln)

                    def s1():
                        k.act(A, A, AF.Exp, scale=-1.0)
                        k.act(B, B, AF.Exp, scale=-1.0)

                    def s2():
                        k.tt("dve", C, C, A, ALU.mult)
                        k.tt("dve", Dd, Dd, B, ALU.mult)
                        k.stt("dve", C, Dd, neglam[:], C, ALU.mult, ALU.add)
                        k.tt("dve", sq_, C, C, ALU.mult)
                        k.mm(pb[3][:, 0:W], ones_b[:], sq_)

                    def s3():
                        k.act(A, pb[3][:, 0:W], AF.Ln, bias=epsc[:], scale=1.0 / 128)
                        k.act(A, A, AF.Exp, scale=-0.5)

                    def s4():
                        k.tt("dve", C, C, A, ALU.mult)
                        k.stt("dve", dst, C_, subg[:], sg_, ALU.mult, ALU.mult)
                    return [s0a, s0, s1, s2, s3, s4]
                O0, S0 = banks(h, 0)

                def g0():
                    k.copy("dve", C, O0[:, 0:W])
                    k.act(A, S0[:, 0:W], AF.Ln)

                def g1():
                    k.act(A, A, AF.Exp, scale=-1.0)

                def g2():
                    k.tt("dve", C, C, A, ALU.mult)
                    k.tt("dve", dst, C_, sg_, ALU.mult)
                return [g0, g1, g2]

            pending = []
            cur_fin = {}

            def issue_PV(i):
                q0, nq, tok0, h, m, ki, ktile, kc0, first, last, hh = steps[i]
                Ob, Sb = banks(h, m)
                e = E[i % 8]
                c0 = hh * 256
                k.mm(Ob[:, c0:c0 + nq], Vaug(ktile, h), e[:, 0:nq], start=first, stop=last)
                k.mm(Sb[:, c0:c0 + nq], ones_b[:], e[:, 0:nq], start=first, stop=last)
                for p_ in list(pending):
                    p_.pop(0)()
                    if not p_:
                        pending.remove(p_)
                fin_now = last and (hh == 1 or not pair)
                if ncomp == 2:
                    if fin_now and m == 0:
                        cur_fin["s"] = finalize_stages(steps[i])
                        cur_fin["s"].pop(0)()
                    elif fin_now and m == 1:
                        stg_ = cur_fin["s"]
                        stg_.pop(0)()
                        pending.append(stg_)
                elif fin_now:
                    stg_ = finalize_stages(steps[i])
                    stg_.pop(0)()
                    pending.append(stg_)

            LA = 2
            for j in range(min(LA, n)):
                issue_S(j)
            for i in range(n):
                if i + LA < n:
                    issue_S(i + LA)
                issue_PV(i)
                if extra is not None:
                    extra(i)
            while pending:
                for p_ in list(pending):
                    p_.pop(0)()
                    if not p_:
                        pending.remove(p_)

        rope_cnt = {"i": 0}

        def rope_evac(ps, dst, col0, n, gcol=None, post=None):
            par = rope_cnt["i"] % 2
            rope_cnt["i"] += 1
            qb = att_tmp[:, par * 256:(par + 1) * 256].bitcast(BF16)
            rt1 = att_tmp[:, 512 + par * 512:1024 + par * 512]
            rt2 = att_tmp[:, 1536:2048]
            if gcol is None:
                k.copy("act", qb[:, 0:n], ps[:, 0:n])
            else:
                k.ts("dve", qb[:, 0:n], ps[:, 0:n], gcol, None, ALU.mult)

            def tail():
                pq = pb[next_pt()]
                k.mm(pq[:, 0:n], perm_b[:], qb[:, 0:n])
                k.tt("dve", rt1[:, 0:n], qb[:, 0:n], ropeC[:, col0:col0 + n], ALU.mult)
                k.tt("dve", rt2[:, 0:n], pq[:, 0:n], ropeS[:, col0:col0 + n], ALU.mult)
                if post is None:
                    k.tt("dve", dst, rt1[:, 0:n], rt2[:, 0:n], ALU.add)
                else:
                    k.tt("dve", rt1[:, 0:n], rt1[:, 0:n], rt2[:, 0:n], ALU.add)
                    k.tt("dve", dst, rt1[:, 0:n], post, ALU.mult)
            defer(tail)

        def layer0_attn(is_sample):
            Tk = 1536 if is_sample else 1024
            nkt = Tk // 128
            qT = RV(OFF_Q, [4, 1024], BF16)
            sgT = RV(OFF_SG0, [4, 1024], BF16)
            kT = RV(OFF_K0, [4, Tk], BF16)
            Va = RV(OFF_V0, [nkt, 4, 130], BF16)
            k.memset("dve", Va[:, :, :, 128:130], 1.0)
            kst = [ring[0][:, i * 512:(i + 1) * 512].bitcast(F32) for i in range(4)]
            kbf = [ring[1][:, i * 256:(i + 1) * 256] for i in range(4)]
            koff = 512 if is_sample else 0
            if is_sample:
                k.dma("sp", ropeC[:], d_ropeD[0])
                k.dma("sp", ropeS[:], d_ropeD[1])
                stg = ring[2][:, 0:2048].rearrange("p (t c) -> p t c", t=4)
                k.dma("pool", stg, d_cdk.rearrange("(t p) c -> p t c", p=128))
                for ct in range(4):
                    bank = pbf(next_pt())
                    for h in range(4):
                        k.transpose(bank[:, h * 128:(h + 1) * 128], stg[:, ct, h * 128:(h + 1) * 128], ident_b[:])
                    k.copy("dve", kT[:, :, ct * 128:(ct + 1) * 128],
                           bank[:, 0:512].rearrange("p (h c) -> p h c", h=4))
                for ct in range(4):
                    k.dma("pool", Va[:, ct, :, 0:128],
                          d_cdv[ct * 128:(ct + 1) * 128, :].rearrange("p (h d) -> p h d", h=4))

            def evac_q(cc, tb, ps):
                if is_sample:
                    rope_evac(ps, qT[:, cc, tb * 512:(tb + 1) * 512], tb * 512, 512)
                else:
                    k.copy("act", qT[:, cc, tb * 512:(tb + 1) * 512], ps[:])

            def evac_k_fm(cc, tb, ps):
                rope_evac(ps, kT[:, cc, koff + tb * 512:koff + (tb + 1) * 512], tb * 512, 512)

            cnt = {"i": 0}

            def evac_k_tm(b, t, ps):
                i = cnt["i"] % 4
                cnt["i"] += 1
                if EXPERIMENT == "B":
                    return
                k.copy("act", kst[i], ps[:, 0:256])
                if EXPERIMENT != "A":
                    k.dma("sp", o_ndk[t * 128:(t + 1) * 128, b * 256:(b + 1) * 256], kst[i])
                k.copy("dve", kbf[i], ps[:, 0:256])

                def tail(i=i, b=b, t=t):
                    bank = pbf(next_pt())
                    for h2 in range(2):
                        k.transpose(bank[:, h2 * 128:(h2 + 1) * 128], kbf[i][:, h2 * 128:(h2 + 1) * 128], ident_b[:])
                    k.copy("dve", kT[:, 2 * b:2 * b + 2, t * 128:(t + 1) * 128],
                           bank[:, 0:256].rearrange("p (h c) -> p h c", h=2))
                defer(tail, delay=2)

            def evac_v(b, t, ps):
                if not is_sample:
                    i = cnt["i"] % 4
                    cnt["i"] += 1
                    k.copy("act", kst[i], ps[:, 0:256])
                    k.dma("sp", o_ndv[t * 128:(t + 1) * 128, b * 256:(b + 1) * 256], kst[i])
                vt = t + (4 if is_sample else 0)
                k.copy("dve", Va[:, vt, 2 * b:2 * b + 2, 0:128], ps[:, 0:256].rearrange("p (h c) -> p h c", h=2))

            def evac_g(cc, tb, ps):
                k.act(sgT[:, cc, tb * 512:(tb + 1) * 512], ps[:], AF.Silu)

            sfx = "S" if is_sample else "P"
            set_pp([0, 1, 4, 5, 6, 7])
            mark("pre" + sfx)
            proj_fm(d_evin, 2048, 2, 1024, evac_q)
            mark("pq" + sfx)
            if is_sample:
                proj_fm(d_evin, 2560, 2, 1024, evac_k_fm)
            else:
                proj_tm(d_evin, 2560, 2, 8, evac_k_tm)
            mark("pk" + sfx)
            proj_tm(d_evin, 3072, 2, 8, evac_v)
            mark("pv" + sfx)
            proj_fm(d_evin, 3584, 2, 1024, evac_g)
            mark("pg" + sfx)
            set_pp([0, 1])
            if is_sample:
                units = [(qb * 512, 512, [(kt, kt * 128) for kt in range(12)], qb * 512) for qb in range(2)]
            else:
                units = [(s_ * 256, 256, [(2 * s_ + i, (2 * s_ + i) * 128) for i in range(2)], s_ * 256)
                         for s_ in range(4)]
            dbg_dump("qT" + ("S" if is_sample else "P"), region[:, OFF_Q // 2:OFF_Q // 2 + 4096], [128, 4096], BF16)
            dbg_dump("kT" + ("S" if is_sample else "P"), region[:, OFF_K0 // 2:OFF_K0 // 2 + 4 * Tk], [128, 4 * Tk], BF16)
            qpad = [hT[:, 0:4, :], hT[:, 4:8, :]]
            k.memset("dve", qpad[0][64:128, :, :], 0.0)
            k.memset("dve", qpad[1][0:64, :, :], 0.0)
            k.copy("dve", qpad[0][0:64, :, :], qT[0:64, :, :])
            k.copy("dve", qpad[1][64:128, :, :], qT[64:128, :, :])
            extra = None
            if (not is_sample) and PREFETCH_MOD1 and "l1" in stages:
                def extra(i):
                    if i % 5 == 0 and i // 5 < 12:
                        mod_block(1, i // 5, wbuf[(i // 5) % 2], pb[2][:, 256:512])
            attention(4, 2, 64, qpad, kT, lambda kt, h: Va[:, kt, h, 0:128], sgT, units, 0.125, 4, True, extra=extra)

        def layer0_group(is_sample):
            xg = xS if is_sample else xP
            cond = 0 if is_sample else 1
            adaln(xg, 8, cond, pre=("S0" if is_sample else "P0"))
            if not is_sample:
                compute_mod(0, blocks=range(8, 12), finish=False)
            mark("adaln" + ("S" if is_sample else "P"))
            dbg_dump("hT" + ("S" if is_sample else "P"), hT[:], [128, 8, 1024], BF16)
            if is_sample:
                hyena_group(cond, LS, 1, d_zposS, d_winS, ffS_piece, finvS_piece)
            else:
                hyena_group(cond, LP, 4, d_zposP, d_winP, ffP_piece, finvP_piece)
            mark("hy" + ("S" if is_sample else "P"))
            layer0_attn(is_sample)
            mark("att" + ("S" if is_sample else "P"))
            dbg_dump("y" + ("S" if is_sample else "P"), ybuf[:], [128, 8, 1024], BF16)
            compute_gate_bc(cond)
            wout_residual(d_evout, xg, 8)
            mark("wout" + ("S" if is_sample else "P"))

        def layer1_group(is_sample):
            xg = xS if is_sample else xP
            cond = 0 if is_sample else 1
            Tq = 512 if is_sample else 1024
            Tk = 1536 if is_sample else 1024
            nkt = Tk // 128
            adaln(xg, 8, cond)
            qT = RV(OFF_Q, [8, Tq], BF16)
            sgT = RV(OFF_SG1, [8, Tq], BF16)
            kT = RV(OFF_K1, [2, Tk], BF16)
            Va = RV(OFF_V1S if is_sample else OFF_V1P, [nkt, 2, 130], BF16)
            k.memset("dve", Va[:, :, :, 128:130], 1.0)
            kst = [ring[0][:, i * 512:(i + 1) * 512].bitcast(F32) for i in range(4)]
            kbf = [ring[1][:, i * 256:(i + 1) * 256] for i in range(4)]
            gkbc = spec[:, 0, :].bitcast(F32)
            k.dma("sp", gkbc, d_gkg.partition_broadcast(128))
            koff = 512 if is_sample else 0
            if is_sample:
                k.dma("sp", ropeC[:], d_ropeG[0])
                k.dma("sp", ropeS[:], d_ropeG[1])
                stg = ring[2][:, 0:1024].rearrange("p (t c) -> p t c", t=4)
                k.dma("pool", stg, d_cgk.rearrange("(t p) c -> p t c", p=128))
                for ct in range(4):
                    bank = pbf(next_pt())
                    for h in range(2):
                        k.transpose(bank[:, h * 128:(h + 1) * 128], stg[:, ct, h * 128:(h + 1) * 128], ident_b[:])
                    k.copy("dve", kT[:, :, ct * 128:(ct + 1) * 128],
                           bank[:, 0:256].rearrange("p (h c) -> p h c", h=2))
                for ct in range(4):
                    k.dma("pool", Va[:, ct, :, 0:128],
                          d_cgv[ct * 128:(ct + 1) * 128, :].rearrange("p (h d) -> p h d", h=2))

            if is_sample:
                sqbs = [RV(8192 + i * 1024, [512], BF16) for i in range(2)]
                rstds = [RV(10240 + i * 2048, [512], F32) for i in range(2)]
            else:
                sqbs = [att_tmp[:, i * 256:(i + 1) * 256].bitcast(BF16) for i in range(2)]
                rstds = [att_tmp[:, 512 + i * 512:1024 + i * 512] for i in range(2)]
            fm_cnt = {"i": 0}

            def fm_norm(ps, n, gcol, dst, rope_col0):
                par = fm_cnt["i"] % 2
                fm_cnt["i"] += 1
                sqb, rstd = sqbs[par], rstds[par]
                k.act(sqb[:, 0:n], ps[:, 0:n], AF.Square)

                def tail():
                    pn = pb[next_pt()]
                    k.mm(pn[:, 0:n], ones_b[:], sqb[:, 0:n])
                    k.act(rstd[:, 0:n], pn[:, 0:n], AF.Ln, bias=epsc[:], scale=1.0 / 128)
                    k.act(rstd[:, 0:n], rstd[:, 0:n], AF.Exp, scale=-0.5)
                    if rope_col0 is None:
                        k.stt("dve", dst, ps[:, 0:n], gcol, rstd[:, 0:n], ALU.mult, ALU.mult)
                if rope_col0 is None:
                    defer(tail)
                else:
                    defer(tail)
                    rope_evac(ps, dst, rope_col0, n, gcol=gcol, post=rstd[:, 0:n])

            qg = vB[:, 15:16]
            kg = vB[:, 16:17]

            def evac_q(cc, tb, ps):
                fm_norm(ps, 512, qg, qT[:, cc, tb * 512:(tb + 1) * 512], (tb * 512) if is_sample else None)

            def evac_k_fm(cc, tb, ps):
                fm_norm(ps, 512, kg, kT[:, cc, koff + tb * 512:koff + (tb + 1) * 512], tb * 512)

            cnt = {"i": 0}

            def evac_k_tm(b, t, ps):
                i = cnt["i"] % 4
                cnt["i"] += 1
                rr = small[:, 16 + 4 * i:20 + 4 * i]
                k.memset("dve", rr[:, 0:2], 0.0)
                for h2 in range(2):
                    k.act(kbf[i][:, h2 * 128:(h2 + 1) * 128], ps[:, h2 * 128:(h2 + 1) * 128], AF.Square,
                          accum_out=rr[:, h2:h2 + 1])

                def tail2(i=i, t=t):
                    bank = pbf(next_pt())
                    for h2 in range(2):
                        k.transpose(bank[:, h2 * 128:(h2 + 1) * 128], kbf[i][:, h2 * 128:(h2 + 1) * 128], ident_b[:])
                    k.copy("dve", kT[:, :, t * 128:(t + 1) * 128], bank[:, 0:256].rearrange("p (h c) -> p h c", h=2))

                def tail1(i=i, t=t, ps=ps, rr=rr):
                    k.act(rr[:, 2:4], rr[:, 0:2], AF.Ln, bias=epsc[:], scale=1.0 / 128)
                    k.act(rr[:, 2:4], rr[:, 2:4], AF.Exp, scale=-0.5)
                    for h2 in range(2):
                        k.stt("dve", kst[i][:, h2 * 128:(h2 + 1) * 128], ps[:, h2 * 128:(h2 + 1) * 128],
                              rr[:, 2 + h2:3 + h2], gkbc, ALU.mult, ALU.mult)
                    k.dma("sp", o_ngk[t * 128:(t + 1) * 128, :], kst[i])
                    k.copy("dve", kbf[i], kst[i])
                    defer(tail2, delay=2)
                defer(tail1, delay=1)

            def evac_v(b, t, ps):
                if not is_sample:
                    i = cnt["i"] % 4
                    cnt["i"] += 1
                    k.copy("act", kst[i], ps[:, 0:256])
                    k.dma("sp", o_ngv[t * 128:(t + 1) * 128, :], kst[i])
                vt = t + (4 if is_sample else 0)
                k.copy("dve", Va[:, vt, :, 0:128], ps[:, 0:256].rearrange("p (h c) -> p h c", h=2))

            def evac_g(cc, tb, ps):
                k.act(sgT[:, cc, tb * 512:(tb + 1) * 512], ps[:], AF.Silu)

            sfx1 = "S" if is_sample else "P"
            mark("l1pre" + sfx1)
            set_pp([0, 1, 4, 5, 6, 7])
            proj_fm(d_odin, 0, 4, Tq, evac_q)
            mark("l1q" + sfx1)
            if is_sample:
                proj_fm(d_odin, 1024, 1, 1024, evac_k_fm)
            else:
                proj_tm(d_odin, 1024, 1, 8, evac_k_tm)
            proj_tm(d_odin, 1280, 1, 8, evac_v)
            mark("l1kv" + sfx1)
            proj_fm(d_odin, 1536, 4, Tq, evac_g)
            mark("l1g" + sfx1)
            set_pp([0, 1])
            if is_sample:
                units = [(0, 512, [(kt, kt * 128) for kt in range(12)], 0)]
            else:
                units = [(s_ * 256, 256, [(2 * s_ + i, (2 * s_ + i) * 128) for i in range(2)], s_ * 256)
                         for s_ in range(4)]
            dbg_dump("q1" + ("S" if is_sample else "P"), region[:, OFF_Q // 2:OFF_Q // 2 + 8 * Tq], [128, 8 * Tq], BF16)
            dbg_dump("k1" + ("S" if is_sample else "P"), region[:, OFF_K1 // 2:OFF_K1 // 2 + 2 * Tk], [128, 2 * Tk], BF16)

            class KV:
                pass
            attention_gqa(qT, kT, Va, sgT, units)
            mark("l1att" + sfx1)
            compute_gate_bc(cond)
            wout_residual(d_odout, xg, Tq // 128)
            mark("l1wout" + sfx1)

        def attention_gqa(qT, kT, Va, sgT, units):
            class KTv:
                def __getitem__(self, idx):
                    p, h, c = idx
                    return kT[p, h // 4, c]
            attention(8, 1, 128, qT, KTv(), lambda kt, h: Va[:, kt, h // 4, 0:128], sgT, units,
                      128.0 ** -0.5, 0, False)

        fn_state = {"loaded": False}

        def final_norm(xg, ntiles, o_y, key, background=False):
            fg = fgb
            if not fn_state["loaded"]:
                k.dma("sp", fg[:], d_fg.partition_broadcast(128))
                fn_state["loaded"] = True
            ss_ = sb("ssf_" + key, [128, 8], F32)
            rs_ = sb("rsf_" + key, [128, 8], F32)
            j0 = 8 if key == "P" else 12
            junk = spec[:, j0:j0 + 4, :].rearrange("p a c -> p (a c)")
            k.memset("dve", ss_[:], 0.0)
            tasks = []
            for t in range(ntiles):
                tasks.append(lambda t=t: k.act(junk, xg[:, t, :], AF.Square, accum_out=ss_[:, t:t + 1]))

            def lnexp():
                k.act(rs_[:, 0:ntiles], ss_[:, 0:ntiles], AF.Ln, bias=epsc[:], scale=1.0 / D)
                k.act(rs_[:, 0:ntiles], rs_[:, 0:ntiles], AF.Exp, scale=-0.5)
            tasks.append(lnexp)

            def store(t):
                k.stt("dve", xg[:, t, :], xg[:, t, :], rs_[:, t:t + 1], fg[:], ALU.mult, ALU.mult)
                k.dma("sp", o_y[t * 128:(t + 1) * 128, :], xg[:, t, :])
            for t in range(ntiles):
                tasks.append(lambda t=t: store(t))
            if background:
                bg_tasks.extend(tasks)
            else:
                for f_ in tasks:
                    f_()

        try:
            mark("setup")
            if "l0" in stages:
                adaln_stats(xP, 8, "P0", 18432)
                for t in range(8):
                    k.dma("sp", xS[:, t, :], d_xs[t * 128:(t + 1) * 128, :])
                compute_mod(0, blocks=range(8))
                adaln_stats(xS, 8, "S0", 20480)
                mark("mod0")
                layer0_group(False)
                layer0_group(True)
            dbg_dump("x1P", xP[:], [128, 8, 1024])
            dbg_dump("x1S", xS[:], [128, 8, 1024])
            if "l1" in stages:
                if PREFETCH_MOD1 and "l0" in stages:
                    mod_finish(1)
                else:
                    compute_mod(1)
                cur["modT"], cur["gsT"] = modTs[1], gsTs[1]
                mark("mod1")
                layer1_group(False)
                mark("l1P")
                if "final" in stages:
                    final_norm(xP, 8, o_yp, "P", background=True)
                layer1_group(True)
                while bg_tasks:
                    bg_tasks.pop(0)()
            if "final" in stages:
                if "l1" not in stages:
                    final_norm(xP, 8, o_yp, "P")
                final_norm(xS, 4, o_ys, "S")
        except _Stop:
            pass

        S.emit(st)
    build_program.last_sched = S
    return nc, dbg_outs


_bf = ml_dtypes.bfloat16


def _rope_tab(pos, head_dim):
    half = head_dim // 2
    inv = (np.float32(10000.0) ** (-np.arange(0, half, 2, dtype=np.float32) / np.float32(half))).astype(np.float32)
    row = (pos // 64).astype(np.float32)
    col = (pos % 64).astype(np.float32)
    ang = np.concatenate([row[:, None] * inv, col[:, None] * inv], axis=-1).astype(np.float32)
    cos, sin = np.cos(ang), np.sin(ang)
    d = np.arange(128) % head_dim
    i = d // 2
    C = cos[:, i].T
    Sg = sin[:, i].T * np.where(d % 2 == 0, -1.0, 1.0)[:, None]
    return np.stack([C, Sg]).astype(np.float32)


def _dft(L, pos):
    f = np.arange(L)
    theta = 2 * np.pi * (f + 0.5) / (2 * L)
    r = np.arange(2 * L)
    rt = r // 128
    j = rt // 2
    part = rt % 2
    fr = j * 128 + r % 128
    th = theta[fr]
    A = np.outer(pos.astype(np.float64), th)
    Ff = np.where(part[None, :] == 0, np.cos(A), -np.sin(A))
    Finv = np.where(part[:, None] == 0, np.cos(A.T), -np.sin(A.T)) / L
    return Ff, Finv


def _dft_half(L, tokpos):
    nj = L // 256
    kt_n = L // 128
    f = np.arange(L // 2)
    theta = 2 * np.pi * (f + 0.5) / (2 * L)
    A = np.outer(tokpos.astype(np.float64), theta)
    C = np.cos(A)
    Sn = -np.sin(A)
    Ff = np.concatenate([C.reshape(kt_n, 128, nj, 128), Sn.reshape(kt_n, 128, nj, 128)], axis=-1)
    FfH = np.ascontiguousarray(Ff.transpose(2, 1, 0, 3))
    Ci = (C / L).T
    Si = (Sn / L).T
    sgn = np.where(np.arange(L) < L // 2, 1.0, -1.0)[None, :]
    parts = [Ci, Si, Ci * sgn, -Si * sgn]
    Fi = np.stack([p_.reshape(nj, 128, kt_n, 128) for p_ in parts], axis=1)
    FinvH = np.ascontiguousarray(Fi.transpose(3, 2, 0, 1, 4).reshape(kt_n, 128, 4 * nj, 128))
    return FfH, FinvH


def _zpos(L, pos):
    t = np.linspace(0.0, 1.0, L, dtype=np.float32)[pos]
    tidx = pos.astype(np.float32)
    bands = np.linspace(1e-4, 15.0, 16, dtype=np.float32)
    ang = (np.float32(2.0 * math.pi) * tidx[:, None] * bands[None, :] / np.float32(L)).astype(np.float32)
    z = np.concatenate([t[:, None], np.cos(ang), -np.sin(ang)], axis=-1).astype(np.float32)
    return np.ascontiguousarray(z.T)


def _window(L, pos):
    t = np.linspace(0.0, 1.0, L, dtype=np.float32)[pos]
    mx = math.log(1e-2) / 0.3
    mn = math.log(1e-2) / 1.5
    deltas = np.abs(np.linspace(mn, mx, 512, dtype=np.float32))
    w = np.exp(-t[:, None] * deltas[None, :]).astype(np.float32)
    wb = w.copy()
    wb[pos == 0] = 0.0
    return np.stack([w, wb]).astype(np.float32)


def _core_consts(h):
    posS = (np.arange(LS) + h * 512) % LS
    posP = np.arange(LP)
    hyS = np.concatenate([posS[0::2], posS[1::2]])
    hyP = posP
    FfS, FinvS = _dft_half(LS, hyS)
    FfP, FinvP = _dft(LP, posP)
    c = {}
    c["FfS"] = FfS.astype(_bf)
    c["FinvS"] = FinvS.astype(_bf)
    c["FfP"] = np.ascontiguousarray(FfP.reshape(2, 128, 512).transpose(1, 0, 2)).astype(_bf)
    c["FinvP"] = np.ascontiguousarray(FinvP.reshape(4, 128, 256).transpose(1, 0, 2)).astype(_bf)
    c["ropeD"] = _rope_tab(posS, 64)
    c["ropeG"] = _rope_tab(posS, 128)
    c["zposP"] = _zpos(LP, hyP)
    c["zposS"] = _zpos(LS, hyS)
    c["winP"] = _window(LP, hyP)
    c["winS"] = _window(LS, hyS)
    a = np.zeros((128, 2), np.float32)
    a[:, 0] = float(h)
    a[:, 1] = 1.0 - float(h)
    c["acol"] = a
    c["ident_bf"] = np.eye(128, dtype=np.float32).astype(_bf)
    c["ident_f"] = np.eye(128, dtype=np.float32)
    pm = np.zeros((128, 128), np.float32)
    pm[np.arange(128) ^ 1, np.arange(128)] = 1.0
    c["perm_bf"] = pm.astype(_bf)
    return c


def make_in_maps(inputs):
    f = lambda a: np.ascontiguousarray(np.asarray(a, dtype=np.float32))
    I = {kk: f(v) for kk, v in inputs.items()}
    vecA = np.zeros((128, 128), np.float32)
    vecA[0:16] = I["norm_g"].reshape(16, 128)
    vecA[16:24] = I["final_g"].reshape(8, 128)
    vecA[24:72] = I["b_mod"].reshape(48, 128)
    vecA[80:88] = I["c_ctx"].reshape(8, 128)
    vecA[88:124] = I["hy_conv_w"][0].reshape(36, 128)
    vecB = np.zeros((128, 128), np.float32)
    vecB[0:12] = I["hy_conv_b"][0].reshape(12, 128)
    vecB[12, 0:64] = I["hy_b1"][0]
    vecB[13, 0:64] = I["hy_freq"][0]
    vecB[14, 0:64] = I["hy_b2"][0]
    vecB[15] = I["gq_q_g"][0]
    vecB[16] = I["gq_k_g"][0]
    vecB[19] = I["df_subln_g"][0]
    consts = [_core_consts(0), _core_consts(1)]
    shared = {
        "w_mod": I["w_mod"], "ev_w_in": I["ev_w_in"][0], "ev_w_out": I["ev_w_out"][0],
        "od_w_in": I["od_w_in"][0], "od_w_out": I["od_w_out"][0],
        "hy_w1": I["hy_w1"][0], "hy_w2": I["hy_w2"][0], "hy_w3": I["hy_w3"][0],
        "hy_skip": I["hy_skip"][0], "final_g": I["final_g"], "df_lambda": I["df_lambda"][0].reshape(1, 256),
        "gq_k_g": I["gq_k_g"][0], "vecB": vecB,
    }
    maps = []
    for c in range(8):
        b, h = c // 2, c % 2
        m = dict(shared)
        m.update(consts[h])
        va = vecA.copy()
        va[72:80] = I["c"][b].reshape(8, 128)
        m["vecA"] = va
        m["xp"] = np.ascontiguousarray(I["x_prompt"][4 * c:4 * c + 4].reshape(NPS * LP, D))
        m["xs"] = np.ascontiguousarray(np.roll(I["x_sample"][b], -h * 512, axis=0))
        m["cdk"] = np.ascontiguousarray(I["cache_diff_k"][b, 0].reshape(PAST, 512))
        m["cdv"] = np.ascontiguousarray(I["cache_diff_v"][b, 0].reshape(PAST, 512))
        m["cgk"] = np.ascontiguousarray(I["cache_gqa_k"][b, 0].reshape(PAST, 256))
        m["cgv"] = np.ascontiguousarray(I["cache_gqa_v"][b, 0].reshape(PAST, 256))
        maps.append(m)
    return maps


def assemble(results):
    yp = np.zeros((32, 256, D), np.float32)
    ys = np.zeros((4, 1024, D), np.float32)
    ndk = np.zeros((32, 1, 256, 4, 2, 64), np.float32)
    ndv = np.zeros((32, 1, 256, 4, 128), np.float32)
    ngk = np.zeros((32, 1, 256, 2, 128), np.float32)
    ngv = np.zeros((32, 1, 256, 2, 128), np.float32)
    for c, r in enumerate(results):
        b, h = c // 2, c % 2
        yp[4 * c:4 * c + 4] = np.asarray(r["y_p"]).reshape(4, 256, D)
        ys[b, h * 512:(h + 1) * 512] = np.asarray(r["y_s"])
        ndk[4 * c:4 * c + 4, 0] = np.asarray(r["ndk"]).reshape(4, 256, 4, 2, 64)
        ndv[4 * c:4 * c + 4, 0] = np.asarray(r["ndv"]).reshape(4, 256, 4, 128)
        ngk[4 * c:4 * c + 4, 0] = np.asarray(r["ngk"]).reshape(4, 256, 2, 128)
        ngv[4 * c:4 * c + 4, 0] = np.asarray(r["ngv"]).reshape(4, 256, 2, 128)
    return (yp, ys, ndk, ndv, ngk, ngv)


def kernel(**inputs):
    nc, _ = build_program()
    maps = make_in_maps(inputs)
    res = run_bass_kernel_spmd(nc, maps, core_ids=list(range(8)))
    return assemble(res.results)
```
